# Optimizing a Trainium2 kernel written in Bass

```python
import jax
import jax.numpy as jnp
from jax import lax
import numpy as np

D_MODEL = 2048
BATCH = 8
SEQ = 2048
DEPTH = 2

CTX_LEN = 256
GRID_W = 64
W_A = 1024
HEAD_DIM = 64
N_HEADS = W_A // HEAD_DIM
W_B = 1024
POOL_WINDOWS = (2, 4, 8, 16)
N_POOL_GROUPS = len(POOL_WINDOWS)
POOL_GROUP_W = W_B // N_POOL_GROUPS
R_DECAY = 64
R_ICLR = 64
R_VRES = 32
NORM_EPS = 1e-6
GN_EPS = HEAD_DIM * 1e-5
OFF_RKV = 0
OFF_POOL = OFF_RKV + 3 * W_A
OFF_GATE_A = OFF_POOL + W_B
OFF_GATE_B = OFF_GATE_A + W_A
OFF_DECAY = OFF_GATE_B + W_B
OFF_ICLR = OFF_DECAY + 2 * R_DECAY
OFF_MERGE = OFF_ICLR + 2 * R_ICLR
IN_WIDTH = OFF_MERGE + 2 * D_MODEL

kernel_name = 'hybrid_bi_rwkv7_multipool_prefix_dit'


def rmsnorm(x, g):
    x32 = x.astype(jnp.float32)
    y = x32 * lax.rsqrt(jnp.mean(x32 * x32, axis=-1, keepdims=True) + NORM_EPS)
    return (y * g.astype(jnp.float32)).astype(x.dtype)


def grid_shift(u):
    b, t, m, ch = u.shape
    rows = t // GRID_W
    q = ch // 4
    g = u.reshape(b, rows, GRID_W, m, ch)
    left = jnp.pad(g[:, :, :-1, :, :q], ((0, 0), (0, 0), (1, 0), (0, 0), (0, 0)))
    right = jnp.pad(g[:, :, 1:, :, q:2 * q], ((0, 0), (0, 0), (0, 1), (0, 0), (0, 0)))
    up = jnp.pad(g[:, :-1, :, :, 2 * q:3 * q], ((0, 0), (1, 0), (0, 0), (0, 0), (0, 0)))
    down = jnp.pad(g[:, 1:, :, :, 3 * q:], ((0, 0), (0, 1), (0, 0), (0, 0), (0, 0)))
    return jnp.concatenate([left, right, up, down], axis=-1).reshape(b, t, m, ch)


def seq_shift(u):
    h = u.shape[-1] // 2
    prev = jnp.pad(u[:, :-1, :, :h], ((0, 0), (1, 0), (0, 0), (0, 0)))
    nxt = jnp.pad(u[:, 1:, :, h:], ((0, 0), (0, 1), (0, 0), (0, 0)))
    return jnp.concatenate([prev, nxt], axis=-1)


def multiscale_pool(p):
    t = p.shape[1]
    p32 = p.astype(jnp.float32)
    cs = jnp.pad(jnp.cumsum(p32, axis=1), ((0, 0), (1, 0), (0, 0)))
    pos = jnp.arange(t)
    outs = []
    for gi, win in enumerate(POOL_WINDOWS):
        left = win // 2
        right = win - 1 - left
        lo = jnp.clip(pos - left, 0, t)
        hi = jnp.clip(pos + right + 1, 0, t)
        sl = slice(gi * POOL_GROUP_W, (gi + 1) * POOL_GROUP_W)
        csg = cs[:, :, sl]
        mean = (csg[:, hi] - csg[:, lo]) / (hi - lo).astype(jnp.float32)[None, :, None]
        outs.append(mean - p32[:, :, sl])
    return jnp.stack(outs, axis=2)


def _heads(z):
    return z.reshape(z.shape[:-1] + (N_HEADS, HEAD_DIM))


def _dirs_shared(z):
    return jnp.stack([z, jnp.flip(z, 1)], axis=0)


def _dirs_split(z):
    return _heads(jnp.stack([z[:, :, 0], jnp.flip(z[:, :, 1], 1)], axis=0))


def wkv_scan(s0, w, k, v, kk, a, r):
    xs = (w, k, v, kk, a) + ((r,) if r is not None else ())
    xs = tuple(jnp.moveaxis(z, 2, 0) for z in xs)

    def step(S, inp):
        w_t, k_t, v_t, kk_t, a_t = inp[:5]
        sa = jnp.einsum('dbhvk,dbhk->dbhv', S, -kk_t)
        S = (S * w_t[..., None, :] + sa[..., :, None] * (kk_t * a_t)[..., None, :]
             + v_t[..., :, None] * k_t[..., None, :])
        y = jnp.einsum('dbhvk,dbhk->dbhv', S, inp[5]) if len(inp) > 5 else None
        return S, y

    s_fin, ys = lax.scan(step, s0, xs)
    return s_fin, (None if r is None else jnp.moveaxis(ys, 0, 2))


def stream_mixer(xn, u, p, v_first, shift_fn, s0, emit_out):
    f32 = jnp.float32
    b, t, _ = u.shape
    rkv = u[..., OFF_RKV:OFF_RKV + 3 * W_A].reshape(b, t, 3, W_A)
    rkv = rkv + p['tok_mu'] * (shift_fn(rkv) - rkv)
    r, k, v = (rkv[:, :, i].astype(f32) for i in range(3))
    if v_first is not None:
        v_gate = jax.nn.sigmoid((p['vres_v0'] + (xn @ p['vres_lora_a']) @ p['vres_lora_b']).astype(f32))
        v = v + (v_first - v) * v_gate
    d_lr = jnp.tanh(u[..., OFF_DECAY:OFF_DECAY + 2 * R_DECAY].reshape(b, t, 2, R_DECAY).astype(f32))
    w_log = -jax.nn.softplus(-(p['decay_w0'] + jnp.einsum('btdr,drc->btdc', d_lr, p['decay_lora_b']))) - 0.5
    decay = jnp.exp(-jnp.exp(w_log))
    a_lr = u[..., OFF_ICLR:OFF_ICLR + 2 * R_ICLR].reshape(b, t, 2, R_ICLR).astype(f32)
    a = jax.nn.sigmoid(p['iclr_a0'] + jnp.einsum('btdr,drc->btdc', a_lr, p['iclr_lora_b']))
    kk = _heads(k * p['key_k'])
    kk = kk / jnp.maximum(jnp.sqrt(jnp.sum(kk * kk, axis=-1, keepdims=True)), 1e-12)
    k_dir = k[:, :, None] * (1.0 + (a - 1.0) * p['key_a'])
    r_h, v_h = _heads(r), _heads(v)
    s_fin, y = wkv_scan(s0, _dirs_split(decay), _dirs_split(k_dir), _dirs_shared(v_h), _dirs_shared(kk),
                        _dirs_split(a), _dirs_shared(r_h) if emit_out else None)
    if not emit_out:
        return None, s_fin, v
    y = y[0] + jnp.flip(y[1], 1)
    mu = jnp.mean(y, axis=-1, keepdims=True)
    var = jnp.mean(jnp.square(y - mu), axis=-1, keepdims=True)
    y = ((y - mu) * lax.rsqrt(var + GN_EPS)).reshape(b, t, W_A) * p['gn_w'] + p['gn_b']
    bonus = jnp.sum(jnp.sum(r_h[:, :, None] * _heads(k_dir) * p['bonus_rk'], axis=-1, keepdims=True)
                    * v_h[:, :, None], axis=2)
    y = y + bonus.reshape(b, t, W_A)
    y_a = (y * jax.nn.silu(u[..., OFF_GATE_A:OFF_GATE_A + W_A].astype(f32))).astype(u.dtype) @ p['w_out_a']
    pooled = multiscale_pool(u[..., OFF_POOL:OFF_POOL + W_B])
    mixed = jnp.einsum('btgi,gio->btgo', pooled, p['pool_w']).reshape(b, t, W_B) * p['pool_scale']
    y_b = (mixed * jax.nn.silu(u[..., OFF_GATE_B:OFF_GATE_B + W_B].astype(f32))).astype(u.dtype) @ p['w_out_b']
    mg = jax.nn.sigmoid(u[..., OFF_MERGE:OFF_MERGE + 2 * D_MODEL].astype(f32)).reshape(b, t, 2, D_MODEL)
    merged = mg[:, :, 0] * y_a + mg[:, :, 1] * y_b
    out = merged.astype(u.dtype) @ p['w_out']
    return out, s_fin, v


def setup_inputs(seed: int = 0) -> dict:
    key = jax.random.key(seed)
    ks = jax.random.split(key, 32)
    f32 = jnp.float32

    def nrm(k, shape, scale):
        return jax.random.normal(k, shape, f32) * scale

    L = DEPTH
    return {
        'x': nrm(ks[0], (BATCH, SEQ, D_MODEL), 1.0),
        'c': nrm(ks[1], (BATCH, D_MODEL), 1.0),
        'ctx': nrm(ks[2], (BATCH, CTX_LEN, D_MODEL), 1.0),
        'c_ctx': nrm(ks[3], (D_MODEL,), 1.0),
        'ada_w': nrm(ks[4], (L, D_MODEL, 3 * D_MODEL), 0.5 * D_MODEL ** -0.5),
        'ada_b': nrm(ks[5], (L, 3 * D_MODEL), 0.02),
        'norm_g': 1.0 + nrm(ks[6], (L, D_MODEL), 0.02),
        'w_in': nrm(ks[7], (L, D_MODEL, IN_WIDTH), D_MODEL ** -0.5),
        'tok_mu': jax.random.uniform(ks[8], (L, 3, W_A), f32),
        'decay_w0': jax.random.uniform(ks[9], (L, 2, W_A), f32, -5.0, 0.0),
        'decay_lora_b': nrm(ks[10], (L, 2, R_DECAY, W_A), 0.1 * R_DECAY ** -0.5),
        'iclr_a0': nrm(ks[11], (L, 2, W_A), 0.1),
        'iclr_lora_b': nrm(ks[12], (L, 2, R_ICLR, W_A), 0.1 * R_ICLR ** -0.5),
        'key_k': 0.85 + nrm(ks[13], (L, W_A), 0.05),
        'key_a': 1.0 + nrm(ks[14], (L, 2, W_A), 0.05),
        'bonus_rk': nrm(ks[15], (L, 2, N_HEADS, HEAD_DIM), 0.1),
        'gn_w': 1.0 + nrm(ks[16], (L, W_A), 0.02),
        'gn_b': nrm(ks[17], (L, W_A), 0.02),
        'vres_v0': nrm(ks[18], (L - 1, W_A), 0.1),
        'vres_lora_a': nrm(ks[19], (L - 1, D_MODEL, R_VRES), D_MODEL ** -0.5),
        'vres_lora_b': nrm(ks[20], (L - 1, R_VRES, W_A), 0.1 * R_VRES ** -0.5),
        'pool_w': nrm(ks[21], (L, N_POOL_GROUPS, POOL_GROUP_W, POOL_GROUP_W), POOL_GROUP_W ** -0.5),
        'pool_scale': 1.0 + nrm(ks[22], (L, W_B), 0.05),
        'w_out_a': nrm(ks[23], (L, W_A, D_MODEL), W_A ** -0.5),
        'w_out_b': nrm(ks[24], (L, W_B, D_MODEL), W_B ** -0.5),
        'w_out': nrm(ks[25], (L, D_MODEL, D_MODEL), D_MODEL ** -0.5),
        'final_g': 1.0 + nrm(ks[26], (D_MODEL,), 0.02),
    }


def reference(x, c, ctx, c_ctx, ada_w, ada_b, norm_g, w_in, tok_mu, decay_w0, decay_lora_b, iclr_a0,
              iclr_lora_b, key_k, key_a, bonus_rk, gn_w, gn_b, vres_v0, vres_lora_a, vres_lora_b,
              pool_w, pool_scale, w_out_a, w_out_b, w_out, final_g):
    b = x.shape[0]
    s_zero = jnp.zeros((2, b, N_HEADS, HEAD_DIM, HEAD_DIM), jnp.float32)
    h_lat, h_ctx = x, ctx
    vf_lat = None
    vf_ctx = None
    for l in range(DEPTH):
        last = l == DEPTH - 1
        p = dict(tok_mu=tok_mu[l], decay_w0=decay_w0[l], decay_lora_b=decay_lora_b[l], iclr_a0=iclr_a0[l],
                 iclr_lora_b=iclr_lora_b[l], key_k=key_k[l], key_a=key_a[l], bonus_rk=bonus_rk[l],
                 gn_w=gn_w[l], gn_b=gn_b[l], pool_w=pool_w[l], pool_scale=pool_scale[l],
                 w_out_a=w_out_a[l], w_out_b=w_out_b[l], w_out=w_out[l])
        if l > 0:
            p.update(vres_v0=vres_v0[l - 1], vres_lora_a=vres_lora_a[l - 1], vres_lora_b=vres_lora_b[l - 1])
        shift_l, scale_l, gate_l = jnp.split((jax.nn.silu(c) @ ada_w[l] + ada_b[l])[:, None, :], 3, axis=-1)
        shift_c, scale_c, gate_c = jnp.split(jax.nn.silu(c_ctx) @ ada_w[l] + ada_b[l], 3, axis=-1)
        xn_c = rmsnorm(h_ctx, norm_g[l]) * (1.0 + scale_c) + shift_c
        xn_l = rmsnorm(h_lat, norm_g[l]) * (1.0 + scale_l) + shift_l
        out_c, s_ctx, v_c = stream_mixer(xn_c, xn_c @ w_in[l], p, vf_ctx, seq_shift, s_zero, not last)
        out_l, _, v_l = stream_mixer(xn_l, xn_l @ w_in[l], p, vf_lat, grid_shift, s_ctx, True)
        if l == 0:
            vf_ctx, vf_lat = v_c, v_l
        h_lat = h_lat + gate_l * out_l
        if not last:
            h_ctx = h_ctx + gate_c * out_c
    return rmsnorm(h_lat, final_g)
```

```python
import contextlib
import numpy as np
import concourse.bass as bass
import concourse.mybir as mybir
from concourse.bass_utils import run_bass_kernel_spmd

F32 = mybir.dt.float32
BF16 = mybir.dt.bfloat16
AF = mybir.ActivationFunctionType
ALU = mybir.AluOpType

import os as _os
SAME_ENGINE_KINDS = tuple(_os.environ.get('KSE', 'raw').split(','))
SEM_LIMIT = 30000

D = 2048
NT = 2304
NCTX = 256
NLAT = 2048
DEPTH = 2
INW = 10496
OFF_POOL = 3072
OFF_GA = 4096
OFF_GB = 5120
OFF_DEC = 6144
OFF_ICL = 6272
OFF_MG = 6400
C0 = float(np.exp(-0.5))
GN_EPS = 64e-5
TILES = [(0, 256), (256, 512), (768, 512), (1280, 512), (1792, 512)]
NCH = NT // 64

PV = {}
_o = 0
for _n, _w in [("mu", 24), ("w0", 16), ("a0", 16), ("kk", 8), ("ka", 16), ("brk", 16), ("gnw", 8), ("gnb", 8),
               ("psc", 8), ("ng", 16), ("adab", 48), ("v0", 8), ("fg", 16), ("omka", 16), ("ommu", 24)]:
    PV[_n] = _o
    _o += _w
NPV = _o
CS = {}
_o = 0
for _n, _w in [("ident", 128), ("ob", 128), ("mg0", 128), ("mg1", 128), ("mq0", 64), ("mq1", 64), ("rst", 512),
               ("pcor", 64)]:
    CS[_n] = _o
    _o += _w
NCS = _o


class Buf:
    __slots__ = ("name", "w", "r", "sem", "ndma")

    def __init__(self, name):
        self.name = name
        self.w = None
        self.r = []
        self.sem = None
        self.ndma = 0


class Op:
    __slots__ = ("eng", "fn", "deps", "signal", "count", "is_dma", "buf")

    def __init__(self, eng, fn, is_dma=False, buf=None):
        self.eng = eng
        self.fn = fn
        self.deps = []
        self.signal = False
        self.count = None
        self.is_dma = is_dma
        self.buf = buf


class Sched:
    ENGS = ("pe", "act", "dve", "pool", "sp")

    def __init__(self, nc):
        self.nc = nc
        self.ops = {e: [] for e in self.ENGS}
        self.dma_bufs = []
        self.bar_id = 0
        self.bar_deps = []
        self.eng_bar = {e: 0 for e in self.ENGS}
        self.dmas_since = []

    def barrier(self):
        X = []
        for e in self.ENGS:
            for o in reversed(self.ops[e]):
                if not o.is_dma:
                    X.append(o)
                    break
        lastd = {}
        for o in self.dmas_since:
            lastd[id(o.buf)] = o
        X.extend(lastd.values())
        self.dmas_since = []
        self.bar_deps = X
        self.bar_id += 1

    def _add(self, op, reads, writes):
        deps = []
        if self.eng_bar[op.eng] < self.bar_id:
            deps.extend((d, "raw") for d in self.bar_deps)
            self.eng_bar[op.eng] = self.bar_id
        for b in reads:
            if b.w is not None:
                deps.append((b.w, "raw"))
        for b in writes:
            if b.w is not None:
                deps.append((b.w, "waw"))
            deps.extend((r, "war") for r in b.r)
        seen = set()
        for d, kind in deps:
            if d is op:
                continue
            if (not d.is_dma) and (not op.is_dma) and d.eng == op.eng:
                if d.eng == "pe" or kind not in SAME_ENGINE_KINDS:
                    continue
            if id(d) in seen:
                continue
            seen.add(id(d))
            d.signal = True
            op.deps.append(d)
        for b in reads:
            b.r.append(op)
        for b in writes:
            b.w = op
            b.r = []
        self.ops[op.eng].append(op)
        return op

    def op(self, eng, fn, reads=(), writes=()):
        return self._add(Op(eng, fn), list(reads), list(writes))

    def dma(self, eng, out_ap, in_ap, reads, write):
        op = Op(eng, None, is_dma=True, buf=write)
        op.fn = lambda e, o=out_ap, i=in_ap: e.dma_start(out=o, in_=i)
        if write.sem is None:
            self.dma_bufs.append(write)
            write.sem = True
        self._add(op, list(reads), [write])
        write.ndma += 1
        op.count = 16 * write.ndma
        op.signal = True
        self.dmas_since.append(op)
        return op

    def emit(self, final_waits=()):
        nc = self.nc
        with contextlib.ExitStack() as st:
            for b in self.dma_bufs:
                b.sem = st.enter_context(nc.semaphore("d_" + b.name))
            esems = {}
            for e in self.ENGS:
                n = 0
                for o in self.ops[e]:
                    if o.is_dma:
                        continue
                    if o.signal:
                        n += 1
                        o.count = n
                nsem = max(0, n - 1) // SEM_LIMIT + 1
                esems[e] = [st.enter_context(nc.semaphore("e_%s%d" % (e, i))) for i in range(nsem)]

            def semval(o):
                if o.is_dma:
                    return o.buf.sem, o.count, ("d", id(o.buf))
                k = (o.count - 1) // SEM_LIMIT
                return esems[o.eng][k], o.count - k * SEM_LIMIT, ("e", o.eng, k)

            block = st.enter_context(nc.Block())

            def run(ename, eng):
                waited = {}
                for o in self.ops[ename]:
                    need = {}
                    for d in o.deps:
                        sem, val, key = semval(d)
                        if waited.get(key, 0) >= val:
                            continue
                        if key not in need or need[key][1] < val:
                            need[key] = (sem, val)
                    for key, (sem, val) in need.items():
                        eng.wait_ge(sem, val)
                        waited[key] = val
                    ins = o.fn(eng)
                    if o.signal:
                        if o.is_dma:
                            ins.then_inc(o.buf.sem, 16)
                        else:
                            k = (o.count - 1) // SEM_LIMIT
                            ins.then_inc(esems[ename][k], 1)
                if ename == "sp":
                    for o in final_waits:
                        sem, val, key = semval(o)
                        eng.wait_ge(sem, val)

            @block.tensor
            def _(eng):
                run("pe", eng)

            @block.scalar
            def _(eng):
                run("act", eng)

            @block.vector
            def _(eng):
                run("dve", eng)

            @block.gpsimd
            def _(eng):
                run("pool", eng)

            @block.sync
            def _(eng):
                run("sp", eng)


class Ctx:
    pass


class _Stop(Exception):
    pass


def build_program(depth=DEPTH, dbg=None):
    nc = bass.Bass("TRN2", target_bir_lowering=False)
    S = Sched(nc)
    dbg = dbg or {}
    X = Ctx()
    X.nc, X.S = nc, S
    import os
    X.stop_after = tuple(int(v) for v in os.environ['KSTOP'].split(',')) if os.environ.get('KSTOP') else None
    X.nj = int(os.environ.get('KNJ', '8'))
    X.skip = os.environ.get('KSKIP', '')
    X.p3stop = int(os.environ.get('KP3', '0'))
    X.noy = os.environ.get('KNOY', '')

    def din(name, shape, dt=F32):
        return nc.dram_tensor(name, list(shape), dt, kind="ExternalInput").ap()

    def dscr(name, shape, dt=F32):
        return nc.dram_tensor(name, list(shape), dt, kind="Internal").ap()

    X.xin = din("xin", [NT, D])
    cT_d = din("cT", [128, 32])
    X.ada_w = din("ada_w", [DEPTH, D, 3 * D])
    X.w_in = din("w_in", [DEPTH, D, INW])
    X.dlb_d = din("dlb", [DEPTH, 128, 1024])
    X.ilb_d = din("ilb", [DEPTH, 128, 1024])
    X.vla_d = din("vla", [D, 32])
    X.vlb_d = din("vlb", [32, 1024])
    X.pw_d = din("pool_w", [DEPTH, 4, 256, 256])
    X.woa_d = din("w_out_a", [DEPTH, 1024, D])
    X.wob_d = din("w_out_b", [DEPTH, 1024, D])
    X.wo_d = din("w_out", [DEPTH, D, D])
    pv_d = din("pv", [DEPTH, 128, NPV])
    cs_d = din("cst", [128, NCS])
    X.fg_d = din("fgrep", [128, D])
    X.out_d = nc.dram_tensor("out", [NLAT, D], F32, kind="ExternalOutput").ap()

    X.U32_d = dscr("U32", [32, 128, NT])
    X.SG_d = dscr("SG", [16, 128, NT], BF16)
    X.MG_d = dscr("MG", [32, 128, NT], BF16)
    X.VF_d = dscr("VF", [8, 128, NT])
    X.H1_d = dscr("H1", [NT, D])
    X.YG_d = dscr("YG", [8, 128, NT], BF16)
    X.MER_d = dscr("MER", [16, 128, NT], BF16)
    X.B_MER = Buf("MER")
    X.B_YG = Buf("YG")
    X.B_U32 = [Buf("U32")] * 32
    X.B_SG = [Buf("SG")] * 16
    X.B_MG = [Buf("MG")] * 32
    X.B_VF = [Buf("VF")] * 8
    X.B_H1 = [Buf("H1")] * 5
    X.B_OUT = [Buf("OUT")] * 5
    X.dbg_out = {}
    for k, shp in dbg.items():
        X.dbg_out[k] = (nc.dram_tensor("dbg_" + k, list(shp), F32, kind="ExternalOutput").ap(), Buf("dbg_" + k))
    X.finals = []

    def MM(out, lhsT, rhs, R, W, start=True, stop=True):
        S.op("pe", lambda e, o=out, l=lhsT, r=rhs, s=start, t=stop: e.matmul(o, lhsT=l, rhs=r, start=s, stop=t), R, W)

    def TR(out, in_, ident, R, W):
        S.op("pe", lambda e, o=out, i=in_, d=ident: e.transpose(o, i, d), R, W)

    def ACT(out, in_, func, R, W, bias=None, scale=None):
        kw = {}
        if bias is not None:
            kw["bias"] = bias
        if scale is not None:
            kw["scale"] = scale
        S.op("act", lambda e, o=out, i=in_, f=func, k=kw: e.activation(out=o, in_=i, func=f, **k), R, W)

    def SQACC(in_, junk, acc, R, W):
        S.op("act", lambda e, o=junk, i=in_, a=acc: e.activation(out=o, in_=i, func=AF.Square, accum_out=a), R, W)

    def CP(eng, out, in_, R, W):
        if eng == "act":
            S.op("act", lambda e, o=out, i=in_: e.copy(out=o, in_=i), R, W)
        else:
            S.op(eng, lambda e, o=out, i=in_: e.tensor_copy(out=o, in_=i), R, W)

    def TT(out, in0, in1, op, R, W, eng="dve"):
        S.op(eng, lambda e, o=out, a=in0, b=in1, p=op: e.tensor_tensor(out=o, in0=a, in1=b, op=p), R, W)

    def TS(out, in0, s1, s2, op0, op1, R, W, eng="dve"):
        if s2 is None:
            S.op(eng, lambda e, o=out, a=in0, x=s1, p=op0: e.tensor_scalar(out=o, in0=a, scalar1=x, scalar2=None, op0=p), R, W)
        else:
            S.op(eng, lambda e, o=out, a=in0, x=s1, y=s2, p=op0, q=op1: e.tensor_scalar(out=o, in0=a, scalar1=x, scalar2=y, op0=p, op1=q), R, W)

    def STT(out, in0, sc, in1, op0, op1, R, W, eng="dve"):
        S.op(eng, lambda e, o=out, a=in0, x=sc, b=in1, p=op0, q=op1: e.scalar_tensor_tensor(out=o, in0=a, scalar=x, in1=b, op0=p, op1=q), R, W)

    def RECIP(out, in_, R, W):
        S.op("dve", lambda e, o=out, i=in_: e.reciprocal(out=o, in_=i), R, W)

    def SCAN(out, d0, d1, R, W):
        S.op("dve", lambda e, o=out, a=d0, b=d1: e.tensor_tensor_scan(out=o, data0=a, data1=b, initial=0.0, op0=ALU.mult, op1=ALU.add), R, W)

    def MEMSET(eng, ap, val, W):
        S.op(eng, lambda e, a=ap, v=val: e.memset(a, v), [], W)

    X.MM, X.TR, X.ACT, X.SQACC, X.CP, X.TT, X.TS, X.STT, X.RECIP, X.SCAN, X.MEMSET = MM, TR, ACT, SQACC, CP, TT, TS, STT, RECIP, SCAN, MEMSET
    uid = [0]

    def sb(stack, name, shape, dt=F32):
        uid[0] += 1
        nm = "%s_%d" % (name, uid[0])
        t = stack.enter_context(nc.sbuf_tensor(nm, list(shape), dt))
        return t, Buf(nm)
    X.sb = sb

    def dbg_dump(key, ap, B):
        if key in X.dbg_out:
            o, Bo = X.dbg_out[key]
            X.finals.append(S.dma("sp", o, ap, [B], Bo))
    X.dbg_dump = dbg_dump

    with contextlib.ExitStack() as st:
        X.PB = []
        for i in range(6):
            t = st.enter_context(nc.psum_tensor("pb%d" % i, [128, 512], F32))
            X.PB.append((t, Buf("pb%d" % i)))
        X.PTRt = st.enter_context(nc.psum_tensor("ptr", [128, 1024], BF16))
        X.PTRt2 = st.enter_context(nc.psum_tensor("ptr2", [128, 1024], BF16))
        X.B_PTR = Buf("ptr")
        PB = X.PB

        X.cst, X.B_cst = sb(st, "cst", [128, NCS])
        X.pv, X.B_pv = sb(st, "pv", [128, DEPTH, NPV])
        X.identb, X.B_identb = sb(st, "identb", [128, 128], BF16)
        X.obb, X.B_obb = sb(st, "obb", [128, 128], BF16)
        X.ob64, X.B_ob64 = sb(st, "ob64", [128, 128])
        cT, B_cT = sb(st, "cTs", [128, 32])
        sc, B_sc = sb(st, "sc", [128, 16, 2], BF16)
        MODS = [sb(st, "mod%d" % i, [128, 48, 2]) for i in range(DEPTH)]
        GSCS = [sb(st, "gsc%d" % i, [128, 16, 2]) for i in range(DEPTH)]
        X.mod, X.B_mod = MODS[0]
        gsc, B_gsc = GSCS[0]
        X.dlrT, X.B_dlrT = sb(st, "dlrT", [128, NT], BF16)
        X.alrT, X.B_alrT = sb(st, "alrT", [128, NT], BF16)
        X.vlT, X.B_vlT = sb(st, "vlT", [32, NT], BF16)
        t32big, _ = sb(st, "t32big", [128, 12 * 512])
        X.t32big = t32big
        X.T32 = [(t32big[:, i * 512:(i + 1) * 512], Buf("t32_%d" % i)) for i in range(12)]
        cst, B_cst, pv, B_pv = X.cst, X.B_cst, X.pv, X.B_pv
        S.dma("sp", cst[:], cs_d, [], B_cst)
        for l in range(DEPTH):
            S.dma("sp", pv[:, l, :], pv_d[l], [], B_pv)
        S.dma("sp", cT[:], cT_d, [], B_cT)
        CP("act", X.identb[:], cst[:, CS["ident"]:CS["ident"] + 128], [B_cst], [X.B_identb])
        CP("act", X.obb[:], cst[:, CS["ob"]:CS["ob"] + 128], [B_cst], [X.B_obb])
        S.op("act", lambda e: e.mul(out=X.ob64[:], in_=cst[:, CS["ob"]:CS["ob"] + 128], mul=1.0 / 64.0), [B_cst], [X.B_ob64])
        ident = cst[:, CS["ident"]:CS["ident"] + 128]
        ACT(sc[:].rearrange("p k t -> p (k t)"), cT[:], AF.Silu, [B_cT], [B_sc])

        def pcol(l, name, j):
            o = PV[name] + j
            return pv[:, l, o:o + 1]
        X.pcol = pcol

        ws_rr = [0]

        def make_wslots(stack):
            X.WS = [sb(stack, "ws%d" % i, [128, 16, 512], BF16) for i in range(2)]

        def load_w(dram_ap_3d, nk, ncols):
            i = ws_rr[0] % 2
            ws_rr[0] += 1
            t, B = X.WS[i]
            S.dma("pool", t[:, 0:nk, 0:ncols], dram_ap_3d, [], B)
            return t, B
        X.load_w = load_w
        pa_rr = [0]

        def pacc():
            i = pa_rr[0] % 2
            pa_rr[0] += 1
            return PB[i]

        for l in range(depth):
            X.l = l
            X.last = last = (l == DEPTH - 1)
            with contextlib.ExitStack() as sA:
                make_wslots(sA)
                xnT, _bx = sb(sA, "xnT", [128, 16, NT], BF16)
                B_xnTs = [Buf("xnT_%d_%d" % (l, i)) for i in range(16)]
                B_xnT = B_xnTs
                stg = [sb(sA, "stg%d" % i, [128, NT]) for i in range(2)]
                stgb = [sb(sA, "stgb%d" % i, [128, NT], BF16) for i in range(2)]
                htile = [sb(sA, "ht%d" % i, [128, D]) for i in range(2)]
                sqj, B_sqj = stg[0][0][:, 0:D], stg[0][1]
                stat, B_stat = sb(sA, "stat", [128, 8])
                B_stats = [Buf("stat%d_%d" % (l, i)) for i in range(4)]
                htile = htile + [(stg[1][0][:, 0:D], stg[1][1])]
                X.mod, X.B_mod = MODS[l]
                mod, B_mod = MODS[l]
                gsc, B_gsc = GSCS[l]

                def phase0_gen(ll, slots, halfk):
                    pm, B_pm = PB[2]
                    mod_, B_mod_ = MODS[ll]
                    gsc_, B_gsc_ = GSCS[ll]
                    nh = 2 if halfk else 1
                    kper = 16 // nh
                    cnt = 0
                    for blk in range(12):
                        for hf in range(nh):
                            wt, Bw = slots[cnt % len(slots)]
                            cnt += 1
                            src_ = X.ada_w[ll][hf * kper * 128:(hf + 1) * kper * 128, blk * 512:(blk + 1) * 512]
                            S.dma("pool", wt[:, 0:kper, :], src_.rearrange("(k p) c -> p k c", p=128), [], Bw)
                            for ft in range(4):
                                fc = blk * 4 + ft
                                for kc in range(kper):
                                    MM(pm[:, hf * 96 + fc * 2:hf * 96 + fc * 2 + 2], wt[:, kc, ft * 128:(ft + 1) * 128], sc[:, hf * kper + kc, :],
                                       [Bw, B_sc], [B_pm], start=(kc == 0), stop=(kc == kper - 1))
                                yield
                    adab = pv[:, ll, PV["adab"]:PV["adab"] + 48]
                    TT(mod_[:], pm[:, 0:96].rearrange("p (f t) -> p f t", t=2), adab.unsqueeze(2).to_broadcast([128, 48, 2]), ALU.add,
                       [B_pm, B_pv], [B_mod_])
                    if halfk:
                        TT(mod_[:], mod_[:], pm[:, 96:192].rearrange("p (f t) -> p f t", t=2), ALU.add, [B_pm, B_mod_], [B_mod_])
                    ngv = pv[:, ll, PV["ng"]:PV["ng"] + 16]
                    STT(gsc_[:], mod_[:, 16:32, :], 1.0, ngv.unsqueeze(2).to_broadcast([128, 16, 2]), ALU.add, ALU.mult, [B_mod_, B_pv], [B_gsc_])
                    yield
                bg = None
                if l == 0:
                    for _ in phase0_gen(0, X.WS, False):
                        pass
                    if depth > 1:
                        aws = [sb(sA, "aws%d" % i, [128, 8, 512], BF16) for i in range(1)]
                        bg = phase0_gen(1, aws, True)
                if X.stop_after == (l, 0):
                    break
                src = X.xin if l == 0 else X.H1_d
                p1_rr = [0]
                for blk in range(NT // 128):
                    t0 = blk * 128
                    si = 1 if t0 < NCTX else 0
                    ht, Bh = htile[blk % 3]
                    rd = [] if l == 0 else [X.B_H1[0 if t0 < 256 else 1 + (t0 - 256) // 512]]
                    S.dma("sp", ht[:], src[t0:t0 + 128, :], rd, Bh)
                    c0 = (blk % 4) * 2
                    B_st = B_stats[blk % 4]
                    SQACC(ht[:], sqj[:], stat[:, c0:c0 + 1], [Bh], [B_sqj, B_st])
                    TS(stat[:, c0 + 1:c0 + 2], stat[:, c0:c0 + 1], 1.0 / D, 1e-6, ALU.mult, ALU.add, [B_st], [B_st])
                    ACT(stat[:, c0 + 1:c0 + 2], stat[:, c0 + 1:c0 + 2], AF.Sqrt, [B_st], [B_st])
                    RECIP(stat[:, c0 + 1:c0 + 2], stat[:, c0 + 1:c0 + 2], [B_st], [B_st])
                    TS(ht[:], ht[:], stat[:, c0 + 1:c0 + 2], None, ALU.mult, None, [Bh, B_st], [Bh])
                    for g4 in range(4):
                        banks = [PB[p1_rr[0] % 6], PB[(p1_rr[0] + 1) % 6]]
                        p1_rr[0] += 2
                        for q in range(4):
                            fc = g4 * 4 + q
                            pt, Bp = banks[q % 2]
                            TR(pt[:, (q // 2) * 128:(q // 2 + 1) * 128], ht[:, fc * 128:(fc + 1) * 128], ident, [Bh, B_cst], [Bp])
                        for q in range(4):
                            fc = g4 * 4 + q
                            pt, Bp = banks[q % 2]
                            psl = pt[:, (q // 2) * 128:(q // 2 + 1) * 128]
                            if q % 2 == 0:
                                ACT(xnT[:, fc, t0:t0 + 128], psl, AF.Identity, [Bp, B_gsc, B_mod], [B_xnTs[fc]],
                                    bias=mod[:, fc, si:si + 1], scale=gsc[:, fc, si:si + 1])
                            else:
                                TS(xnT[:, fc, t0:t0 + 128], psl, gsc[:, fc, si:si + 1], mod[:, fc, si:si + 1],
                                   ALU.mult, ALU.add, [Bp, B_gsc, B_mod], [B_xnTs[fc]])
                if l == 0:
                    dbg_dump("xn0", xnT[:, 0, :], B_xnT) if "xn0" in X.dbg_out and False else None

                if X.stop_after == (l, 1):
                    break
                def project(col0, ncols, handler):
                    wt, Bw = load_w(X.w_in[l][:, col0:col0 + ncols].rearrange("(k p) c -> p k c", p=128), 16, ncols)
                    for ft in range(ncols // 128):
                        ftile_ = col0 // 128 + ft
                        ctx_needed = (not last) or (8 <= ftile_ < 24) or (ftile_ in (OFF_DEC // 128, OFF_ICL // 128))
                        for (t0, n) in TILES:
                            if t0 == 0 and not ctx_needed:
                                continue
                            pt, Bp = pacc()
                            for kc in range(16):
                                MM(pt[:, 0:n], wt[:, kc, ft * 128:(ft + 1) * 128], xnT[:, kc, t0:t0 + n], [Bw, B_xnTs[kc]], [Bp],
                                   start=(kc == 0), stop=(kc == 15))
                            handler(ftile_, t0, n, pt, Bp)
                        if bg is not None:
                            next(bg, None)

                ev_rr = [0]

                def h_f32(ftile, t0, n, pt, Bp):
                    s_, Bs = stg[ftile % 2]
                    eng = "act" if ev_rr[0] % 2 else "dve"
                    ev_rr[0] += 1
                    CP(eng, s_[:, t0:t0 + n], pt[:, 0:n], [Bp], [Bs])
                    if t0 + n == NT:
                        S.dma("sp", X.U32_d[ftile], s_[:], [Bs], X.B_U32[ftile])

                def h_silu(ftile, t0, n, pt, Bp):
                    idx = ftile - OFF_GA // 128
                    s_, Bs = stgb[idx % 2]
                    ACT(s_[:, t0:t0 + n], pt[:, 0:n], AF.Silu, [Bp], [Bs])
                    if t0 + n == NT:
                        S.dma("sp", X.SG_d[idx], s_[:], [Bs], X.B_SG[idx])

                def h_lora(ftile, t0, n, pt, Bp):
                    if ftile > OFF_DEC // 128 + 1:
                        return
                    if ftile == OFF_DEC // 128:
                        ACT(X.dlrT[:, t0:t0 + n], pt[:, 0:n], AF.Tanh if 'T' not in X.skip else AF.Sigmoid, [Bp], [X.B_dlrT])
                    else:
                        CP("dve", X.alrT[:, t0:t0 + n], pt[:, 0:n], [Bp], [X.B_alrT])

                def h_sig(ftile, t0, n, pt, Bp):
                    idx = ftile - OFF_MG // 128
                    s_, Bs = stgb[idx % 2]
                    ACT(s_[:, t0:t0 + n], pt[:, 0:n], AF.Sigmoid, [Bp], [Bs])
                    if t0 + n == NT:
                        S.dma("sp", X.MG_d[idx], s_[:], [Bs], X.B_MG[idx])

                for blk in range(8 if 'a' not in X.skip else 1):
                    project(blk * 512, 512, h_f32)
                for blk in range(4 if 'b' not in X.skip else 0):
                    project(OFF_GA + blk * 512, 512, h_silu)
                if 'c' not in X.skip:
                    project(OFF_DEC, 512, h_lora)
                for blk in range(8 if 'd' not in X.skip else 0):
                    project(OFF_MG + blk * 512, 512, h_sig)
                if l > 0:
                    i = ws_rr[0] % 2
                    ws_rr[0] += 1
                    wt, Bw = X.WS[i]
                    vst, B_vst = stg[0]
                    S.dma("sp", vst[:, 0:512].rearrange("p (k c) -> p k c", c=32), X.vla_d.rearrange("(k p) c -> p k c", p=128), [], B_vst)
                    CP("act", wt[:, :, 0:32], vst[:, 0:512].rearrange("p (k c) -> p k c", c=32), [B_vst], [Bw])
                    for (t0, n) in TILES:
                        pt, Bp = pacc()
                        for kc in range(16):
                            MM(pt[0:32, 0:n], wt[:, kc, 0:32], xnT[:, kc, t0:t0 + n], [Bw, B_xnTs[kc]], [Bp], start=(kc == 0), stop=(kc == 15))
                        CP("dve", X.vlT[:, t0:t0 + n], pt[0:32, 0:n], [Bp], [X.B_vlT])
                if bg is not None:
                    for _ in bg:
                        pass
                if X.stop_after == (l, 2):
                    break
            S.barrier()
            with contextlib.ExitStack() as sB:
                stopped = False
                with contextlib.ExitStack() as s3:
                    X.ygst = [sb(s3, "ygst%d" % i, [128, 512], BF16) for i in range(2)]
                    try:
                        phase3(X, s3)
                    except _Stop:
                        stopped = True
                S.barrier()
                if stopped or X.stop_after == (l, 3):
                    break
                X.yg, X.B_yg = sb(sB, "yg", [128, 8, NT], BF16)
                for j_ in range(8):
                    S.dma("sp", X.yg[:, j_, :], X.YG_d[j_], [X.B_YG], X.B_yg)
                X.yb, X.B_yb = sb(sB, "yb", [128, 8, NT], BF16)
                with contextlib.ExitStack() as s4:
                    phase4(X, s4)
                S.barrier()
                with contextlib.ExitStack() as s5:
                    make_wslots(s5)
                    phase5a(X, s5)
            S.barrier()
            with contextlib.ExitStack() as s5b:
                phase5b(X, s5b)
            S.barrier()

        S.emit(final_waits=X.finals)
    return nc


def phase3(X, stk):
    S, l, last = X.S, X.l, X.last
    MM, TR, ACT, CP, TT, TS, STT, RECIP, SCAN, MEMSET = X.MM, X.TR, X.ACT, X.CP, X.TT, X.TS, X.STT, X.RECIP, X.SCAN, X.MEMSET
    PB, PTRt, cst, B_cst, B_pv, pcol = X.PB, X.PTRt, X.cst, X.B_cst, X.B_pv, X.pcol
    identb, B_identb, obb, B_obb, ob64, B_ob64 = X.identb, X.B_identb, X.obb, X.B_obb, X.ob64, X.B_ob64
    dlrT, B_dlrT, alrT, B_alrT, vlT, B_vlT = X.dlrT, X.B_dlrT, X.alrT, X.B_alrT, X.vlT, X.B_vlT
    T32 = X.T32

    def sb(name, shape, dt=F32):
        return X.sb(stk, "p3" + name, shape, dt)

    XS = [sb("xs%d" % i, [128, NT]) for i in range(3)]
    KK, B_KK = sb("kk", [128, NT])
    SGa, B_SGa = sb("sga", [128, NT], BF16)
    LW, B_lw = sb("lw", [128, 2, 1024], BF16)
    VLB, B_vlb = sb("vlb", [32, 1024], BF16)
    VP, B_VP = sb("vp", [128, 8, 2, 64], BF16)
    RKB, B_RKB = sb("rkb", [128, 512], BF16)
    RKBs = None
    SQ, B_SQ = sb("sq", [128, 512])
    t32b, _ = sb("t32b", [128, 8 * 512])
    temps = [T32[0:8], [(t32b[:, i * 512:(i + 1) * 512], Buf("t32b_%d" % i)) for i in range(8)]]
    RKBs = [(RKB, B_RKB), (temps[1][0][0].bitcast(BF16)[:, 0:512], temps[1][0][1]), (temps[1][1][0].bitcast(BF16)[:, 0:512], temps[1][1][1])]
    PTRh = [(PTRt[:, 0:512], Buf("ptr0")), (X.PTRt2[:, 0:512], Buf("ptr1"))]
    PSH, B_PSH = PB[0]

    class St:
        pass
    STR = []
    for d in range(2):
        Z = St()
        Z.d = d
        Z.KB = sb("kb%d" % d, [128, NT], BF16)
        Z.Y = sb("y%d" % d, [128, NT])
        Z.UV = sb("uv%d" % d, [128, NCH, 2, 64], BF16)
        Z.AR = sb("ar%d" % d, [128, 8, 2, 64], BF16)
        Z.BK = sb("bk%d" % d, [128, 8, 2, 64], BF16)
        Z.BKp = sb("bkp%d" % d, [128, 8, 2, 64], BF16)
        Z.BKpT = sb("bkpt%d" % d, [128, 8, 128], BF16)
        Z.AqT = sb("aqt%d" % d, [64, 8, 128], BF16)
        Z.GL = sb("gl%d" % d, [128, 8, 64], BF16)
        Z.GR = sb("gr%d" % d, [128, 16, 64], BF16)
        Z.Qs = [sb("q%d_%d" % (d, i), [64, 8, 64], BF16) for i in range(2)]
        Z.Ps = [sb("p%d_%d" % (d, i), [64, 8, 64], BF16) for i in range(2)]
        Z.Ts = [sb("tt%d_%d" % (d, i), [64, 8, 64], BF16) for i in range(2)]
        Z.XL = sb("xl%d" % d, [64, 8, 64], BF16)
        Z.UL = sb("ul%d" % d, [64, 16, 64], BF16)
        Z.AqP = sb("aqp%d" % d, [128, 8, 64], BF16)
        Z.PC = sb("pc%d" % d, [128, 8])
        Z.ST32 = sb("st32_%d" % d, [128, 64])
        Z.STb = sb("stb%d" % d, [128, 64], BF16)
        Z.T = temps[d]
        Z.s0, Z.s1, Z.s2 = PB[3 * d], PB[3 * d + 1], PB[3 * d + 2]
        Z.PTR = PTRh[d]
        STR.append(Z)
    X32, B_X32 = STR[1].Y
    MEMSET("dve", VP[:], 0.0, [B_VP])
    for which, src_d in ((0, X.dlb_d), (1, X.ilb_d)):
        f_ = X.t32big[:, which * 1024:(which + 1) * 1024]
        Bs_ = [T32[which * 2][1], T32[which * 2 + 1][1]]
        S.op("dve", lambda e, a=T32[which * 2 + 1][0][:, 0:1]: e.memset(a, 0.0), [], [Bs_[1]])
        S.dma("sp", f_, src_d[l], [Bs_[1]], Bs_[0])
        CP("act", LW[:, which, :], f_, Bs_, [B_lw])
    if l > 0:
        f_ = X.t32big[0:32, 4 * 512:6 * 512]
        Bs_ = [T32[4][1], T32[5][1]]
        S.op("dve", lambda e, a=T32[5][0][:, 0:1]: e.memset(a, 0.0), [], [Bs_[1]])
        S.dma("sp", f_, X.vlb_d, [Bs_[1]], Bs_[0])
        CP("act", VLB[:], f_, Bs_, [B_vlb])

    mq = [cst[0:64, CS["mq0"]:CS["mq0"] + 64], cst[0:64, CS["mq1"]:CS["mq1"] + 64]]
    mgm = [cst[:, CS["mg0"]:CS["mg0"] + 128], cst[:, CS["mg1"]:CS["mg1"] + 128]]
    rst = cst[:, CS["rst"]:CS["rst"] + 512]
    id64 = cst[0:64, CS["ident"]:CS["ident"] + 64]

    def shift_mix(j, src, dst, Bs, Bd, mu_col, ommu_col):
        def mix(dsl_d, sl_from, sl_self, bnd_d, bnd_s):
            TT(dsl_d, sl_from, sl_self, ALU.subtract, [Bs], [Bd])
            STT(dsl_d, dsl_d, mu_col, sl_self, ALU.mult, ALU.add, [Bs, Bd, B_pv], [Bd])
            TS(bnd_d, bnd_s, ommu_col, None, ALU.mult, None, [Bs, B_pv], [Bd])
        if j < 4:
            mix(dst[:, 1:NCTX], src[:, 0:NCTX - 1], src[:, 1:NCTX], dst[:, 0:1], src[:, 0:1])
        else:
            mix(dst[:, 0:NCTX - 1], src[:, 1:NCTX], src[:, 0:NCTX - 1], dst[:, NCTX - 1:NCTX], src[:, NCTX - 1:NCTX])
        s3 = src[:, NCTX:NT].rearrange("p (r c) -> p r c", c=64)
        d3 = dst[:, NCTX:NT].rearrange("p (r c) -> p r c", c=64)
        q = j // 2
        if q == 0:
            mix(d3[:, :, 1:64], s3[:, :, 0:63], s3[:, :, 1:64], d3[:, :, 0:1], s3[:, :, 0:1])
        elif q == 1:
            mix(d3[:, :, 0:63], s3[:, :, 1:64], s3[:, :, 0:63], d3[:, :, 63:64], s3[:, :, 63:64])
        elif q == 2:
            mix(d3[:, 1:32, :], s3[:, 0:31, :], s3[:, 1:32, :], d3[:, 0:1, :], s3[:, 0:1, :])
        else:
            mix(d3[:, 0:31, :], s3[:, 1:32, :], s3[:, 0:31, :], d3[:, 31:32, :], s3[:, 31:32, :])

    def c3(ap2):
        return ap2.rearrange("p (c t) -> p c t", t=64)

    def rr(gens):
        gens = list(gens)
        while gens:
            for g in list(gens):
                try:
                    next(g)
                except StopIteration:
                    gens.remove(g)

    (RS, B_RS), (KS, B_KS), (VS, B_VS) = XS
    X32b, B_X32b = STR[0].Y

    def sweep(j, Z):
        d = Z.d
        (KB, B_KB), (Yd, B_Yd), (UV, B_UV) = Z.KB, Z.Y, Z.UV
        (AR, B_AR), (BK, B_BK), (BKp, B_BKp) = Z.AR, Z.BK, Z.BKp
        (BKpT, B_BKpT), (AqT, B_AqT), (GL, B_GL), (GR, B_GR) = Z.BKpT, Z.AqT, Z.GL, Z.GR
        Qs, Ps, Ts = Z.Qs, Z.Ps, Z.Ts
        (XL, B_XL), (UL, B_UL), (AqP, B_AqP), (PC, B_PC) = Z.XL, Z.UL, Z.AqP, Z.PC
        (ST32, B_ST32), (STb, B_STb) = Z.ST32, Z.STb
        (P0, B_P0), (P1, B_P1), (P2, B_P2) = Z.s0, Z.s1, Z.s2
        PTRd, B_PTRd = Z.PTR
        lw = LW[:, :, j * 128:(j + 1) * 128]
        order = [0, 1, 2, 3, 4] if d == 0 else [0, 4, 3, 2, 1]
        MEMSET("dve", ST32[:], 0.0, [B_ST32])
        MEMSET("dve", STb[:], 0.0, [B_STb])
        dsl = slice(d * 64, (d + 1) * 64)
        for ti in order:
            t0, n = TILES[ti]
            nch = n // 64
            c0 = t0 // 64
            (A32, B_A32), (SIG, B_SIG), (LS, B_LS), (EA, B_EA), (EB, B_EB), (KD, B_KD), (KA, B_KA), (TMP, B_TMP) = Z.T
            MM(P0[:, 0:n], lw[dsl, 1, :], alrT[dsl, t0:t0 + n], [B_lw, B_alrT], [B_P0])
            ACT(A32[:, 0:n], P0[:, 0:n], AF.Sigmoid, [B_P0, B_pv], [B_A32], bias=pcol(l, "a0", d * 8 + j))
            MM(P0[:, 0:n], lw[dsl, 0, :], dlrT[dsl, t0:t0 + n], [B_lw, B_dlrT], [B_P0])
            ACT(SIG[:, 0:n], P0[:, 0:n], AF.Sigmoid, [B_P0, B_pv], [B_SIG], bias=pcol(l, "w0", d * 8 + j))
            if 'a' not in X.noy:
                yield
            SCAN(LS[:, 0:n], rst[:, 0:n], SIG[:, 0:n], [B_cst, B_SIG], [B_LS])
            L3 = c3(LS[:, 0:n])
            S3 = c3(SIG[:, 0:n])
            T3 = c3(TMP[:, 0:n])
            if d == 1:
                TT(T3, L3[:, :, 63:64].to_broadcast([128, nch, 64]), L3, ALU.subtract, [B_LS], [B_TMP])
                TT(L3, T3, S3, ALU.add, [B_TMP, B_SIG], [B_LS])
                endc = 0
            else:
                endc = 63
            if 'a' not in X.noy:
                yield
            ACT(KD[:, 0:n], A32[:, 0:n], AF.Identity, [B_A32, B_pv], [B_KD], scale=pcol(l, "ka", d * 8 + j), bias=pcol(l, "omka", d * 8 + j))
            TT(KD[:, 0:n], KD[:, 0:n], KS[:, t0:t0 + n], ALU.mult, [B_KD, B_KS], [B_KD])
            TT(KA[:, 0:n], KK[:, t0:t0 + n], A32[:, 0:n], ALU.mult, [B_KK, B_A32], [B_KA])
            ACT(KB[:, t0:t0 + n], KD[:, 0:n], AF.Identity, [B_KD, B_pv], [B_KB], scale=pcol(l, "brk", d * 8 + j))
            if 'a' not in X.noy:
                yield
            TT(TMP[:, 0:n], LS[:, 0:n], SIG[:, 0:n], ALU.subtract, [B_LS, B_SIG], [B_TMP])
            ACT(EA[:, 0:n], TMP[:, 0:n], AF.Exp, [B_TMP], [B_EA], scale=-C0)
            ACT(EB[:, 0:n], LS[:, 0:n], AF.Exp, [B_LS], [B_EB], scale=-C0)
            STT(AR[:, 0:nch, 0, :], c3(KK[:, t0:t0 + n]), -1.0, c3(EA[:, 0:n]), ALU.mult, ALU.mult, [B_KK, B_EA], [B_AR])
            if 'a' not in X.noy:
                yield
            TT(AR[:, 0:nch, 1, :], c3(RS[:, t0:t0 + n]), c3(EB[:, 0:n]), ALU.mult, [B_RS, B_EB], [B_AR])
            CP("act", PC[:, 0:nch], c3(EB[:, 0:n])[:, :, endc:endc + 1].rearrange("p c o -> p (c o)"), [B_EB], [B_PC])
            ACT(EA[:, 0:n], LS[:, 0:n], AF.Exp, [B_LS], [B_EA], scale=C0)
            if 'a' not in X.noy:
                yield
            TT(BK[:, 0:nch, 0, :], c3(KA[:, 0:n]), c3(EA[:, 0:n]), ALU.mult, [B_KA, B_EA], [B_BK])
            TT(BK[:, 0:nch, 1, :], c3(KD[:, 0:n]), c3(EA[:, 0:n]), ALU.mult, [B_KD, B_EA], [B_BK])
            if 'a' not in X.noy:
                yield
            TT(BKp[:, 0:nch, :, :].rearrange("p c a t -> p c (a t)"), BK[:, 0:nch, :, :].rearrange("p c a t -> p c (a t)"),
               PC[:, 0:nch].unsqueeze(2).to_broadcast([128, nch, 128]), ALU.mult, [B_BK, B_PC], [B_BKp])
            if 'a' not in X.noy:
                yield
            for r0 in range(0, nch, 4):
                for c in range(r0, r0 + 4):
                    TR(PTRd[:, (c - r0) * 128:(c - r0 + 1) * 128], BKp[:, c, :, :].rearrange("p a t -> p (a t)"), identb[:],
                       [B_BKp, B_identb], [B_PTRd])
                CP("act", BKpT[:, r0:r0 + 4, :], PTRd[:, 0:512].rearrange("p (c f) -> p c f", f=128), [B_PTRd], [B_BKpT])
                if 'b' not in X.noy:
                    yield
                for c in range(r0, r0 + 4):
                    TR(PTRd[0:64, (c - r0) * 128:(c - r0 + 1) * 128], AR[:, c, 0, :], identb[:], [B_AR, B_identb], [B_PTRd])
                CP("act", AqT[:, r0:r0 + 4, :], PTRd[0:64, 0:512].rearrange("p (c f) -> p c f", f=128), [B_PTRd], [B_AqT])
                if 'b' not in X.noy:
                    yield
            for g0 in range(0, nch, 4):
                units = [(g0 + cl, h) for h in range(2) for cl in range(4)]
                for u, (c, h) in enumerate(units):
                    hs = slice(h * 64, (h + 1) * 64)
                    PGt, B_PGt = (P1, B_P1) if h == 0 else (P2, B_P2)
                    uo = (u % 4) * 128
                    MM(PGt[:, uo:uo + 128], BK[hs, c, :, :].rearrange("p a t -> p (a t)"), AR[hs, c, :, :].rearrange("p a t -> p (a t)"),
                       [B_BK, B_AR], [B_PGt])
                if 'c' not in X.noy:
                    yield
                for half, (PGt, B_PGt) in enumerate(((P1, B_P1), (P2, B_P2))):
                    p4 = PGt[:, :].rearrange("p (u f) -> p u f", f=128)
                    mb = mgm[d].unsqueeze(1).to_broadcast([128, 4, 128])
                    TT(GL[:, half * 4:half * 4 + 4, :], p4[:, :, 0:64], mb[:, :, 0:64], ALU.mult, [B_PGt, B_cst], [B_GL])
                    TT(GR[:, g0 * 2 + half * 4:g0 * 2 + half * 4 + 4, :], p4[:, :, 64:128], mb[:, :, 64:128], ALU.mult, [B_PGt, B_cst], [B_GR])
                for u, (c, h) in enumerate(units):
                    hs = slice(h * 64, (h + 1) * 64)
                    PQh, B_PQh = (P0, B_P0) if h == 0 else (P1, B_P1)
                    MM(PQh[0:64, (u % 4) * 64:(u % 4 + 1) * 64], AR[hs, c, 0, :], BK[hs, c, 0, :], [B_AR, B_BK], [B_PQh])
                if 'c' not in X.noy:
                    yield
                Q0, B_Q0 = Qs[0]
                TT(Q0[:, 0:4, :], P0[0:64, 0:256].rearrange("p (u f) -> p u f", f=64), mq[d].unsqueeze(1).to_broadcast([64, 4, 64]), ALU.mult,
                   [B_P0, B_cst], [B_Q0])
                TT(Q0[:, 4:8, :], P1[0:64, 0:256].rearrange("p (u f) -> p u f", f=64), mq[d].unsqueeze(1).to_broadcast([64, 4, 64]), ALU.mult,
                   [B_P1, B_cst], [B_Q0])
                T0, B_T0 = Ts[0]
                TT(T0[:], GL[0:64, :, :], id64.unsqueeze(1).to_broadcast([64, 8, 64]), ALU.add, [B_GL, B_cst], [B_T0])
                if 'c' not in X.noy:
                    yield
                Pprev, B_Pprev = GL[0:64, :, :], B_GL
                Qprev, B_Qprev = Q0[:], B_Q0
                Tprev, B_Tprev = T0[:], B_T0
                for lvl in range(1, 6):
                    Qn, B_Qn = Qs[lvl % 2]
                    Pn, B_Pn = Ps[lvl % 2]
                    Tn, B_Tn = Ts[lvl % 2]
                    for u in range(8):
                        MM(P0[0:64, u * 64:(u + 1) * 64], Pprev[:, u, :], Qprev[:, u, :], [B_Pprev, B_Qprev], [B_P0])
                    if lvl < 5:
                        for u in range(8):
                            MM(P1[0:64, u * 64:(u + 1) * 64], Qprev[:, u, :], Pprev[:, u, :], [B_Pprev, B_Qprev], [B_P1])
                    if 'c' not in X.noy:
                        yield
                    CP("act", Qn[:], P0[0:64, :].rearrange("p (u f) -> p u f", f=64), [B_P0], [B_Qn])
                    if lvl < 5:
                        CP("act", Pn[:], P1[0:64, :].rearrange("p (u f) -> p u f", f=64), [B_P1], [B_Pn])
                    if 'c' not in X.noy:
                        yield
                    for u in range(8):
                        MM(P2[0:64, u * 64:(u + 1) * 64], Qn[:, u, :], Tprev[:, u, :], [B_Qn, B_Tprev], [B_P2])
                    TT(Tn[:], P2[0:64, :].rearrange("p (u f) -> p u f", f=64), Tprev, ALU.add, [B_P2, B_Tprev], [B_Tn])
                    if 'c' not in X.noy:
                        yield
                    if lvl < 5:
                        Pprev, B_Pprev = Pn[:], B_Pn
                    Qprev, B_Qprev = Qn[:], B_Qn
                    Tprev, B_Tprev = Tn[:], B_Tn
                Tf, B_Tf = Tprev, B_Tprev
                for u, (c, h) in enumerate(units):
                    MM(P1[h * 64:(h + 1) * 64, (u % 4) * 64:(u % 4) * 64 + 64], AqT[:, c, h * 64:(h + 1) * 64], Tf[:, u, :],
                       [B_AqT, B_Tf], [B_P1])
                for u, (c, h) in enumerate(units):
                    MM(P0[0:64, u * 64:(u + 1) * 64], GL[64:128, u, :], UV[64:128, c0 + c, h, :], [B_GL, B_UV], [B_P0])
                if 'c' not in X.noy:
                    yield
                CP("act", AqP[:, g0:g0 + 4, :], P1[:, 0:256].rearrange("p (c f) -> p c f", f=64), [B_P1], [B_AqP])
                CP("act", XL[:], P0[0:64, :].rearrange("p (u f) -> p u f", f=64), [B_P0], [B_XL])
                if 'c' not in X.noy:
                    yield
                for u in range(8):
                    MM(P2[0:64, u * 64:(u + 1) * 64], Tf[:, u, :], XL[:, u, :], [B_Tf, B_XL], [B_P2])
                CP("act", UL[:, g0 * 2:g0 * 2 + 8, :], P2[0:64, :].rearrange("p (u f) -> p u f", f=64), [B_P2], [B_UL])
                if 'c' not in X.noy:
                    yield
            corder = list(range(nch)) if d == 0 else list(range(nch - 1, -1, -1))
            for c in corder:
                cg = c0 + c

                def gi_(h_):
                    return (c // 4) * 8 + h_ * 4 + (c % 4)
                PYs = ((P0, B_P0), (P2, B_P2))
                for h in range(2):
                    hs = slice(h * 64, (h + 1) * 64)
                    MM(PYs[h][0][hs, c * 64:(c + 1) * 64], STb[hs, :], AR[hs, c, 1, :], [B_STb, B_AR], [PYs[h][1]], start=True, stop=False)
                hs0, hs1 = slice(0, 64), slice(64, 128)
                MM(P1[0:64, 0:64], AqP[hs0, c, :], STb[hs0, :], [B_AqP, B_STb], [B_P1])
                MM(P2[0:64, 0:64], AqP[hs1, c, :], STb[hs1, :], [B_AqP, B_STb], [B_P2])
                if 'e' not in X.noy:
                    yield
                TT(UV[0:64, cg, 0, :], P1[0:64, 0:64], UL[:, gi_(0), :], ALU.add, [B_P1, B_UL], [B_UV])
                TT(UV[0:64, cg, 1, :], P2[0:64, 0:64], UL[:, gi_(1), :], ALU.add, [B_P2, B_UL], [B_UV])
                if 'e' not in X.noy:
                    yield
                for h in range(2):
                    hs = slice(h * 64, (h + 1) * 64)
                    MM(P1[hs, 64:128], BKpT[:, c, hs], UV[:, cg, h, :], [B_BKpT, B_UV], [B_P1])
                for h in range(2):
                    hs = slice(h * 64, (h + 1) * 64)
                    MM(PYs[h][0][hs, c * 64:(c + 1) * 64], UV[:, cg, h, :], GR[:, gi_(h), :], [B_UV, B_GR], [PYs[h][1]], start=False, stop=True)
                if 'e' not in X.noy:
                    yield
                STT(STb[:], ST32[:], PC[:, c:c + 1], P1[:, 64:128], ALU.mult, ALU.add, [B_ST32, B_PC, B_P1], [B_STb])
                STT(ST32[:], ST32[:], PC[:, c:c + 1], P1[:, 64:128], ALU.mult, ALU.add, [B_ST32, B_PC, B_P1], [B_ST32])
                if 'e' not in X.noy:
                    yield
            CP("act", Yd[0:64, t0:t0 + n], P0[0:64, 0:n], [B_P0], [B_Yd])
            CP("act", Yd[64:128, t0:t0 + n], P2[64:128, 0:n], [B_P2], [B_Yd])
            if 'a' not in X.noy:
                yield

    for j in range(X.nj):
        S.dma("sp", SGa[:], X.SG_d[j], [X.B_SG[j]], B_SGa)
        vlb = VLB[:, j * 128:(j + 1) * 128]
        lbufs = [(X32, B_X32), (X32b, B_X32b), (X32, B_X32)]

        def ld(m):
            S.dma("sp", lbufs[m][0][:], X.U32_d[m * 8 + j], [X.B_U32[m * 8 + j]], lbufs[m][1])

        def sh(m):
            shift_mix(j, lbufs[m][0], XS[m][0], lbufs[m][1], XS[m][1], pcol(l, "mu", m * 8 + j), pcol(l, "ommu", m * 8 + j))
        ld(0)
        ld(1)
        sh(0)
        ld(2)
        sh(1)
        sh(2)
        if l == 0:
            S.dma("sp", X.VF_d[j], VS[:], [B_VS], X.B_VF[j])
        else:
            S.dma("sp", X32[:], X.VF_d[j], [X.B_VF[j]], B_X32)

            def vres_chain(i, t0, n):
                pb_, Bpb_ = PB[i]
                g_, Bg = T32[2 * i]
                d_, Bd_ = T32[2 * i + 1]
                MM(pb_[0:128, 0:n], vlb, vlT[:, t0:t0 + n], [B_vlb, B_vlT], [Bpb_])
                yield
                ACT(g_[:, 0:n], pb_[:, 0:n], AF.Sigmoid, [Bpb_, B_pv], [Bg], bias=pcol(l, "v0", j))
                TT(d_[:, 0:n], X32[:, t0:t0 + n], VS[:, t0:t0 + n], ALU.subtract, [B_X32, B_VS], [Bd_])
                yield
                TT(d_[:, 0:n], d_[:, 0:n], g_[:, 0:n], ALU.mult, [Bd_, Bg], [Bd_])
                yield
                TT(VS[:, t0:t0 + n], VS[:, t0:t0 + n], d_[:, 0:n], ALU.add, [B_VS, Bd_], [B_VSt[i]])
                yield
            B_VSt = [Buf("vs_t%d_%d_%d" % (l, j, i)) for i in range(5)]
            rr([vres_chain(i, t0, n) for i, (t0, n) in enumerate(TILES)])
            S.op("dve", lambda e, a=T32[11][0][:, 0:1]: e.memset(a, 0.0), B_VSt, [T32[11][1], B_VS])

        def kk_chain(i, t0, n):
            pb_, Bpb_ = PB[i]
            kr, Bkr = T32[3 * i]
            nr, Bnr = T32[3 * i + 1]
            sq_, Bsq = T32[3 * i + 2]
            ACT(kr[:, 0:n], KS[:, t0:t0 + n], AF.Identity, [B_KS, B_pv], [Bkr], scale=pcol(l, "kk", j))
            yield
            TT(sq_[:, 0:n], kr[:, 0:n], kr[:, 0:n], ALU.mult, [Bkr], [Bsq])
            yield
            MM(pb_[:, 0:n], cst[:, CS["ob"]:CS["ob"] + 128], sq_[:, 0:n], [B_cst, Bsq], [Bpb_])
            yield
            ACT(nr[:, 0:n], pb_[:, 0:n], AF.Sqrt, [Bpb_], [Bnr])
            yield
            TS(nr[:, 0:n], nr[:, 0:n], 1e-12, None, ALU.max, None, [Bnr], [Bnr])
            RECIP(nr[:, 0:n], nr[:, 0:n], [Bnr], [Bnr])
            yield
            TT(KK[:, t0:t0 + n], kr[:, 0:n], nr[:, 0:n], ALU.mult, [Bkr, Bnr], [B_KKt[i]])
            yield
        B_KKt = [Buf("kk_t%d_%d_%d" % (l, j, i)) for i in range(5)]
        if 'K' in X.skip:
            for i, (t0, n) in enumerate(TILES[0:4]):
                rr([kk_chain(i, t0, n)])
        else:
            rr([kk_chain(i, t0, n) for i, (t0, n) in enumerate(TILES[0:4])])
        rr([kk_chain(0, *TILES[4])])
        S.op("dve", lambda e, a=T32[11][0][:, 1:2]: e.memset(a, 0.0), B_KKt, [T32[11][1], B_KK])
        Bp01 = [PTRh[0][1], PTRh[1][1]]
        for (t0, n) in TILES:
            nch = n // 64
            c0 = t0 // 64
            CP("act", VP[:, 0:nch, 1, :], c3(VS[:, t0:t0 + n]), [B_VS], [B_VP])
            for c in range(nch):
                TR(PTRt[:, c * 128:(c + 1) * 128], VP[:, c, :, :].rearrange("p a t -> p (a t)"), identb[:], [B_VP, B_identb], Bp01[0:1])
            for Z in STR:
                CP("dve", Z.UV[0][64:128, c0:c0 + nch, :, :].rearrange("p c h v -> p c (h v)"),
                   PTRt[64:128, 0:nch * 128].rearrange("p (c f) -> p c f", f=128), Bp01[0:1], [Z.UV[1]])
        if X.p3stop == 3:
            raise _Stop()
        gens = [sweep(j, STR[0]), sweep(j, STR[1])]
        if 'Q' in X.skip:
            for g in gens:
                for _ in g:
                    pass
            gens = []
        while gens:
            for g in list(gens):
                try:
                    next(g)
                except StopIteration:
                    gens.remove(g)
        (Y0, B_Y0), (Y1, B_Y1) = STR[0].Y, STR[1].Y
        (KB0, B_KB0), (KB1, B_KB1) = STR[0].KB, STR[1].KB

        def fin_chain(k, ti, t0, n):
            (YS, B_YS), (YC, B_YC), (BON, B_BON), (KBS, B_KBS) = T32[4 * k:4 * k + 4]
            SQf, B_SQf = KBS, B_KBS
            pb_, Bpb_ = PB[k]
            TT(YS[:, 0:n], Y0[:, t0:t0 + n], Y1[:, t0:t0 + n], ALU.add, [B_Y0, B_Y1], [B_YS])
            TT(KBS[:, 0:n], KB0[:, t0:t0 + n], KB1[:, t0:t0 + n], ALU.add, [B_KB0, B_KB1], [B_KBS])
            yield
            rkb, B_rkb = RKBs[k]
            TT(rkb[:, 0:n], RS[:, t0:t0 + n], KBS[:, 0:n], ALU.mult, [B_RS, B_KBS], [B_rkb])
            yield
            MM(pb_[:, 0:n], obb[:], rkb[:, 0:n], [B_obb, B_rkb], [Bpb_])
            yield
            TT(BON[:, 0:n], pb_[:, 0:n], VS[:, t0:t0 + n], ALU.mult, [Bpb_, B_VS], [B_BON])
            MM(pb_[:, 0:n], ob64[:], YS[:, 0:n], [B_ob64, B_YS], [Bpb_])
            yield
            TT(YC[:, 0:n], YS[:, 0:n], pb_[:, 0:n], ALU.subtract, [B_YS, Bpb_], [B_YC])
            yield
            TT(SQf[:, 0:n], YC[:, 0:n], YC[:, 0:n], ALU.mult, [B_YC], [B_SQf])
            yield
            MM(pb_[:, 0:n], ob64[:], SQf[:, 0:n], [B_ob64, B_SQf], [Bpb_])
            yield
            TS(YS[:, 0:n], pb_[:, 0:n], GN_EPS, None, ALU.add, None, [Bpb_], [B_YS])
            yield
            ACT(YS[:, 0:n], YS[:, 0:n], AF.Sqrt, [B_YS], [B_YS])
            yield
            RECIP(YS[:, 0:n], YS[:, 0:n], [B_YS], [B_YS])
            yield
            TT(YC[:, 0:n], YC[:, 0:n], YS[:, 0:n], ALU.mult, [B_YC, B_YS], [B_YC])
            yield
            ACT(YC[:, 0:n], YC[:, 0:n], AF.Identity, [B_YC, B_pv], [B_YC], scale=pcol(l, "gnw", j), bias=pcol(l, "gnb", j))
            yield
            TT(YC[:, 0:n], YC[:, 0:n], BON[:, 0:n], ALU.add, [B_YC, B_BON], [B_YC])
            yield
            ygt, B_ygt = X.ygst[(j * 5 + ti) % 2]
            TT(ygt[:, 0:n], YC[:, 0:n], SGa[:, t0:t0 + n], ALU.mult, [B_YC, B_SGa], [B_ygt])
            S.dma("sp", X.YG_d[j][:, t0:t0 + n], ygt[:, 0:n], [B_ygt], X.B_YG)
            yield
        ftiles = [(ti, t0, n) for ti, (t0, n) in enumerate(TILES) if not (last and ti == 0)]
        NF = 3
        for i0 in range(0, len(ftiles), NF):
            if 'F' in X.skip:
                for k in range(min(NF, len(ftiles) - i0)):
                    rr([fin_chain(k, *ftiles[i0 + k])])
            else:
                rr([fin_chain(k, *ftiles[i0 + k]) for k in range(min(NF, len(ftiles) - i0))])


def phase4(X, stk):
    S, l, last = X.S, X.l, X.last
    MM, TT, STT, MEMSET = X.MM, X.TT, X.STT, X.MEMSET
    PB, cst, B_cst, B_pv, pcol = X.PB, X.cst, X.B_cst, X.B_pv, X.pcol
    yb, B_yb = X.yb, X.B_yb
    T32 = X.T32

    def sb(name, shape, dt=F32):
        return X.sb(stk, "p4" + name, shape, dt)
    PPs = [sb("pp%d" % i, [128, NT + 32]) for i in range(2)]
    PW = [sb("w%d" % i, [128, 2, 128], BF16) for i in range(2)]
    PLD = [sb("pl%d" % i, [128, NT], BF16) for i in range(2)]
    SGb, B_SGb = sb("sgb", [128, NT], BF16)
    WA, B_WA = sb("wa", [128, 2080])
    WB, B_WB = sb("wb", [128, 2080])
    for t_, B_ in PPs:
        MEMSET("dve", t_[:], 0.0, [B_])
    wins = (2, 4, 8, 16)
    segs = [(8, NCTX, 0), (NCTX + 24, NLAT, NCTX)]
    for gi in range(4):
        w = wins[gi]
        for k2 in range(2):
            ptile = gi * 2 + k2
            pp, B_pp = PPs[k2]
            pl, B_pl = PLD[k2]
            for (po, ln, to) in segs:
                S.dma("sp", pp[:, po:po + ln], X.U32_d[24 + ptile][:, to:to + ln], [X.B_U32[24 + ptile]], B_pp)
            for (po, ln, to) in segs:
                for s0 in range(0, ln, 2048):
                    n = min(2048, ln - s0)
                    base = po + s0
                    cur, B_cur, nxt, B_nxt = WA, B_WA, WB, B_WB
                    TT(cur[:, 1:n + 15], pp[:, base - 8:base + n + 6], pp[:, base - 7:base + n + 7], ALU.add, [B_pp], [B_cur])
                    ww, lo, hi = 2, 1, n + 15
                    while ww < w:
                        hf = ww // 2
                        lo2, hi2 = lo + hf, hi - hf
                        TT(nxt[:, lo2:hi2], cur[:, lo2 - hf:hi2 - hf], cur[:, lo2 + hf:hi2 + hf], ALU.add, [B_cur], [B_nxt])
                        cur, B_cur, nxt, B_nxt = nxt, B_nxt, cur, B_cur
                        lo, hi = lo2, hi2
                        ww *= 2
                    STT(pl[:, to + s0:to + s0 + n], cur[:, 8:8 + n], 1.0 / w, pp[:, base:base + n], ALU.mult, ALU.subtract, [B_cur, B_pp], [B_pl])
                    tm, B_tm = T32[2]
                    if s0 == 0:
                        cc = cst[:, CS["pcor"] + gi * 16:CS["pcor"] + gi * 16 + 8]
                        TT(tm[:, 0:8], cur[:, 8:16], cc, ALU.mult, [B_cur, B_cst], [B_tm])
                        TT(pl[:, to:to + 8], tm[:, 0:8], pp[:, base:base + 8], ALU.subtract, [B_tm, B_pp], [B_pl])
                    if s0 + n == ln:
                        cc = cst[:, CS["pcor"] + gi * 16 + 8:CS["pcor"] + gi * 16 + 16]
                        TT(tm[:, 8:16], cur[:, n:n + 8], cc, ALU.mult, [B_cur, B_cst], [B_tm])
                        TT(pl[:, to + ln - 8:to + ln], tm[:, 8:16], pp[:, base + n - 8:base + n], ALU.subtract, [B_tm, B_pp], [B_pl])
        for k2 in range(2):
            wt, Bw = PW[k2]
            f_, Bf_ = T32[4 + k2]
            S.dma("sp", f_[:, 0:256], X.pw_d[l][gi][k2 * 128:(k2 + 1) * 128, :], [], Bf_)
            X.CP("act", wt[:], f_[:, 0:256].rearrange("p (o c) -> p o c", c=128), [Bf_], [Bw])
        for o2 in range(2):
            otile = gi * 2 + o2
            S.dma("sp", SGb[:], X.SG_d[8 + otile], [X.B_SG[8 + otile]], B_SGb)
            for ti, (t0, n) in enumerate(TILES):
                if last and ti == 0:
                    continue
                pt, Bp = PB[(o2 + ti) % 2]
                for k2 in range(2):
                    MM(pt[:, 0:n], PW[k2][0][:, o2, :], PLD[k2][0][:, t0:t0 + n], [PW[k2][1], PLD[k2][1]], [Bp], start=(k2 == 0), stop=(k2 == 1))
                STT(yb[:, otile, t0:t0 + n], pt[:, 0:n], pcol(l, "psc", otile), SGb[:, t0:t0 + n], ALU.mult, ALU.mult, [Bp, B_pv, B_SGb], [B_yb])


def phase5a(X, stk):
    S, l, last = X.S, X.l, X.last
    MM, TT = X.MM, X.TT
    PB = X.PB
    yg, B_yg, yb, B_yb = X.yg, X.B_yg, X.yb, X.B_yb

    def sb(name, shape, dt=F32):
        return X.sb(stk, "p5a" + name, shape, dt)
    MGT = [sb("mg%d" % i, [128, 2, NT], BF16) for i in range(2)]
    MST = [sb("ms%d" % i, [128, NT], BF16) for i in range(2)]
    YA = [X.T32[0], X.T32[1]]
    tiles = [(ti, t0, n) for ti, (t0, n) in enumerate(TILES) if not (last and ti == 0)]
    tlo = tiles[0][1]
    rr = 0
    for fb in range(4):
        wab, Bwab = X.WS[fb % 2]
        S.dma("pool", wab[:, 0:8, :], X.woa_d[l][:, fb * 512:(fb + 1) * 512].rearrange("(k p) c -> p k c", p=128), [], Bwab)
        S.dma("pool", wab[:, 8:16, :], X.wob_d[l][:, fb * 512:(fb + 1) * 512].rearrange("(k p) c -> p k c", p=128), [Bwab], Bwab)
        wa, Bwa = wab[:, 0:8, :], Bwab
        wb, Bwb = wab[:, 8:16, :], Bwab
        for q in range(4):
            f = fb * 4 + q
            mg, Bmg = MGT[f % 2]
            ms, Bms = MST[f % 2]
            S.dma("sp", mg[:, 0, tlo:NT], X.MG_d[f][:, tlo:NT], [X.B_MG[f]], Bmg)
            S.dma("sp", mg[:, 1, tlo:NT], X.MG_d[16 + f][:, tlo:NT], [X.B_MG[16 + f]], Bmg)
            for (ti, t0, n) in tiles:
                pa_, Bpa = PB[(rr * 2) % 6]
                pb_, Bpb = PB[(rr * 2 + 1) % 6]
                ya, B_ya = YA[rr % 2]
                rr += 1
                for k in range(8):
                    MM(pa_[:, 0:n], wa[:, k, q * 128:(q + 1) * 128], yg[:, k, t0:t0 + n], [Bwa, B_yg], [Bpa], start=(k == 0), stop=(k == 7))
                for k in range(8):
                    MM(pb_[:, 0:n], wb[:, k, q * 128:(q + 1) * 128], yb[:, k, t0:t0 + n], [Bwb, B_yb], [Bpb], start=(k == 0), stop=(k == 7))
                TT(ya[:, 0:n], pa_[:, 0:n], mg[:, 0, t0:t0 + n], ALU.mult, [Bpa, Bmg], [B_ya])
                TT(ms[:, t0:t0 + n], pb_[:, 0:n], mg[:, 1, t0:t0 + n], ALU.mult, [Bpb, Bmg], [Bms])
                TT(ms[:, t0:t0 + n], ms[:, t0:t0 + n], ya[:, 0:n], ALU.add, [Bms, B_ya], [Bms])
            S.dma("sp", X.MER_d[f][:, tlo:NT], ms[:, tlo:NT], [Bms], X.B_MER)


def phase5b(X, stk):
    S, l, last = X.S, X.l, X.last
    MM, TR, ACT, SQACC, TT, TS, STT, RECIP = X.MM, X.TR, X.ACT, X.SQACC, X.TT, X.TS, X.STT, X.RECIP
    PB, cst, B_cst, mod, B_mod = X.PB, X.cst, X.B_cst, X.mod, X.B_mod
    ident = cst[:, CS["ident"]:CS["ident"] + 128]

    def sb(name, shape, dt=F32):
        return X.sb(stk, "p5b" + name, shape, dt)
    WO, B_WO = sb("wo", [128, 16, D], BF16)
    WOB = [Buf("wo_%d_%d" % (l, i)) for i in range(4)]
    MERs = [sb("mer%d" % i, [128, 16, 512], BF16) for i in range(2)]
    H, B_H = sb("H", [128, 4, D])
    OG = [X.T32[0], X.T32[1]]
    ST5, B_ST5 = sb("st", [128, 8])
    for fb in range(4):
        S.dma("pool", WO[:, :, fb * 512:(fb + 1) * 512], X.wo_d[l][:, fb * 512:(fb + 1) * 512].rearrange("(k p) c -> p k c", p=128), [], WOB[fb])
    if last:
        FG = X.t32big[:, 3 * 512:7 * 512]
        FGB = [X.T32[i][1] for i in range(3, 7)]
        B_FG = FGB[0]
        for B_ in FGB[1:]:
            S.op("dve", lambda e, a=X.T32[7][0][:, 0:1]: e.memset(a, 0.0), [], [B_, X.T32[7][1]])
        S.dma("sp", FG, X.fg_d, [X.T32[7][1]], B_FG)
    src = X.xin if l == 0 else X.H1_d
    cnt = 0
    for ti, (t0, n) in enumerate(TILES):
        if last and ti == 0:
            continue
        nb = n // 128
        si = 1 if ti == 0 else 0
        rd = [] if l == 0 else [X.B_H1[ti]]
        MER, B_MER = MERs[cnt % 2]
        cnt += 1
        S.dma("sp", MER[:, :, 0:n], X.MER_d[:, :, t0:t0 + n].rearrange("f p t -> p f t"), [X.B_MER], B_MER)
        S.dma("sp", H[:, 0:nb, :], src[t0:t0 + n, :].rearrange("(b p) f -> p b f", p=128), rd, B_H)
        def tail(f, og, Bog):
            ptt, Bptt = PB[3 + (f % 3)]
            for b in range(nb):
                TR(ptt[:, b * 128:(b + 1) * 128], og[:, b * 128:(b + 1) * 128], ident, [Bog, B_cst], [Bptt])
            TT(H[:, 0:nb, f * 128:(f + 1) * 128], H[:, 0:nb, f * 128:(f + 1) * 128],
               ptt[:, 0:nb * 128].rearrange("p (b f) -> p b f", f=128), ALU.add, [B_H, Bptt], [B_H])
        pend = None
        for f in range(16):
            po_, Bpo = PB[f % 3]
            for k in range(16):
                MM(po_[:, 0:n], WO[:, k, f * 128:(f + 1) * 128], MER[:, k, 0:n], [WOB[f // 4], B_MER], [Bpo], start=(k == 0), stop=(k == 15))
            og, Bog = OG[f % 2]
            ACT(og[:, 0:n], po_[:, 0:n], AF.Identity, [Bpo, B_mod], [Bog], scale=mod[:, 32 + f, si:si + 1])
            if pend is not None:
                tail(*pend)
            pend = (f, og, Bog)
        tail(*pend)
        if not last:
            S.dma("sp", X.H1_d[t0:t0 + n, :].rearrange("(b p) f -> p b f", p=128), H[:, 0:nb, :], [B_H], X.B_H1[ti])
        else:
            for b in range(nb):
                SQACC(H[:, b, :], MER[:, 0:4, :].rearrange("p a t -> p (a t)"), ST5[:, b:b + 1], [B_H], [B_MER, B_ST5])
            TS(ST5[:, 0:nb], ST5[:, 0:nb], 1.0 / D, 1e-6, ALU.mult, ALU.add, [B_ST5], [B_ST5])
            ACT(ST5[:, 0:nb], ST5[:, 0:nb], AF.Sqrt, [B_ST5], [B_ST5])
            RECIP(ST5[:, 0:nb], ST5[:, 0:nb], [B_ST5], [B_ST5])
            for b in range(nb):
                STT(H[:, b, :], H[:, b, :], ST5[:, b:b + 1], FG, ALU.mult, ALU.mult, [B_H, B_ST5, B_FG], [B_H])
            lt0 = t0 - NCTX
            X.finals.append(S.dma("sp", X.out_d[lt0:lt0 + n, :].rearrange("(b p) f -> p b f", p=128), H[:, 0:nb, :], [B_H], X.B_OUT[ti]))


def _pk(v):
    v = np.asarray(v, np.float32).reshape(-1, 128)
    return np.ascontiguousarray(v.T)


def _consts():
    cs = np.zeros((128, NCS), np.float32)
    cs[:, CS["ident"]:CS["ident"] + 128] = np.eye(128, dtype=np.float32)
    ob = np.zeros((128, 128), np.float32)
    ob[0:64, 0:64] = 1.0
    ob[64:128, 64:128] = 1.0
    cs[:, CS["ob"]:CS["ob"] + 128] = ob
    s = np.arange(64)[:, None]
    t = np.arange(64)[None, :]
    for d in range(2):
        strict = (s < t) if d == 0 else (s > t)
        incl = (s <= t) if d == 0 else (s >= t)
        m = np.zeros((128, 128), np.float32)
        m[0:64, 0:64] = strict
        m[64:128, 0:64] = strict
        m[0:64, 64:128] = incl
        m[64:128, 64:128] = incl
        cs[:, CS["mg%d" % d]:CS["mg%d" % d] + 128] = m
        cs[0:64, CS["mq%d" % d]:CS["mq%d" % d] + 64] = strict.T
    r = np.ones(512, np.float32)
    r[::64] = 0.0
    cs[:, CS["rst"]:CS["rst"] + 512] = r[None, :]
    for gi, w in enumerate((2, 4, 8, 16)):
        left = w // 2
        right = w - 1 - left
        for i in range(8):
            cnt_f = min(i + right, 10 ** 9) - max(i - left, 0) + 1
            cs[:, CS["pcor"] + gi * 16 + i] = 1.0 / cnt_f
            tb = -8 + i
            cnt_b = min(tb + right, -1) - (tb - left) + 1
            cs[:, CS["pcor"] + gi * 16 + 8 + i] = 1.0 / cnt_b
    return cs


def _pack_pv(inp):
    pvs = np.zeros((DEPTH, 128, NPV), np.float32)
    for l in range(DEPTH):
        def put(name, arr):
            a = _pk(arr)
            pvs[l, :, PV[name]:PV[name] + a.shape[1]] = a
        put("mu", inp["tok_mu"][l].reshape(-1))
        put("ommu", 1.0 - inp["tok_mu"][l].reshape(-1))
        put("w0", inp["decay_w0"][l].reshape(-1))
        put("a0", inp["iclr_a0"][l].reshape(-1))
        put("kk", inp["key_k"][l])
        put("ka", inp["key_a"][l].reshape(-1))
        put("omka", 1.0 - inp["key_a"][l].reshape(-1))
        put("brk", inp["bonus_rk"][l].reshape(-1))
        put("gnw", inp["gn_w"][l])
        put("gnb", inp["gn_b"][l])
        put("psc", inp["pool_scale"][l])
        put("ng", inp["norm_g"][l])
        put("adab", inp["ada_b"][l])
        if l > 0:
            put("v0", inp["vres_v0"][l - 1])
        put("fg", inp["final_g"])
    return pvs


_NC_CACHE = {}


def kernel(**inputs):
    inp = {k: np.asarray(v) for k, v in inputs.items()}
    nb = inp["x"].shape[0]
    if "nc" not in _NC_CACHE:
        _NC_CACHE["nc"] = build_program()
    nc = _NC_CACHE["nc"]
    cs = _consts()
    pvs = _pack_pv(inp)
    fgrep = np.ascontiguousarray(np.broadcast_to(inp["final_g"].astype(np.float32)[None, :], (128, D)))
    shared = {
        "ada_w": np.ascontiguousarray(inp["ada_w"], np.float32),
        "w_in": np.ascontiguousarray(inp["w_in"], np.float32),
        "dlb": np.ascontiguousarray(inp["decay_lora_b"].reshape(DEPTH, 128, 1024), np.float32),
        "ilb": np.ascontiguousarray(inp["iclr_lora_b"].reshape(DEPTH, 128, 1024), np.float32),
        "vla": np.ascontiguousarray(inp["vres_lora_a"][0], np.float32),
        "vlb": np.ascontiguousarray(inp["vres_lora_b"][0], np.float32),
        "pool_w": np.ascontiguousarray(inp["pool_w"], np.float32),
        "w_out_a": np.ascontiguousarray(inp["w_out_a"], np.float32),
        "w_out_b": np.ascontiguousarray(inp["w_out_b"], np.float32),
        "w_out": np.ascontiguousarray(inp["w_out"], np.float32),
        "pv": pvs,
        "cst": cs,
        "fgrep": fgrep,
    }
    in_maps = []
    for b in range(nb):
        m = dict(shared)
        m["xin"] = np.ascontiguousarray(np.concatenate([inp["ctx"][b], inp["x"][b]], axis=0), np.float32)
        cT = np.zeros((128, 32), np.float32)
        cT[:, 0::2] = _pk(inp["c"][b])
        cT[:, 1::2] = _pk(inp["c_ctx"])
        m["cT"] = cT
        in_maps.append(m)
    res = run_bass_kernel_spmd(nc, in_maps, core_ids=list(range(nb)))
    out = np.stack([np.asarray(r["out"], np.float32) for r in res.results], axis=0)
    return out
```

```python
import contextlib
import numpy as np
import concourse.bass as bass
import concourse.mybir as mybir
from concourse.bass_utils import run_bass_kernel_spmd

F32 = mybir.dt.float32
BF16 = mybir.dt.bfloat16
AF = mybir.ActivationFunctionType
ALU = mybir.AluOpType

import os as _os
SAME_ENGINE_KINDS = tuple(_os.environ.get('KSE', 'raw').split(','))
SEM_LIMIT = 30000

D = 2048
NT = 2304
NCTX = 256
NLAT = 2048
DEPTH = 2
INW = 10496
OFF_POOL = 3072
OFF_GA = 4096
OFF_GB = 5120
OFF_DEC = 6144
OFF_ICL = 6272
OFF_MG = 6400
C0 = float(np.exp(-0.5))
GN_EPS = 64e-5
TILES = [(0, 256), (256, 512), (768, 512), (1280, 512), (1792, 512)]
NCH = NT // 64

PV = {}
_o = 0
for _n, _w in [("mu", 24), ("w0", 16), ("a0", 16), ("kk", 8), ("ka", 16), ("brk", 16), ("gnw", 8), ("gnb", 8),
               ("psc", 8), ("ng", 16), ("adab", 48), ("v0", 8), ("fg", 16), ("omka", 16), ("ommu", 24)]:
    PV[_n] = _o
    _o += _w
NPV = _o
CS = {}
_o = 0
for _n, _w in [("ident", 128), ("ob", 128), ("mg0", 128), ("mg1", 128), ("mq0", 64), ("mq1", 64), ("rst", 512),
               ("pcor", 64)]:
    CS[_n] = _o
    _o += _w
NCS = _o


class Buf:
    __slots__ = ("name", "w", "r", "sem", "ndma")

    def __init__(self, name):
        self.name = name
        self.w = None
        self.r = []
        self.sem = None
        self.ndma = 0


class Op:
    __slots__ = ("eng", "fn", "deps", "signal", "count", "is_dma", "buf")

    def __init__(self, eng, fn, is_dma=False, buf=None):
        self.eng = eng
        self.fn = fn
        self.deps = []
        self.signal = False
        self.count = None
        self.is_dma = is_dma
        self.buf = buf


class Sched:
    ENGS = ("pe", "act", "dve", "pool", "sp")

    def __init__(self, nc):
        self.nc = nc
        self.ops = {e: [] for e in self.ENGS}
        self.dma_bufs = []
        self.bar_id = 0
        self.bar_deps = []
        self.eng_bar = {e: 0 for e in self.ENGS}
        self.dmas_since = []

    def barrier(self):
        X = []
        for e in self.ENGS:
            for o in reversed(self.ops[e]):
                if not o.is_dma:
                    X.append(o)
                    break
        lastd = {}
        for o in self.dmas_since:
            lastd[id(o.buf)] = o
        X.extend(lastd.values())
        self.dmas_since = []
        self.bar_deps = X
        self.bar_id += 1

    def _add(self, op, reads, writes):
        deps = []
        if self.eng_bar[op.eng] < self.bar_id:
            deps.extend((d, "raw") for d in self.bar_deps)
            self.eng_bar[op.eng] = self.bar_id
        for b in reads:
            if b.w is not None:
                deps.append((b.w, "raw"))
        for b in writes:
            if b.w is not None:
                deps.append((b.w, "waw"))
            deps.extend((r, "war") for r in b.r)
        seen = set()
        for d, kind in deps:
            if d is op:
                continue
            if (not d.is_dma) and (not op.is_dma) and d.eng == op.eng:
                if d.eng == "pe" or kind not in SAME_ENGINE_KINDS:
                    continue
            if id(d) in seen:
                continue
            seen.add(id(d))
            d.signal = True
            op.deps.append(d)
        for b in reads:
            b.r.append(op)
        for b in writes:
            b.w = op
            b.r = []
        self.ops[op.eng].append(op)
        return op

    def op(self, eng, fn, reads=(), writes=()):
        return self._add(Op(eng, fn), list(reads), list(writes))

    def dma(self, eng, out_ap, in_ap, reads, write):
        op = Op(eng, None, is_dma=True, buf=write)
        op.fn = lambda e, o=out_ap, i=in_ap: e.dma_start(out=o, in_=i)
        if write.sem is None:
            self.dma_bufs.append(write)
            write.sem = True
        self._add(op, list(reads), [write])
        write.ndma += 1
        op.count = 16 * write.ndma
        op.signal = True
        self.dmas_since.append(op)
        return op

    def emit(self, final_waits=()):
        nc = self.nc
        with contextlib.ExitStack() as st:
            for b in self.dma_bufs:
                b.sem = st.enter_context(nc.semaphore("d_" + b.name))
            esems = {}
            for e in self.ENGS:
                n = 0
                for o in self.ops[e]:
                    if o.is_dma:
                        continue
                    if o.signal:
                        n += 1
                        o.count = n
                nsem = max(0, n - 1) // SEM_LIMIT + 1
                esems[e] = [st.enter_context(nc.semaphore("e_%s%d" % (e, i))) for i in range(nsem)]

            def semval(o):
                if o.is_dma:
                    return o.buf.sem, o.count, ("d", id(o.buf))
                k = (o.count - 1) // SEM_LIMIT
                return esems[o.eng][k], o.count - k * SEM_LIMIT, ("e", o.eng, k)

            block = st.enter_context(nc.Block())

            def run(ename, eng):
                waited = {}
                for o in self.ops[ename]:
                    need = {}
                    for d in o.deps:
                        sem, val, key = semval(d)
                        if waited.get(key, 0) >= val:
                            continue
                        if key not in need or need[key][1] < val:
                            need[key] = (sem, val)
                    for key, (sem, val) in need.items():
                        eng.wait_ge(sem, val)
                        waited[key] = val
                    ins = o.fn(eng)
                    if o.signal:
                        if o.is_dma:
                            ins.then_inc(o.buf.sem, 16)
                        else:
                            k = (o.count - 1) // SEM_LIMIT
                            ins.then_inc(esems[ename][k], 1)
                if ename == "sp":
                    for o in final_waits:
                        sem, val, key = semval(o)
                        eng.wait_ge(sem, val)

            @block.tensor
            def _(eng):
                run("pe", eng)

            @block.scalar
            def _(eng):
                run("act", eng)

            @block.vector
            def _(eng):
                run("dve", eng)

            @block.gpsimd
            def _(eng):
                run("pool", eng)

            @block.sync
            def _(eng):
                run("sp", eng)


class Ctx:
    pass


class _Stop(Exception):
    pass


def build_program(depth=DEPTH, dbg=None):
    nc = bass.Bass("TRN2", target_bir_lowering=False)
    S = Sched(nc)
    dbg = dbg or {}
    X = Ctx()
    X.nc, X.S = nc, S
    import os
    X.stop_after = tuple(int(v) for v in os.environ['KSTOP'].split(',')) if os.environ.get('KSTOP') else None
    X.nj = int(os.environ.get('KNJ', '8'))
    X.skip = os.environ.get('KSKIP', '')
    X.p3stop = int(os.environ.get('KP3', '0'))
    X.noy = os.environ.get('KNOY', '')

    def din(name, shape, dt=F32):
        return nc.dram_tensor(name, list(shape), dt, kind="ExternalInput").ap()

    def dscr(name, shape, dt=F32):
        return nc.dram_tensor(name, list(shape), dt, kind="Internal").ap()

    X.xin = din("xin", [NT, D])
    cT_d = din("cT", [128, 32])
    X.ada_w = din("ada_w", [DEPTH, D, 3 * D])
    X.w_in = din("w_in", [DEPTH, D, INW])
    X.dlb_d = din("dlb", [DEPTH, 128, 1024])
    X.ilb_d = din("ilb", [DEPTH, 128, 1024])
    X.vla_d = din("vla", [D, 32])
    X.vlb_d = din("vlb", [32, 1024])
    X.pw_d = din("pool_w", [DEPTH, 4, 256, 256])
    X.woa_d = din("w_out_a", [DEPTH, 1024, D])
    X.wob_d = din("w_out_b", [DEPTH, 1024, D])
    X.wo_d = din("w_out", [DEPTH, D, D])
    pv_d = din("pv", [DEPTH, 128, NPV])
    cs_d = din("cst", [128, NCS])
    X.fg_d = din("fgrep", [128, D])
    X.out_d = nc.dram_tensor("out", [NLAT, D], F32, kind="ExternalOutput").ap()

    X.U32_d = dscr("U32", [32, 128, NT])
    X.SG_d = dscr("SG", [16, 128, NT], BF16)
    X.MG_d = dscr("MG", [32, 128, NT], BF16)
    X.VF_d = dscr("VF", [8, 128, NT])
    X.H1_d = dscr("H1", [NT, D])
    X.YG_d = dscr("YG", [8, 128, NT], BF16)
    X.MER_d = dscr("MER", [16, 128, NT], BF16)
    X.B_MER = Buf("MER")
    X.B_YG = Buf("YG")
    X.B_U32 = [Buf("U32")] * 32
    X.B_SG = [Buf("SG")] * 16
    X.B_MG = [Buf("MG")] * 32
    X.B_VF = [Buf("VF")] * 8
    X.B_H1 = [Buf("H1")] * 5
    X.B_OUT = [Buf("OUT")] * 5
    X.dbg_out = {}
    for k, shp in dbg.items():
        X.dbg_out[k] = (nc.dram_tensor("dbg_" + k, list(shp), F32, kind="ExternalOutput").ap(), Buf("dbg_" + k))
    X.finals = []

    def MM(out, lhsT, rhs, R, W, start=True, stop=True):
        S.op("pe", lambda e, o=out, l=lhsT, r=rhs, s=start, t=stop: e.matmul(o, lhsT=l, rhs=r, start=s, stop=t), R, W)

    def TR(out, in_, ident, R, W):
        S.op("pe", lambda e, o=out, i=in_, d=ident: e.transpose(o, i, d), R, W)

    def ACT(out, in_, func, R, W, bias=None, scale=None):
        kw = {}
        if bias is not None:
            kw["bias"] = bias
        if scale is not None:
            kw["scale"] = scale
        S.op("act", lambda e, o=out, i=in_, f=func, k=kw: e.activation(out=o, in_=i, func=f, **k), R, W)

    def SQACC(in_, junk, acc, R, W):
        S.op("act", lambda e, o=junk, i=in_, a=acc: e.activation(out=o, in_=i, func=AF.Square, accum_out=a), R, W)

    def CP(eng, out, in_, R, W):
        if eng == "act":
            S.op("act", lambda e, o=out, i=in_: e.copy(out=o, in_=i), R, W)
        else:
            S.op(eng, lambda e, o=out, i=in_: e.tensor_copy(out=o, in_=i), R, W)

    def TT(out, in0, in1, op, R, W, eng="dve"):
        S.op(eng, lambda e, o=out, a=in0, b=in1, p=op: e.tensor_tensor(out=o, in0=a, in1=b, op=p), R, W)

    def TS(out, in0, s1, s2, op0, op1, R, W, eng="dve"):
        if s2 is None:
            S.op(eng, lambda e, o=out, a=in0, x=s1, p=op0: e.tensor_scalar(out=o, in0=a, scalar1=x, scalar2=None, op0=p), R, W)
        else:
            S.op(eng, lambda e, o=out, a=in0, x=s1, y=s2, p=op0, q=op1: e.tensor_scalar(out=o, in0=a, scalar1=x, scalar2=y, op0=p, op1=q), R, W)

    def STT(out, in0, sc, in1, op0, op1, R, W, eng="dve"):
        S.op(eng, lambda e, o=out, a=in0, x=sc, b=in1, p=op0, q=op1: e.scalar_tensor_tensor(out=o, in0=a, scalar=x, in1=b, op0=p, op1=q), R, W)

    def RECIP(out, in_, R, W):
        S.op("dve", lambda e, o=out, i=in_: e.reciprocal(out=o, in_=i), R, W)

    def SCAN(out, d0, d1, R, W):
        S.op("dve", lambda e, o=out, a=d0, b=d1: e.tensor_tensor_scan(out=o, data0=a, data1=b, initial=0.0, op0=ALU.mult, op1=ALU.add), R, W)

    def MEMSET(eng, ap, val, W):
        S.op(eng, lambda e, a=ap, v=val: e.memset(a, v), [], W)

    X.MM, X.TR, X.ACT, X.SQACC, X.CP, X.TT, X.TS, X.STT, X.RECIP, X.SCAN, X.MEMSET = MM, TR, ACT, SQACC, CP, TT, TS, STT, RECIP, SCAN, MEMSET
    uid = [0]

    def sb(stack, name, shape, dt=F32):
        uid[0] += 1
        nm = "%s_%d" % (name, uid[0])
        t = stack.enter_context(nc.sbuf_tensor(nm, list(shape), dt))
        return t, Buf(nm)
    X.sb = sb

    def dbg_dump(key, ap, B):
        if key in X.dbg_out:
            o, Bo = X.dbg_out[key]
            X.finals.append(S.dma("sp", o, ap, [B], Bo))
    X.dbg_dump = dbg_dump

    with contextlib.ExitStack() as st:
        X.PB = []
        for i in range(6):
            t = st.enter_context(nc.psum_tensor("pb%d" % i, [128, 512], F32))
            X.PB.append((t, Buf("pb%d" % i)))
        X.PTRt = st.enter_context(nc.psum_tensor("ptr", [128, 1024], BF16))
        X.PTRt2 = st.enter_context(nc.psum_tensor("ptr2", [128, 1024], BF16))
        X.B_PTR = Buf("ptr")
        PB = X.PB

        X.cst, X.B_cst = sb(st, "cst", [128, NCS])
        X.pv, X.B_pv = sb(st, "pv", [128, DEPTH, NPV])
        X.identb, X.B_identb = sb(st, "identb", [128, 128], BF16)
        X.obb, X.B_obb = sb(st, "obb", [128, 128], BF16)
        X.ob64, X.B_ob64 = sb(st, "ob64", [128, 128])
        cT, B_cT = sb(st, "cTs", [128, 32])
        sc, B_sc = sb(st, "sc", [128, 16, 2], BF16)
        MODS = [sb(st, "mod%d" % i, [128, 48, 2]) for i in range(DEPTH)]
        GSCS = [sb(st, "gsc%d" % i, [128, 16, 2]) for i in range(DEPTH)]
        X.mod, X.B_mod = MODS[0]
        gsc, B_gsc = GSCS[0]
        X.dlrT, X.B_dlrT = sb(st, "dlrT", [128, NT], BF16)
        X.alrT, X.B_alrT = sb(st, "alrT", [128, NT], BF16)
        X.vlT, X.B_vlT = sb(st, "vlT", [32, NT], BF16)
        t32big, _ = sb(st, "t32big", [128, 12 * 512])
        X.t32big = t32big
        X.T32 = [(t32big[:, i * 512:(i + 1) * 512], Buf("t32_%d" % i)) for i in range(12)]
        cst, B_cst, pv, B_pv = X.cst, X.B_cst, X.pv, X.B_pv
        S.dma("sp", cst[:], cs_d, [], B_cst)
        for l in range(DEPTH):
            S.dma("sp", pv[:, l, :], pv_d[l], [], B_pv)
        S.dma("sp", cT[:], cT_d, [], B_cT)
        CP("act", X.identb[:], cst[:, CS["ident"]:CS["ident"] + 128], [B_cst], [X.B_identb])
        CP("act", X.obb[:], cst[:, CS["ob"]:CS["ob"] + 128], [B_cst], [X.B_obb])
        S.op("act", lambda e: e.mul(out=X.ob64[:], in_=cst[:, CS["ob"]:CS["ob"] + 128], mul=1.0 / 64.0), [B_cst], [X.B_ob64])
        ident = cst[:, CS["ident"]:CS["ident"] + 128]
        ACT(sc[:].rearrange("p k t -> p (k t)"), cT[:], AF.Silu, [B_cT], [B_sc])

        def pcol(l, name, j):
            o = PV[name] + j
            return pv[:, l, o:o + 1]
        X.pcol = pcol

        ws_rr = [0]

        def make_wslots(stack):
            X.WS = [sb(stack, "ws%d" % i, [128, 16, 512], BF16) for i in range(2)]

        def load_w(dram_ap_3d, nk, ncols):
            i = ws_rr[0] % 2
            ws_rr[0] += 1
            t, B = X.WS[i]
            S.dma("pool", t[:, 0:nk, 0:ncols], dram_ap_3d, [], B)
            return t, B
        X.load_w = load_w
        pa_rr = [0]

        def pacc():
            i = pa_rr[0] % 2
            pa_rr[0] += 1
            return PB[i]

        for l in range(depth):
            X.l = l
            X.last = last = (l == DEPTH - 1)
            with contextlib.ExitStack() as sA:
                make_wslots(sA)
                xnT, _bx = sb(sA, "xnT", [128, 16, NT], BF16)
                B_xnTs = [Buf("xnT_%d_%d" % (l, i)) for i in range(16)]
                B_xnT = B_xnTs
                stg = [sb(sA, "stg%d" % i, [128, NT]) for i in range(2)]
                stgb = [sb(sA, "stgb%d" % i, [128, NT], BF16) for i in range(2)]
                htile = [sb(sA, "ht%d" % i, [128, D]) for i in range(2)]
                sqj, B_sqj = stg[0][0][:, 0:D], stg[0][1]
                stat, B_stat = sb(sA, "stat", [128, 8])
                B_stats = [Buf("stat%d_%d" % (l, i)) for i in range(4)]
                htile = htile + [(stg[1][0][:, 0:D], stg[1][1])]
                X.mod, X.B_mod = MODS[l]
                mod, B_mod = MODS[l]
                gsc, B_gsc = GSCS[l]

                def phase0_gen(ll, slots, halfk):
                    pm, B_pm = PB[2]
                    mod_, B_mod_ = MODS[ll]
                    gsc_, B_gsc_ = GSCS[ll]
                    nh = 2 if halfk else 1
                    kper = 16 // nh
                    cnt = 0
                    for blk in range(12):
                        for hf in range(nh):
                            wt, Bw = slots[cnt % len(slots)]
                            cnt += 1
                            src_ = X.ada_w[ll][hf * kper * 128:(hf + 1) * kper * 128, blk * 512:(blk + 1) * 512]
                            S.dma("pool", wt[:, 0:kper, :], src_.rearrange("(k p) c -> p k c", p=128), [], Bw)
                            for ft in range(4):
                                fc = blk * 4 + ft
                                for kc in range(kper):
                                    MM(pm[:, hf * 96 + fc * 2:hf * 96 + fc * 2 + 2], wt[:, kc, ft * 128:(ft + 1) * 128], sc[:, hf * kper + kc, :],
                                       [Bw, B_sc], [B_pm], start=(kc == 0), stop=(kc == kper - 1))
                                yield
                    adab = pv[:, ll, PV["adab"]:PV["adab"] + 48]
                    TT(mod_[:], pm[:, 0:96].rearrange("p (f t) -> p f t", t=2), adab.unsqueeze(2).to_broadcast([128, 48, 2]), ALU.add,
                       [B_pm, B_pv], [B_mod_])
                    if halfk:
                        TT(mod_[:], mod_[:], pm[:, 96:192].rearrange("p (f t) -> p f t", t=2), ALU.add, [B_pm, B_mod_], [B_mod_])
                    ngv = pv[:, ll, PV["ng"]:PV["ng"] + 16]
                    STT(gsc_[:], mod_[:, 16:32, :], 1.0, ngv.unsqueeze(2).to_broadcast([128, 16, 2]), ALU.add, ALU.mult, [B_mod_, B_pv], [B_gsc_])
                    yield
                bg = None
                if l == 0:
                    for _ in phase0_gen(0, X.WS, False):
                        pass
                    if depth > 1:
                        aws = [sb(sA, "aws%d" % i, [128, 8, 512], BF16) for i in range(1)]
                        bg = phase0_gen(1, aws, True)
                if X.stop_after == (l, 0):
                    break
                src = X.xin if l == 0 else X.H1_d
                p1_rr = [0]
                for blk in range(NT // 128):
                    t0 = blk * 128
                    si = 1 if t0 < NCTX else 0
                    ht, Bh = htile[blk % 3]
                    rd = [] if l == 0 else [X.B_H1[0 if t0 < 256 else 1 + (t0 - 256) // 512]]
                    S.dma("sp", ht[:], src[t0:t0 + 128, :], rd, Bh)
                    c0 = (blk % 4) * 2
                    B_st = B_stats[blk % 4]
                    SQACC(ht[:], sqj[:], stat[:, c0:c0 + 1], [Bh], [B_sqj, B_st])
                    TS(stat[:, c0 + 1:c0 + 2], stat[:, c0:c0 + 1], 1.0 / D, 1e-6, ALU.mult, ALU.add, [B_st], [B_st])
                    ACT(stat[:, c0 + 1:c0 + 2], stat[:, c0 + 1:c0 + 2], AF.Sqrt, [B_st], [B_st])
                    RECIP(stat[:, c0 + 1:c0 + 2], stat[:, c0 + 1:c0 + 2], [B_st], [B_st])
                    TS(ht[:], ht[:], stat[:, c0 + 1:c0 + 2], None, ALU.mult, None, [Bh, B_st], [Bh])
                    for g4 in range(4):
                        banks = [PB[p1_rr[0] % 6], PB[(p1_rr[0] + 1) % 6]]
                        p1_rr[0] += 2
                        for q in range(4):
                            fc = g4 * 4 + q
                            pt, Bp = banks[q % 2]
                            TR(pt[:, (q // 2) * 128:(q // 2 + 1) * 128], ht[:, fc * 128:(fc + 1) * 128], ident, [Bh, B_cst], [Bp])
                        for q in range(4):
                            fc = g4 * 4 + q
                            pt, Bp = banks[q % 2]
                            psl = pt[:, (q // 2) * 128:(q // 2 + 1) * 128]
                            if q % 2 == 0:
                                ACT(xnT[:, fc, t0:t0 + 128], psl, AF.Identity, [Bp, B_gsc, B_mod], [B_xnTs[fc]],
                                    bias=mod[:, fc, si:si + 1], scale=gsc[:, fc, si:si + 1])
                            else:
                                TS(xnT[:, fc, t0:t0 + 128], psl, gsc[:, fc, si:si + 1], mod[:, fc, si:si + 1],
                                   ALU.mult, ALU.add, [Bp, B_gsc, B_mod], [B_xnTs[fc]])
                if l == 0:
                    dbg_dump("xn0", xnT[:, 0, :], B_xnT) if "xn0" in X.dbg_out and False else None

                if X.stop_after == (l, 1):
                    break
                def project(col0, ncols, handler):
                    wt, Bw = load_w(X.w_in[l][:, col0:col0 + ncols].rearrange("(k p) c -> p k c", p=128), 16, ncols)
                    for ft in range(ncols // 128):
                        ftile_ = col0 // 128 + ft
                        ctx_needed = (not last) or (8 <= ftile_ < 24) or (ftile_ in (OFF_DEC // 128, OFF_ICL // 128))
                        for (t0, n) in TILES:
                            if t0 == 0 and not ctx_needed:
                                continue
                            pt, Bp = pacc()
                            for kc in range(16):
                                MM(pt[:, 0:n], wt[:, kc, ft * 128:(ft + 1) * 128], xnT[:, kc, t0:t0 + n], [Bw, B_xnTs[kc]], [Bp],
                                   start=(kc == 0), stop=(kc == 15))
                            handler(ftile_, t0, n, pt, Bp)
                        if bg is not None:
                            next(bg, None)

                ev_rr = [0]

                def h_f32(ftile, t0, n, pt, Bp):
                    s_, Bs = stg[ftile % 2]
                    eng = "act" if ev_rr[0] % 2 else "dve"
                    ev_rr[0] += 1
                    CP(eng, s_[:, t0:t0 + n], pt[:, 0:n], [Bp], [Bs])
                    if t0 + n == NT:
                        S.dma("sp", X.U32_d[ftile], s_[:], [Bs], X.B_U32[ftile])

                def h_silu(ftile, t0, n, pt, Bp):
                    idx = ftile - OFF_GA // 128
                    s_, Bs = stgb[idx % 2]
                    ACT(s_[:, t0:t0 + n], pt[:, 0:n], AF.Silu, [Bp], [Bs])
                    if t0 + n == NT:
                        S.dma("sp", X.SG_d[idx], s_[:], [Bs], X.B_SG[idx])

                def h_lora(ftile, t0, n, pt, Bp):
                    if ftile > OFF_DEC // 128 + 1:
                        return
                    if ftile == OFF_DEC // 128:
                        ACT(X.dlrT[:, t0:t0 + n], pt[:, 0:n], AF.Tanh if 'T' not in X.skip else AF.Sigmoid, [Bp], [X.B_dlrT])
                    else:
                        CP("dve", X.alrT[:, t0:t0 + n], pt[:, 0:n], [Bp], [X.B_alrT])

                def h_sig(ftile, t0, n, pt, Bp):
                    idx = ftile - OFF_MG // 128
                    s_, Bs = stgb[idx % 2]
                    ACT(s_[:, t0:t0 + n], pt[:, 0:n], AF.Sigmoid, [Bp], [Bs])
                    if t0 + n == NT:
                        S.dma("sp", X.MG_d[idx], s_[:], [Bs], X.B_MG[idx])

                for blk in range(8 if 'a' not in X.skip else 1):
                    project(blk * 512, 512, h_f32)
                for blk in range(4 if 'b' not in X.skip else 0):
                    project(OFF_GA + blk * 512, 512, h_silu)
                if 'c' not in X.skip:
                    project(OFF_DEC, 512, h_lora)
                for blk in range(8 if 'd' not in X.skip else 0):
                    project(OFF_MG + blk * 512, 512, h_sig)
                if l > 0:
                    i = ws_rr[0] % 2
                    ws_rr[0] += 1
                    wt, Bw = X.WS[i]
                    vst, B_vst = stg[0]
                    S.dma("sp", vst[:, 0:512].rearrange("p (k c) -> p k c", c=32), X.vla_d.rearrange("(k p) c -> p k c", p=128), [], B_vst)
                    CP("act", wt[:, :, 0:32], vst[:, 0:512].rearrange("p (k c) -> p k c", c=32), [B_vst], [Bw])
                    for (t0, n) in TILES:
                        pt, Bp = pacc()
                        for kc in range(16):
                            MM(pt[0:32, 0:n], wt[:, kc, 0:32], xnT[:, kc, t0:t0 + n], [Bw, B_xnTs[kc]], [Bp], start=(kc == 0), stop=(kc == 15))
                        CP("dve", X.vlT[:, t0:t0 + n], pt[0:32, 0:n], [Bp], [X.B_vlT])
                if bg is not None:
                    for _ in bg:
                        pass
                if X.stop_after == (l, 2):
                    break
            S.barrier()
            with contextlib.ExitStack() as sB:
                stopped = False
                with contextlib.ExitStack() as s3:
                    X.ygst = [sb(s3, "ygst%d" % i, [128, 512], BF16) for i in range(2)]
                    try:
                        phase3(X, s3)
                    except _Stop:
                        stopped = True
                S.barrier()
                if stopped or X.stop_after == (l, 3):
                    break
                X.yg, X.B_yg = sb(sB, "yg", [128, 8, NT], BF16)
                for j_ in range(8):
                    S.dma("sp", X.yg[:, j_, :], X.YG_d[j_], [X.B_YG], X.B_yg)
                X.yb, X.B_yb = sb(sB, "yb", [128, 8, NT], BF16)
                with contextlib.ExitStack() as s4:
                    phase4(X, s4)
                S.barrier()
                with contextlib.ExitStack() as s5:
                    make_wslots(s5)
                    phase5a(X, s5)
            S.barrier()
            with contextlib.ExitStack() as s5b:
                phase5b(X, s5b)
            S.barrier()

        S.emit(final_waits=X.finals)
    return nc


def phase3(X, stk):
    S, l, last = X.S, X.l, X.last
    MM, TR, ACT, CP, TT, TS, STT, RECIP, SCAN, MEMSET = X.MM, X.TR, X.ACT, X.CP, X.TT, X.TS, X.STT, X.RECIP, X.SCAN, X.MEMSET
    PB, PTRt, cst, B_cst, B_pv, pcol = X.PB, X.PTRt, X.cst, X.B_cst, X.B_pv, X.pcol
    identb, B_identb, obb, B_obb, ob64, B_ob64 = X.identb, X.B_identb, X.obb, X.B_obb, X.ob64, X.B_ob64
    dlrT, B_dlrT, alrT, B_alrT, vlT, B_vlT = X.dlrT, X.B_dlrT, X.alrT, X.B_alrT, X.vlT, X.B_vlT
    T32 = X.T32

    def sb(name, shape, dt=F32):
        return X.sb(stk, "p3" + name, shape, dt)

    XS = [sb("xs%d" % i, [128, NT]) for i in range(3)]
    KK, B_KK = sb("kk", [128, NT])
    SGa, B_SGa = sb("sga", [128, NT], BF16)
    LW, B_lw = sb("lw", [128, 2, 1024], BF16)
    VLB, B_vlb = sb("vlb", [32, 1024], BF16)
    VP, B_VP = sb("vp", [128, 8, 2, 64], BF16)
    RKB, B_RKB = sb("rkb", [128, 512], BF16)
    RKBs = None
    SQ, B_SQ = sb("sq", [128, 512])
    t32b, _ = sb("t32b", [128, 8 * 512])
    temps = [T32[0:8], [(t32b[:, i * 512:(i + 1) * 512], Buf("t32b_%d" % i)) for i in range(8)]]
    RKBs = [(RKB, B_RKB), (temps[1][0][0].bitcast(BF16)[:, 0:512], temps[1][0][1]), (temps[1][1][0].bitcast(BF16)[:, 0:512], temps[1][1][1])]
    PTRh = [(PTRt[:, 0:512], Buf("ptr0")), (X.PTRt2[:, 0:512], Buf("ptr1"))]
    PSH, B_PSH = PB[0]

    class St:
        pass
    STR = []
    for d in range(2):
        Z = St()
        Z.d = d
        Z.KB = sb("kb%d" % d, [128, NT], BF16)
        Z.Y = sb("y%d" % d, [128, NT])
        Z.UV = sb("uv%d" % d, [128, NCH, 2, 64], BF16)
        Z.AR = sb("ar%d" % d, [128, 8, 2, 64], BF16)
        Z.BK = sb("bk%d" % d, [128, 8, 2, 64], BF16)
        Z.BKp = sb("bkp%d" % d, [128, 8, 2, 64], BF16)
        Z.BKpT = sb("bkpt%d" % d, [128, 8, 128], BF16)
        Z.AqT = sb("aqt%d" % d, [64, 8, 128], BF16)
        Z.GL = sb("gl%d" % d, [128, 8, 64], BF16)
        Z.GR = sb("gr%d" % d, [128, 16, 64], BF16)
        Z.Qs = [sb("q%d_%d" % (d, i), [64, 8, 64], BF16) for i in range(2)]
        Z.Ps = [sb("p%d_%d" % (d, i), [64, 8, 64], BF16) for i in range(2)]
        Z.Ts = [sb("tt%d_%d" % (d, i), [64, 8, 64], BF16) for i in range(2)]
        Z.XL = sb("xl%d" % d, [64, 8, 64], BF16)
        Z.UL = sb("ul%d" % d, [64, 16, 64], BF16)
        Z.AqP = sb("aqp%d" % d, [128, 8, 64], BF16)
        Z.PC = sb("pc%d" % d, [128, 8])
        Z.ST32 = sb("st32_%d" % d, [128, 64])
        Z.STb = sb("stb%d" % d, [128, 64], BF16)
        Z.T = temps[d]
        Z.s0, Z.s1, Z.s2 = PB[3 * d], PB[3 * d + 1], PB[3 * d + 2]
        Z.PTR = PTRh[d]
        STR.append(Z)
    X32, B_X32 = STR[1].Y
    MEMSET("dve", VP[:], 0.0, [B_VP])
    for which, src_d in ((0, X.dlb_d), (1, X.ilb_d)):
        f_ = X.t32big[:, which * 1024:(which + 1) * 1024]
        Bs_ = [T32[which * 2][1], T32[which * 2 + 1][1]]
        S.op("dve", lambda e, a=T32[which * 2 + 1][0][:, 0:1]: e.memset(a, 0.0), [], [Bs_[1]])
        S.dma("sp", f_, src_d[l], [Bs_[1]], Bs_[0])
        CP("act", LW[:, which, :], f_, Bs_, [B_lw])
    if l > 0:
        f_ = X.t32big[0:32, 4 * 512:6 * 512]
        Bs_ = [T32[4][1], T32[5][1]]
        S.op("dve", lambda e, a=T32[5][0][:, 0:1]: e.memset(a, 0.0), [], [Bs_[1]])
        S.dma("sp", f_, X.vlb_d, [Bs_[1]], Bs_[0])
        CP("act", VLB[:], f_, Bs_, [B_vlb])

    mq = [cst[0:64, CS["mq0"]:CS["mq0"] + 64], cst[0:64, CS["mq1"]:CS["mq1"] + 64]]
    mgm = [cst[:, CS["mg0"]:CS["mg0"] + 128], cst[:, CS["mg1"]:CS["mg1"] + 128]]
    rst = cst[:, CS["rst"]:CS["rst"] + 512]
    id64 = cst[0:64, CS["ident"]:CS["ident"] + 64]

    def shift_mix(j, src, dst, Bs, Bd, mu_col, ommu_col):
        def mix(dsl_d, sl_from, sl_self, bnd_d, bnd_s):
            TT(dsl_d, sl_from, sl_self, ALU.subtract, [Bs], [Bd])
            STT(dsl_d, dsl_d, mu_col, sl_self, ALU.mult, ALU.add, [Bs, Bd, B_pv], [Bd])
            TS(bnd_d, bnd_s, ommu_col, None, ALU.mult, None, [Bs, B_pv], [Bd])
        if j < 4:
            mix(dst[:, 1:NCTX], src[:, 0:NCTX - 1], src[:, 1:NCTX], dst[:, 0:1], src[:, 0:1])
        else:
            mix(dst[:, 0:NCTX - 1], src[:, 1:NCTX], src[:, 0:NCTX - 1], dst[:, NCTX - 1:NCTX], src[:, NCTX - 1:NCTX])
        s3 = src[:, NCTX:NT].rearrange("p (r c) -> p r c", c=64)
        d3 = dst[:, NCTX:NT].rearrange("p (r c) -> p r c", c=64)
        q = j // 2
        if q == 0:
            mix(d3[:, :, 1:64], s3[:, :, 0:63], s3[:, :, 1:64], d3[:, :, 0:1], s3[:, :, 0:1])
        elif q == 1:
            mix(d3[:, :, 0:63], s3[:, :, 1:64], s3[:, :, 0:63], d3[:, :, 63:64], s3[:, :, 63:64])
        elif q == 2:
            mix(d3[:, 1:32, :], s3[:, 0:31, :], s3[:, 1:32, :], d3[:, 0:1, :], s3[:, 0:1, :])
        else:
            mix(d3[:, 0:31, :], s3[:, 1:32, :], s3[:, 0:31, :], d3[:, 31:32, :], s3[:, 31:32, :])

    def c3(ap2):
        return ap2.rearrange("p (c t) -> p c t", t=64)

    def rr(gens):
        gens = list(gens)
        while gens:
            for g in list(gens):
                try:
                    next(g)
                except StopIteration:
                    gens.remove(g)

    (RS, B_RS), (KS, B_KS), (VS, B_VS) = XS
    X32b, B_X32b = STR[0].Y

    def sweep(j, Z):
        d = Z.d
        (KB, B_KB), (Yd, B_Yd), (UV, B_UV) = Z.KB, Z.Y, Z.UV
        (AR, B_AR), (BK, B_BK), (BKp, B_BKp) = Z.AR, Z.BK, Z.BKp
        (BKpT, B_BKpT), (AqT, B_AqT), (GL, B_GL), (GR, B_GR) = Z.BKpT, Z.AqT, Z.GL, Z.GR
        Qs, Ps, Ts = Z.Qs, Z.Ps, Z.Ts
        (XL, B_XL), (UL, B_UL), (AqP, B_AqP), (PC, B_PC) = Z.XL, Z.UL, Z.AqP, Z.PC
        (ST32, B_ST32), (STb, B_STb) = Z.ST32, Z.STb
        (P0, B_P0), (P1, B_P1), (P2, B_P2) = Z.s0, Z.s1, Z.s2
        PTRd, B_PTRd = Z.PTR
        lw = LW[:, :, j * 128:(j + 1) * 128]
        order = [0, 1, 2, 3, 4] if d == 0 else [0, 4, 3, 2, 1]
        MEMSET("dve", ST32[:], 0.0, [B_ST32])
        MEMSET("dve", STb[:], 0.0, [B_STb])
        dsl = slice(d * 64, (d + 1) * 64)
        for ti in order:
            t0, n = TILES[ti]
            nch = n // 64
            c0 = t0 // 64
            (A32, B_A32), (SIG, B_SIG), (LS, B_LS), (EA, B_EA), (EB, B_EB), (KD, B_KD), (KA, B_KA), (TMP, B_TMP) = Z.T
            MM(P0[:, 0:n], lw[dsl, 1, :], alrT[dsl, t0:t0 + n], [B_lw, B_alrT], [B_P0])
            ACT(A32[:, 0:n], P0[:, 0:n], AF.Sigmoid, [B_P0, B_pv], [B_A32], bias=pcol(l, "a0", d * 8 + j))
            MM(P0[:, 0:n], lw[dsl, 0, :], dlrT[dsl, t0:t0 + n], [B_lw, B_dlrT], [B_P0])
            ACT(SIG[:, 0:n], P0[:, 0:n], AF.Sigmoid, [B_P0, B_pv], [B_SIG], bias=pcol(l, "w0", d * 8 + j))
            if 'a' not in X.noy:
                yield
            SCAN(LS[:, 0:n], rst[:, 0:n], SIG[:, 0:n], [B_cst, B_SIG], [B_LS])
            L3 = c3(LS[:, 0:n])
            S3 = c3(SIG[:, 0:n])
            T3 = c3(TMP[:, 0:n])
            if d == 1:
                TT(T3, L3[:, :, 63:64].to_broadcast([128, nch, 64]), L3, ALU.subtract, [B_LS], [B_TMP])
                TT(L3, T3, S3, ALU.add, [B_TMP, B_SIG], [B_LS])
                endc = 0
            else:
                endc = 63
            if 'a' not in X.noy:
                yield
            ACT(KD[:, 0:n], A32[:, 0:n], AF.Identity, [B_A32, B_pv], [B_KD], scale=pcol(l, "ka", d * 8 + j), bias=pcol(l, "omka", d * 8 + j))
            TT(KD[:, 0:n], KD[:, 0:n], KS[:, t0:t0 + n], ALU.mult, [B_KD, B_KS], [B_KD])
            TT(KA[:, 0:n], KK[:, t0:t0 + n], A32[:, 0:n], ALU.mult, [B_KK, B_A32], [B_KA])
            ACT(KB[:, t0:t0 + n], KD[:, 0:n], AF.Identity, [B_KD, B_pv], [B_KB], scale=pcol(l, "brk", d * 8 + j))
            if 'a' not in X.noy:
                yield
            TT(TMP[:, 0:n], LS[:, 0:n], SIG[:, 0:n], ALU.subtract, [B_LS, B_SIG], [B_TMP])
            ACT(EA[:, 0:n], TMP[:, 0:n], AF.Exp, [B_TMP], [B_EA], scale=-C0)
            ACT(EB[:, 0:n], LS[:, 0:n], AF.Exp, [B_LS], [B_EB], scale=-C0)
            STT(AR[:, 0:nch, 0, :], c3(KK[:, t0:t0 + n]), -1.0, c3(EA[:, 0:n]), ALU.mult, ALU.mult, [B_KK, B_EA], [B_AR])
            if 'a' not in X.noy:
                yield
            TT(AR[:, 0:nch, 1, :], c3(RS[:, t0:t0 + n]), c3(EB[:, 0:n]), ALU.mult, [B_RS, B_EB], [B_AR])
            CP("act", PC[:, 0:nch], c3(EB[:, 0:n])[:, :, endc:endc + 1].rearrange("p c o -> p (c o)"), [B_EB], [B_PC])
            ACT(EA[:, 0:n], LS[:, 0:n], AF.Exp, [B_LS], [B_EA], scale=C0)
            if 'a' not in X.noy:
                yield
            TT(BK[:, 0:nch, 0, :], c3(KA[:, 0:n]), c3(EA[:, 0:n]), ALU.mult, [B_KA, B_EA], [B_BK])
            TT(BK[:, 0:nch, 1, :], c3(KD[:, 0:n]), c3(EA[:, 0:n]), ALU.mult, [B_KD, B_EA], [B_BK])
            if 'a' not in X.noy:
                yield
            TT(BKp[:, 0:nch, :, :].rearrange("p c a t -> p c (a t)"), BK[:, 0:nch, :, :].rearrange("p c a t -> p c (a t)"),
               PC[:, 0:nch].unsqueeze(2).to_broadcast([128, nch, 128]), ALU.mult, [B_BK, B_PC], [B_BKp])
            if 'a' not in X.noy:
                yield
            for r0 in range(0, nch, 4):
                for c in range(r0, r0 + 4):
                    TR(PTRd[:, (c - r0) * 128:(c - r0 + 1) * 128], BKp[:, c, :, :].rearrange("p a t -> p (a t)"), identb[:],
                       [B_BKp, B_identb], [B_PTRd])
                CP("act", BKpT[:, r0:r0 + 4, :], PTRd[:, 0:512].rearrange("p (c f) -> p c f", f=128), [B_PTRd], [B_BKpT])
                if 'b' not in X.noy:
                    yield
                for c in range(r0, r0 + 4):
                    TR(PTRd[0:64, (c - r0) * 128:(c - r0 + 1) * 128], AR[:, c, 0, :], identb[:], [B_AR, B_identb], [B_PTRd])
                CP("act", AqT[:, r0:r0 + 4, :], PTRd[0:64, 0:512].rearrange("p (c f) -> p c f", f=128), [B_PTRd], [B_AqT])
                if 'b' not in X.noy:
                    yield
            for g0 in range(0, nch, 4):
                units = [(g0 + cl, h) for h in range(2) for cl in range(4)]
                for u, (c, h) in enumerate(units):
                    hs = slice(h * 64, (h + 1) * 64)
                    PGt, B_PGt = (P1, B_P1) if h == 0 else (P2, B_P2)
                    uo = (u % 4) * 128
                    MM(PGt[:, uo:uo + 128], BK[hs, c, :, :].rearrange("p a t -> p (a t)"), AR[hs, c, :, :].rearrange("p a t -> p (a t)"),
                       [B_BK, B_AR], [B_PGt])
                if 'c' not in X.noy:
                    yield
                for half, (PGt, B_PGt) in enumerate(((P1, B_P1), (P2, B_P2))):
                    p4 = PGt[:, :].rearrange("p (u f) -> p u f", f=128)
                    mb = mgm[d].unsqueeze(1).to_broadcast([128, 4, 128])
                    TT(GL[:, half * 4:half * 4 + 4, :], p4[:, :, 0:64], mb[:, :, 0:64], ALU.mult, [B_PGt, B_cst], [B_GL])
                    TT(GR[:, g0 * 2 + half * 4:g0 * 2 + half * 4 + 4, :], p4[:, :, 64:128], mb[:, :, 64:128], ALU.mult, [B_PGt, B_cst], [B_GR])
                for u, (c, h) in enumerate(units):
                    hs = slice(h * 64, (h + 1) * 64)
                    PQh, B_PQh = (P0, B_P0) if h == 0 else (P1, B_P1)
                    MM(PQh[0:64, (u % 4) * 64:(u % 4 + 1) * 64], AR[hs, c, 0, :], BK[hs, c, 0, :], [B_AR, B_BK], [B_PQh])
                if 'c' not in X.noy:
                    yield
                Q0, B_Q0 = Qs[0]
                TT(Q0[:, 0:4, :], P0[0:64, 0:256].rearrange("p (u f) -> p u f", f=64), mq[d].unsqueeze(1).to_broadcast([64, 4, 64]), ALU.mult,
                   [B_P0, B_cst], [B_Q0])
                TT(Q0[:, 4:8, :], P1[0:64, 0:256].rearrange("p (u f) -> p u f", f=64), mq[d].unsqueeze(1).to_broadcast([64, 4, 64]), ALU.mult,
                   [B_P1, B_cst], [B_Q0])
                T0, B_T0 = Ts[0]
                TT(T0[:], GL[0:64, :, :], id64.unsqueeze(1).to_broadcast([64, 8, 64]), ALU.add, [B_GL, B_cst], [B_T0])
                if 'c' not in X.noy:
                    yield
                Pprev, B_Pprev = GL[0:64, :, :], B_GL
                Qprev, B_Qprev = Q0[:], B_Q0
                Tprev, B_Tprev = T0[:], B_T0
                for lvl in range(1, 6):
                    Qn, B_Qn = Qs[lvl % 2]
                    Pn, B_Pn = Ps[lvl % 2]
                    Tn, B_Tn = Ts[lvl % 2]
                    for u in range(8):
                        MM(P0[0:64, u * 64:(u + 1) * 64], Pprev[:, u, :], Qprev[:, u, :], [B_Pprev, B_Qprev], [B_P0])
                    if lvl < 5:
                        for u in range(8):
                            MM(P1[0:64, u * 64:(u + 1) * 64], Qprev[:, u, :], Pprev[:, u, :], [B_Pprev, B_Qprev], [B_P1])
                    if 'c' not in X.noy:
                        yield
                    CP("act", Qn[:], P0[0:64, :].rearrange("p (u f) -> p u f", f=64), [B_P0], [B_Qn])
                    if lvl < 5:
                        CP("act", Pn[:], P1[0:64, :].rearrange("p (u f) -> p u f", f=64), [B_P1], [B_Pn])
                    if 'c' not in X.noy:
                        yield
                    for u in range(8):
                        MM(P2[0:64, u * 64:(u + 1) * 64], Qn[:, u, :], Tprev[:, u, :], [B_Qn, B_Tprev], [B_P2])
                    TT(Tn[:], P2[0:64, :].rearrange("p (u f) -> p u f", f=64), Tprev, ALU.add, [B_P2, B_Tprev], [B_Tn])
                    if 'c' not in X.noy:
                        yield
                    if lvl < 5:
                        Pprev, B_Pprev = Pn[:], B_Pn
                    Qprev, B_Qprev = Qn[:], B_Qn
                    Tprev, B_Tprev = Tn[:], B_Tn
                Tf, B_Tf = Tprev, B_Tprev
                for u, (c, h) in enumerate(units):
                    MM(P1[h * 64:(h + 1) * 64, (u % 4) * 64:(u % 4) * 64 + 64], AqT[:, c, h * 64:(h + 1) * 64], Tf[:, u, :],
                       [B_AqT, B_Tf], [B_P1])
                for u, (c, h) in enumerate(units):
                    MM(P0[0:64, u * 64:(u + 1) * 64], GL[64:128, u, :], UV[64:128, c0 + c, h, :], [B_GL, B_UV], [B_P0])
                if 'c' not in X.noy:
                    yield
                CP("act", AqP[:, g0:g0 + 4, :], P1[:, 0:256].rearrange("p (c f) -> p c f", f=64), [B_P1], [B_AqP])
                CP("act", XL[:], P0[0:64, :].rearrange("p (u f) -> p u f", f=64), [B_P0], [B_XL])
                if 'c' not in X.noy:
                    yield
                for u in range(8):
                    MM(P2[0:64, u * 64:(u + 1) * 64], Tf[:, u, :], XL[:, u, :], [B_Tf, B_XL], [B_P2])
                CP("act", UL[:, g0 * 2:g0 * 2 + 8, :], P2[0:64, :].rearrange("p (u f) -> p u f", f=64), [B_P2], [B_UL])
                if 'c' not in X.noy:
                    yield
            corder = list(range(nch)) if d == 0 else list(range(nch - 1, -1, -1))
            for c in corder:
                cg = c0 + c

                def gi_(h_):
                    return (c // 4) * 8 + h_ * 4 + (c % 4)
                PYs = ((P0, B_P0), (P2, B_P2))
                for h in range(2):
                    hs = slice(h * 64, (h + 1) * 64)
                    MM(PYs[h][0][hs, c * 64:(c + 1) * 64], STb[hs, :], AR[hs, c, 1, :], [B_STb, B_AR], [PYs[h][1]], start=True, stop=False)
                hs0, hs1 = slice(0, 64), slice(64, 128)
                MM(P1[0:64, 0:64], AqP[hs0, c, :], STb[hs0, :], [B_AqP, B_STb], [B_P1])
                MM(P2[0:64, 0:64], AqP[hs1, c, :], STb[hs1, :], [B_AqP, B_STb], [B_P2])
                if 'e' not in X.noy:
                    yield
                TT(UV[0:64, cg, 0, :], P1[0:64, 0:64], UL[:, gi_(0), :], ALU.add, [B_P1, B_UL], [B_UV])
                TT(UV[0:64, cg, 1, :], P2[0:64, 0:64], UL[:, gi_(1), :], ALU.add, [B_P2, B_UL], [B_UV])
                if 'e' not in X.noy:
                    yield
                for h in range(2):
                    hs = slice(h * 64, (h + 1) * 64)
                    MM(P1[hs, 64:128], BKpT[:, c, hs], UV[:, cg, h, :], [B_BKpT, B_UV], [B_P1])
                for h in range(2):
                    hs = slice(h * 64, (h + 1) * 64)
                    MM(PYs[h][0][hs, c * 64:(c + 1) * 64], UV[:, cg, h, :], GR[:, gi_(h), :], [B_UV, B_GR], [PYs[h][1]], start=False, stop=True)
                if 'e' not in X.noy:
                    yield
                STT(STb[:], ST32[:], PC[:, c:c + 1], P1[:, 64:128], ALU.mult, ALU.add, [B_ST32, B_PC, B_P1], [B_STb])
                STT(ST32[:], ST32[:], PC[:, c:c + 1], P1[:, 64:128], ALU.mult, ALU.add, [B_ST32, B_PC, B_P1], [B_ST32])
                if 'e' not in X.noy:
                    yield
            CP("act", Yd[0:64, t0:t0 + n], P0[0:64, 0:n], [B_P0], [B_Yd])
            CP("act", Yd[64:128, t0:t0 + n], P2[64:128, 0:n], [B_P2], [B_Yd])
            if 'a' not in X.noy:
                yield

    for j in range(X.nj):
        S.dma("sp", SGa[:], X.SG_d[j], [X.B_SG[j]], B_SGa)
        vlb = VLB[:, j * 128:(j + 1) * 128]
        lbufs = [(X32, B_X32), (X32b, B_X32b), (X32, B_X32)]

        def ld(m):
            S.dma("sp", lbufs[m][0][:], X.U32_d[m * 8 + j], [X.B_U32[m * 8 + j]], lbufs[m][1])

        def sh(m):
            shift_mix(j, lbufs[m][0], XS[m][0], lbufs[m][1], XS[m][1], pcol(l, "mu", m * 8 + j), pcol(l, "ommu", m * 8 + j))
        ld(0)
        ld(1)
        sh(0)
        ld(2)
        sh(1)
        sh(2)
        if l == 0:
            S.dma("sp", X.VF_d[j], VS[:], [B_VS], X.B_VF[j])
        else:
            S.dma("sp", X32[:], X.VF_d[j], [X.B_VF[j]], B_X32)

            def vres_chain(i, t0, n):
                pb_, Bpb_ = PB[i]
                g_, Bg = T32[2 * i]
                d_, Bd_ = T32[2 * i + 1]
                MM(pb_[0:128, 0:n], vlb, vlT[:, t0:t0 + n], [B_vlb, B_vlT], [Bpb_])
                yield
                ACT(g_[:, 0:n], pb_[:, 0:n], AF.Sigmoid, [Bpb_, B_pv], [Bg], bias=pcol(l, "v0", j))
                TT(d_[:, 0:n], X32[:, t0:t0 + n], VS[:, t0:t0 + n], ALU.subtract, [B_X32, B_VS], [Bd_])
                yield
                TT(d_[:, 0:n], d_[:, 0:n], g_[:, 0:n], ALU.mult, [Bd_, Bg], [Bd_])
                yield
                TT(VS[:, t0:t0 + n], VS[:, t0:t0 + n], d_[:, 0:n], ALU.add, [B_VS, Bd_], [B_VSt[i]])
                yield
            B_VSt = [Buf("vs_t%d_%d_%d" % (l, j, i)) for i in range(5)]
            rr([vres_chain(i, t0, n) for i, (t0, n) in enumerate(TILES)])
            S.op("dve", lambda e, a=T32[11][0][:, 0:1]: e.memset(a, 0.0), B_VSt, [T32[11][1], B_VS])

        def kk_chain(i, t0, n):
            pb_, Bpb_ = PB[i]
            kr, Bkr = T32[3 * i]
            nr, Bnr = T32[3 * i + 1]
            sq_, Bsq = T32[3 * i + 2]
            ACT(kr[:, 0:n], KS[:, t0:t0 + n], AF.Identity, [B_KS, B_pv], [Bkr], scale=pcol(l, "kk", j))
            yield
            TT(sq_[:, 0:n], kr[:, 0:n], kr[:, 0:n], ALU.mult, [Bkr], [Bsq])
            yield
            MM(pb_[:, 0:n], cst[:, CS["ob"]:CS["ob"] + 128], sq_[:, 0:n], [B_cst, Bsq], [Bpb_])
            yield
            ACT(nr[:, 0:n], pb_[:, 0:n], AF.Sqrt, [Bpb_], [Bnr])
            yield
            TS(nr[:, 0:n], nr[:, 0:n], 1e-12, None, ALU.max, None, [Bnr], [Bnr])
            RECIP(nr[:, 0:n], nr[:, 0:n], [Bnr], [Bnr])
            yield
            TT(KK[:, t0:t0 + n], kr[:, 0:n], nr[:, 0:n], ALU.mult, [Bkr, Bnr], [B_KKt[i]])
            yield
        B_KKt = [Buf("kk_t%d_%d_%d" % (l, j, i)) for i in range(5)]
        if 'K' in X.skip:
            for i, (t0, n) in enumerate(TILES[0:4]):
                rr([kk_chain(i, t0, n)])
        else:
            rr([kk_chain(i, t0, n) for i, (t0, n) in enumerate(TILES[0:4])])
        rr([kk_chain(0, *TILES[4])])
        S.op("dve", lambda e, a=T32[11][0][:, 1:2]: e.memset(a, 0.0), B_KKt, [T32[11][1], B_KK])
        Bp01 = [PTRh[0][1], PTRh[1][1]]
        for (t0, n) in TILES:
            nch = n // 64
            c0 = t0 // 64
            CP("act", VP[:, 0:nch, 1, :], c3(VS[:, t0:t0 + n]), [B_VS], [B_VP])
            for c in range(nch):
                TR(PTRt[:, c * 128:(c + 1) * 128], VP[:, c, :, :].rearrange("p a t -> p (a t)"), identb[:], [B_VP, B_identb], Bp01[0:1])
            for Z in STR:
                CP("dve", Z.UV[0][64:128, c0:c0 + nch, :, :].rearrange("p c h v -> p c (h v)"),
                   PTRt[64:128, 0:nch * 128].rearrange("p (c f) -> p c f", f=128), Bp01[0:1], [Z.UV[1]])
        if X.p3stop == 3:
            raise _Stop()
        gens = [sweep(j, STR[0]), sweep(j, STR[1])]
        if 'Q' in X.skip:
            for g in gens:
                for _ in g:
                    pass
            gens = []
        while gens:
            for g in list(gens):
                try:
                    next(g)
                except StopIteration:
                    gens.remove(g)
        (Y0, B_Y0), (Y1, B_Y1) = STR[0].Y, STR[1].Y
        (KB0, B_KB0), (KB1, B_KB1) = STR[0].KB, STR[1].KB

        def fin_chain(k, ti, t0, n):
            (YS, B_YS), (YC, B_YC), (BON, B_BON), (KBS, B_KBS) = T32[4 * k:4 * k + 4]
            SQf, B_SQf = KBS, B_KBS
            pb_, Bpb_ = PB[k]
            TT(YS[:, 0:n], Y0[:, t0:t0 + n], Y1[:, t0:t0 + n], ALU.add, [B_Y0, B_Y1], [B_YS])
            TT(KBS[:, 0:n], KB0[:, t0:t0 + n], KB1[:, t0:t0 + n], ALU.add, [B_KB0, B_KB1], [B_KBS])
            yield
            rkb, B_rkb = RKBs[k]
            TT(rkb[:, 0:n], RS[:, t0:t0 + n], KBS[:, 0:n], ALU.mult, [B_RS, B_KBS], [B_rkb])
            yield
            MM(pb_[:, 0:n], obb[:], rkb[:, 0:n], [B_obb, B_rkb], [Bpb_])
            yield
            TT(BON[:, 0:n], pb_[:, 0:n], VS[:, t0:t0 + n], ALU.mult, [Bpb_, B_VS], [B_BON])
            MM(pb_[:, 0:n], ob64[:], YS[:, 0:n], [B_ob64, B_YS], [Bpb_])
            yield
            TT(YC[:, 0:n], YS[:, 0:n], pb_[:, 0:n], ALU.subtract, [B_YS, Bpb_], [B_YC])
            yield
            TT(SQf[:, 0:n], YC[:, 0:n], YC[:, 0:n], ALU.mult, [B_YC], [B_SQf])
            yield
            MM(pb_[:, 0:n], ob64[:], SQf[:, 0:n], [B_ob64, B_SQf], [Bpb_])
            yield
            TS(YS[:, 0:n], pb_[:, 0:n], GN_EPS, None, ALU.add, None, [Bpb_], [B_YS])
            yield
            ACT(YS[:, 0:n], YS[:, 0:n], AF.Sqrt, [B_YS], [B_YS])
            yield
            RECIP(YS[:, 0:n], YS[:, 0:n], [B_YS], [B_YS])
            yield
            TT(YC[:, 0:n], YC[:, 0:n], YS[:, 0:n], ALU.mult, [B_YC, B_YS], [B_YC])
            yield
            ACT(YC[:, 0:n], YC[:, 0:n], AF.Identity, [B_YC, B_pv], [B_YC], scale=pcol(l, "gnw", j), bias=pcol(l, "gnb", j))
            yield
            TT(YC[:, 0:n], YC[:, 0:n], BON[:, 0:n], ALU.add, [B_YC, B_BON], [B_YC])
            yield
            ygt, B_ygt = X.ygst[(j * 5 + ti) % 2]
            TT(ygt[:, 0:n], YC[:, 0:n], SGa[:, t0:t0 + n], ALU.mult, [B_YC, B_SGa], [B_ygt])
            S.dma("sp", X.YG_d[j][:, t0:t0 + n], ygt[:, 0:n], [B_ygt], X.B_YG)
            yield
        ftiles = [(ti, t0, n) for ti, (t0, n) in enumerate(TILES) if not (last and ti == 0)]
        NF = 3
        for i0 in range(0, len(ftiles), NF):
            if 'F' in X.skip:
                for k in range(min(NF, len(ftiles) - i0)):
                    rr([fin_chain(k, *ftiles[i0 + k])])
            else:
                rr([fin_chain(k, *ftiles[i0 + k]) for k in range(min(NF, len(ftiles) - i0))])


def phase4(X, stk):
    S, l, last = X.S, X.l, X.last
    MM, TT, STT, MEMSET = X.MM, X.TT, X.STT, X.MEMSET
    PB, cst, B_cst, B_pv, pcol = X.PB, X.cst, X.B_cst, X.B_pv, X.pcol
    yb, B_yb = X.yb, X.B_yb
    T32 = X.T32

    def sb(name, shape, dt=F32):
        return X.sb(stk, "p4" + name, shape, dt)
    PPs = [sb("pp%d" % i, [128, NT + 32]) for i in range(2)]
    PW = [sb("w%d" % i, [128, 2, 128], BF16) for i in range(2)]
    PLD = [sb("pl%d" % i, [128, NT], BF16) for i in range(2)]
    SGb, B_SGb = sb("sgb", [128, NT], BF16)
    WA, B_WA = sb("wa", [128, 2080])
    WB, B_WB = sb("wb", [128, 2080])
    for t_, B_ in PPs:
        MEMSET("dve", t_[:], 0.0, [B_])
    wins = (2, 4, 8, 16)
    segs = [(8, NCTX, 0), (NCTX + 24, NLAT, NCTX)]
    for gi in range(4):
        w = wins[gi]
        for k2 in range(2):
            ptile = gi * 2 + k2
            pp, B_pp = PPs[k2]
            pl, B_pl = PLD[k2]
            for (po, ln, to) in segs:
                S.dma("sp", pp[:, po:po + ln], X.U32_d[24 + ptile][:, to:to + ln], [X.B_U32[24 + ptile]], B_pp)
            for (po, ln, to) in segs:
                for s0 in range(0, ln, 2048):
                    n = min(2048, ln - s0)
                    base = po + s0
                    cur, B_cur, nxt, B_nxt = WA, B_WA, WB, B_WB
                    TT(cur[:, 1:n + 15], pp[:, base - 8:base + n + 6], pp[:, base - 7:base + n + 7], ALU.add, [B_pp], [B_cur])
                    ww, lo, hi = 2, 1, n + 15
                    while ww < w:
                        hf = ww // 2
                        lo2, hi2 = lo + hf, hi - hf
                        TT(nxt[:, lo2:hi2], cur[:, lo2 - hf:hi2 - hf], cur[:, lo2 + hf:hi2 + hf], ALU.add, [B_cur], [B_nxt])
                        cur, B_cur, nxt, B_nxt = nxt, B_nxt, cur, B_cur
                        lo, hi = lo2, hi2
                        ww *= 2
                    STT(pl[:, to + s0:to + s0 + n], cur[:, 8:8 + n], 1.0 / w, pp[:, base:base + n], ALU.mult, ALU.subtract, [B_cur, B_pp], [B_pl])
                    tm, B_tm = T32[2]
                    if s0 == 0:
                        cc = cst[:, CS["pcor"] + gi * 16:CS["pcor"] + gi * 16 + 8]
                        TT(tm[:, 0:8], cur[:, 8:16], cc, ALU.mult, [B_cur, B_cst], [B_tm])
                        TT(pl[:, to:to + 8], tm[:, 0:8], pp[:, base:base + 8], ALU.subtract, [B_tm, B_pp], [B_pl])
                    if s0 + n == ln:
                        cc = cst[:, CS["pcor"] + gi * 16 + 8:CS["pcor"] + gi * 16 + 16]
                        TT(tm[:, 8:16], cur[:, n:n + 8], cc, ALU.mult, [B_cur, B_cst], [B_tm])
                        TT(pl[:, to + ln - 8:to + ln], tm[:, 8:16], pp[:, base + n - 8:base + n], ALU.subtract, [B_tm, B_pp], [B_pl])
        for k2 in range(2):
            wt, Bw = PW[k2]
            f_, Bf_ = T32[4 + k2]
            S.dma("sp", f_[:, 0:256], X.pw_d[l][gi][k2 * 128:(k2 + 1) * 128, :], [], Bf_)
            X.CP("act", wt[:], f_[:, 0:256].rearrange("p (o c) -> p o c", c=128), [Bf_], [Bw])
        for o2 in range(2):
            otile = gi * 2 + o2
            S.dma("sp", SGb[:], X.SG_d[8 + otile], [X.B_SG[8 + otile]], B_SGb)
            for ti, (t0, n) in enumerate(TILES):
                if last and ti == 0:
                    continue
                pt, Bp = PB[(o2 + ti) % 2]
                for k2 in range(2):
                    MM(pt[:, 0:n], PW[k2][0][:, o2, :], PLD[k2][0][:, t0:t0 + n], [PW[k2][1], PLD[k2][1]], [Bp], start=(k2 == 0), stop=(k2 == 1))
                STT(yb[:, otile, t0:t0 + n], pt[:, 0:n], pcol(l, "psc", otile), SGb[:, t0:t0 + n], ALU.mult, ALU.mult, [Bp, B_pv, B_SGb], [B_yb])


def phase5a(X, stk):
    S, l, last = X.S, X.l, X.last
    MM, TT = X.MM, X.TT
    PB = X.PB
    yg, B_yg, yb, B_yb = X.yg, X.B_yg, X.yb, X.B_yb

    def sb(name, shape, dt=F32):
        return X.sb(stk, "p5a" + name, shape, dt)
    MGT = [sb("mg%d" % i, [128, 2, NT], BF16) for i in range(2)]
    MST = [sb("ms%d" % i, [128, NT], BF16) for i in range(2)]
    YA = [X.T32[0], X.T32[1]]
    tiles = [(ti, t0, n) for ti, (t0, n) in enumerate(TILES) if not (last and ti == 0)]
    tlo = tiles[0][1]
    rr = 0
    for fb in range(4):
        wab, Bwab = X.WS[fb % 2]
        S.dma("pool", wab[:, 0:8, :], X.woa_d[l][:, fb * 512:(fb + 1) * 512].rearrange("(k p) c -> p k c", p=128), [], Bwab)
        S.dma("pool", wab[:, 8:16, :], X.wob_d[l][:, fb * 512:(fb + 1) * 512].rearrange("(k p) c -> p k c", p=128), [Bwab], Bwab)
        wa, Bwa = wab[:, 0:8, :], Bwab
        wb, Bwb = wab[:, 8:16, :], Bwab
        for q in range(4):
            f = fb * 4 + q
            mg, Bmg = MGT[f % 2]
            ms, Bms = MST[f % 2]
            S.dma("sp", mg[:, 0, tlo:NT], X.MG_d[f][:, tlo:NT], [X.B_MG[f]], Bmg)
            S.dma("sp", mg[:, 1, tlo:NT], X.MG_d[16 + f][:, tlo:NT], [X.B_MG[16 + f]], Bmg)
            for (ti, t0, n) in tiles:
                pa_, Bpa = PB[(rr * 2) % 6]
                pb_, Bpb = PB[(rr * 2 + 1) % 6]
                ya, B_ya = YA[rr % 2]
                rr += 1
                for k in range(8):
                    MM(pa_[:, 0:n], wa[:, k, q * 128:(q + 1) * 128], yg[:, k, t0:t0 + n], [Bwa, B_yg], [Bpa], start=(k == 0), stop=(k == 7))
                for k in range(8):
                    MM(pb_[:, 0:n], wb[:, k, q * 128:(q + 1) * 128], yb[:, k, t0:t0 + n], [Bwb, B_yb], [Bpb], start=(k == 0), stop=(k == 7))
                TT(ya[:, 0:n], pa_[:, 0:n], mg[:, 0, t0:t0 + n], ALU.mult, [Bpa, Bmg], [B_ya])
                TT(ms[:, t0:t0 + n], pb_[:, 0:n], mg[:, 1, t0:t0 + n], ALU.mult, [Bpb, Bmg], [Bms])
                TT(ms[:, t0:t0 + n], ms[:, t0:t0 + n], ya[:, 0:n], ALU.add, [Bms, B_ya], [Bms])
            S.dma("act", X.MER_d[f][:, tlo:NT], ms[:, tlo:NT], [Bms], X.B_MER)


def phase5b(X, stk):
    S, l, last = X.S, X.l, X.last
    MM, TR, ACT, SQACC, TT, TS, STT, RECIP = X.MM, X.TR, X.ACT, X.SQACC, X.TT, X.TS, X.STT, X.RECIP
    PB, cst, B_cst, mod, B_mod = X.PB, X.cst, X.B_cst, X.mod, X.B_mod
    ident = cst[:, CS["ident"]:CS["ident"] + 128]

    def sb(name, shape, dt=F32):
        return X.sb(stk, "p5b" + name, shape, dt)
    WO, B_WO = sb("wo", [128, 16, D], BF16)
    WOB = [Buf("wo_%d_%d" % (l, i)) for i in range(4)]
    MERs = [sb("mer%d" % i, [128, 16, 512], BF16) for i in range(2)]
    H, B_H = sb("H", [128, 4, D])
    OG = [X.T32[0], X.T32[1]]
    ST5, B_ST5 = sb("st", [128, 8])
    for fb in range(4):
        S.dma("pool", WO[:, :, fb * 512:(fb + 1) * 512], X.wo_d[l][:, fb * 512:(fb + 1) * 512].rearrange("(k p) c -> p k c", p=128), [], WOB[fb])
    if last:
        FG = X.t32big[:, 3 * 512:7 * 512]
        FGB = [X.T32[i][1] for i in range(3, 7)]
        B_FG = FGB[0]
        for B_ in FGB[1:]:
            S.op("dve", lambda e, a=X.T32[7][0][:, 0:1]: e.memset(a, 0.0), [], [B_, X.T32[7][1]])
        S.dma("sp", FG, X.fg_d, [X.T32[7][1]], B_FG)
    src = X.xin if l == 0 else X.H1_d
    cnt = 0
    for ti, (t0, n) in enumerate(TILES):
        if last and ti == 0:
            continue
        nb = n // 128
        si = 1 if ti == 0 else 0
        rd = [] if l == 0 else [X.B_H1[ti]]
        MER, B_MER = MERs[cnt % 2]
        cnt += 1
        S.dma("sp", MER[:, :, 0:n], X.MER_d[:, :, t0:t0 + n].rearrange("f p t -> p f t"), [X.B_MER], B_MER)
        S.dma("sp", H[:, 0:nb, :], src[t0:t0 + n, :].rearrange("(b p) f -> p b f", p=128), rd, B_H)
        def tail(f, og, Bog):
            ptt, Bptt = PB[3 + (f % 3)]
            for b in range(nb):
                TR(ptt[:, b * 128:(b + 1) * 128], og[:, b * 128:(b + 1) * 128], ident, [Bog, B_cst], [Bptt])
            TT(H[:, 0:nb, f * 128:(f + 1) * 128], H[:, 0:nb, f * 128:(f + 1) * 128],
               ptt[:, 0:nb * 128].rearrange("p (b f) -> p b f", f=128), ALU.add, [B_H, Bptt], [B_H])
        pend = None
        for f in range(16):
            po_, Bpo = PB[f % 3]
            for k in range(16):
                MM(po_[:, 0:n], WO[:, k, f * 128:(f + 1) * 128], MER[:, k, 0:n], [WOB[f // 4], B_MER], [Bpo], start=(k == 0), stop=(k == 15))
            og, Bog = OG[f % 2]
            ACT(og[:, 0:n], po_[:, 0:n], AF.Identity, [Bpo, B_mod], [Bog], scale=mod[:, 32 + f, si:si + 1])
            if pend is not None:
                tail(*pend)
            pend = (f, og, Bog)
        tail(*pend)
        if not last:
            S.dma("act", X.H1_d[t0:t0 + n, :].rearrange("(b p) f -> p b f", p=128), H[:, 0:nb, :], [B_H], X.B_H1[ti])
        else:
            for b in range(nb):
                SQACC(H[:, b, :], MER[:, 0:4, :].rearrange("p a t -> p (a t)"), ST5[:, b:b + 1], [B_H], [B_MER, B_ST5])
            TS(ST5[:, 0:nb], ST5[:, 0:nb], 1.0 / D, 1e-6, ALU.mult, ALU.add, [B_ST5], [B_ST5])
            ACT(ST5[:, 0:nb], ST5[:, 0:nb], AF.Sqrt, [B_ST5], [B_ST5])
            RECIP(ST5[:, 0:nb], ST5[:, 0:nb], [B_ST5], [B_ST5])
            for b in range(nb):
                STT(H[:, b, :], H[:, b, :], ST5[:, b:b + 1], FG, ALU.mult, ALU.mult, [B_H, B_ST5, B_FG], [B_H])
            lt0 = t0 - NCTX
            X.finals.append(S.dma("act", X.out_d[lt0:lt0 + n, :].rearrange("(b p) f -> p b f", p=128), H[:, 0:nb, :], [B_H], X.B_OUT[ti]))


def _pk(v):
    v = np.asarray(v, np.float32).reshape(-1, 128)
    return np.ascontiguousarray(v.T)


def _consts():
    cs = np.zeros((128, NCS), np.float32)
    cs[:, CS["ident"]:CS["ident"] + 128] = np.eye(128, dtype=np.float32)
    ob = np.zeros((128, 128), np.float32)
    ob[0:64, 0:64] = 1.0
    ob[64:128, 64:128] = 1.0
    cs[:, CS["ob"]:CS["ob"] + 128] = ob
    s = np.arange(64)[:, None]
    t = np.arange(64)[None, :]
    for d in range(2):
        strict = (s < t) if d == 0 else (s > t)
        incl = (s <= t) if d == 0 else (s >= t)
        m = np.zeros((128, 128), np.float32)
        m[0:64, 0:64] = strict
        m[64:128, 0:64] = strict
        m[0:64, 64:128] = incl
        m[64:128, 64:128] = incl
        cs[:, CS["mg%d" % d]:CS["mg%d" % d] + 128] = m
        cs[0:64, CS["mq%d" % d]:CS["mq%d" % d] + 64] = strict.T
    r = np.ones(512, np.float32)
    r[::64] = 0.0
    cs[:, CS["rst"]:CS["rst"] + 512] = r[None, :]
    for gi, w in enumerate((2, 4, 8, 16)):
        left = w // 2
        right = w - 1 - left
        for i in range(8):
            cnt_f = min(i + right, 10 ** 9) - max(i - left, 0) + 1
            cs[:, CS["pcor"] + gi * 16 + i] = 1.0 / cnt_f
            tb = -8 + i
            cnt_b = min(tb + right, -1) - (tb - left) + 1
            cs[:, CS["pcor"] + gi * 16 + 8 + i] = 1.0 / cnt_b
    return cs


def _pack_pv(inp):
    pvs = np.zeros((DEPTH, 128, NPV), np.float32)
    for l in range(DEPTH):
        def put(name, arr):
            a = _pk(arr)
            pvs[l, :, PV[name]:PV[name] + a.shape[1]] = a
        put("mu", inp["tok_mu"][l].reshape(-1))
        put("ommu", 1.0 - inp["tok_mu"][l].reshape(-1))
        put("w0", inp["decay_w0"][l].reshape(-1))
        put("a0", inp["iclr_a0"][l].reshape(-1))
        put("kk", inp["key_k"][l])
        put("ka", inp["key_a"][l].reshape(-1))
        put("omka", 1.0 - inp["key_a"][l].reshape(-1))
        put("brk", inp["bonus_rk"][l].reshape(-1))
        put("gnw", inp["gn_w"][l])
        put("gnb", inp["gn_b"][l])
        put("psc", inp["pool_scale"][l])
        put("ng", inp["norm_g"][l])
        put("adab", inp["ada_b"][l])
        if l > 0:
            put("v0", inp["vres_v0"][l - 1])
        put("fg", inp["final_g"])
    return pvs


_NC_CACHE = {}


def kernel(**inputs):
    inp = {k: np.asarray(v) for k, v in inputs.items()}
    nb = inp["x"].shape[0]
    if "nc" not in _NC_CACHE:
        _NC_CACHE["nc"] = build_program()
    nc = _NC_CACHE["nc"]
    cs = _consts()
    pvs = _pack_pv(inp)
    fgrep = np.ascontiguousarray(np.broadcast_to(inp["final_g"].astype(np.float32)[None, :], (128, D)))
    shared = {
        "ada_w": np.ascontiguousarray(inp["ada_w"], np.float32),
        "w_in": np.ascontiguousarray(inp["w_in"], np.float32),
        "dlb": np.ascontiguousarray(inp["decay_lora_b"].reshape(DEPTH, 128, 1024), np.float32),
        "ilb": np.ascontiguousarray(inp["iclr_lora_b"].reshape(DEPTH, 128, 1024), np.float32),
        "vla": np.ascontiguousarray(inp["vres_lora_a"][0], np.float32),
        "vlb": np.ascontiguousarray(inp["vres_lora_b"][0], np.float32),
        "pool_w": np.ascontiguousarray(inp["pool_w"], np.float32),
        "w_out_a": np.ascontiguousarray(inp["w_out_a"], np.float32),
        "w_out_b": np.ascontiguousarray(inp["w_out_b"], np.float32),
        "w_out": np.ascontiguousarray(inp["w_out"], np.float32),
        "pv": pvs,
        "cst": cs,
        "fgrep": fgrep,
    }
    in_maps = []
    for b in range(nb):
        m = dict(shared)
        m["xin"] = np.ascontiguousarray(np.concatenate([inp["ctx"][b], inp["x"][b]], axis=0), np.float32)
        cT = np.zeros((128, 32), np.float32)
        cT[:, 0::2] = _pk(inp["c"][b])
        cT[:, 1::2] = _pk(inp["c_ctx"])
        m["cT"] = cT
        in_maps.append(m)
    res = run_bass_kernel_spmd(nc, in_maps, core_ids=list(range(nb)))
    out = np.stack([np.asarray(r["out"], np.float32) for r in res.results], axis=0)
    return out
```

```python
import contextlib
import numpy as np
import concourse.bass as bass
import concourse.mybir as mybir
from concourse.bass_utils import run_bass_kernel_spmd

F32 = mybir.dt.float32
BF16 = mybir.dt.bfloat16
AF = mybir.ActivationFunctionType
ALU = mybir.AluOpType

import os as _os
SAME_ENGINE_KINDS = tuple(_os.environ.get('KSE', 'raw').split(','))
SEM_LIMIT = 30000

D = 2048
NT = 2304
NCTX = 256
NLAT = 2048
DEPTH = 2
INW = 10496
OFF_POOL = 3072
OFF_GA = 4096
OFF_GB = 5120
OFF_DEC = 6144
OFF_ICL = 6272
OFF_MG = 6400
C0 = float(np.exp(-0.5))
GN_EPS = 64e-5
TILES = [(0, 256), (256, 512), (768, 512), (1280, 512), (1792, 512)]
NCH = NT // 64

PV = {}
_o = 0
for _n, _w in [("mu", 24), ("w0", 16), ("a0", 16), ("kk", 8), ("ka", 16), ("brk", 16), ("gnw", 8), ("gnb", 8),
               ("psc", 8), ("ng", 16), ("adab", 48), ("v0", 8), ("fg", 16), ("omka", 16), ("ommu", 24)]:
    PV[_n] = _o
    _o += _w
NPV = _o
CS = {}
_o = 0
for _n, _w in [("ident", 128), ("ob", 128), ("mg0", 128), ("mg1", 128), ("mq0", 64), ("mq1", 64), ("rst", 512),
               ("pcor", 64)]:
    CS[_n] = _o
    _o += _w
NCS = _o


class Buf:
    __slots__ = ("name", "w", "r", "sem", "ndma")

    def __init__(self, name):
        self.name = name
        self.w = None
        self.r = []
        self.sem = None
        self.ndma = 0


class Op:
    __slots__ = ("eng", "fn", "deps", "signal", "count", "is_dma", "buf")

    def __init__(self, eng, fn, is_dma=False, buf=None):
        self.eng = eng
        self.fn = fn
        self.deps = []
        self.signal = False
        self.count = None
        self.is_dma = is_dma
        self.buf = buf


class Sched:
    ENGS = ("pe", "act", "dve", "pool", "sp")

    def __init__(self, nc):
        self.nc = nc
        self.ops = {e: [] for e in self.ENGS}
        self.dma_bufs = []
        self.bar_id = 0
        self.bar_deps = []
        self.eng_bar = {e: 0 for e in self.ENGS}
        self.dmas_since = []

    def barrier(self):
        X = []
        for e in self.ENGS:
            for o in reversed(self.ops[e]):
                if not o.is_dma:
                    X.append(o)
                    break
        lastd = {}
        for o in self.dmas_since:
            lastd[id(o.buf)] = o
        X.extend(lastd.values())
        self.dmas_since = []
        self.bar_deps = X
        self.bar_id += 1

    def _add(self, op, reads, writes):
        deps = []
        if self.eng_bar[op.eng] < self.bar_id:
            deps.extend((d, "raw") for d in self.bar_deps)
            self.eng_bar[op.eng] = self.bar_id
        for b in reads:
            if b.w is not None:
                deps.append((b.w, "raw"))
        for b in writes:
            if b.w is not None:
                deps.append((b.w, "waw"))
            deps.extend((r, "war") for r in b.r)
        seen = set()
        for d, kind in deps:
            if d is op:
                continue
            if (not d.is_dma) and (not op.is_dma) and d.eng == op.eng:
                if d.eng == "pe" or kind not in SAME_ENGINE_KINDS:
                    continue
            if id(d) in seen:
                continue
            seen.add(id(d))
            d.signal = True
            op.deps.append(d)
        for b in reads:
            b.r.append(op)
        for b in writes:
            b.w = op
            b.r = []
        self.ops[op.eng].append(op)
        return op

    def op(self, eng, fn, reads=(), writes=()):
        return self._add(Op(eng, fn), list(reads), list(writes))

    def dma(self, eng, out_ap, in_ap, reads, write):
        op = Op(eng, None, is_dma=True, buf=write)
        op.fn = lambda e, o=out_ap, i=in_ap: e.dma_start(out=o, in_=i)
        if write.sem is None:
            self.dma_bufs.append(write)
            write.sem = True
        self._add(op, list(reads), [write])
        write.ndma += 1
        op.count = 16 * write.ndma
        op.signal = True
        self.dmas_since.append(op)
        return op

    def emit(self, final_waits=()):
        nc = self.nc
        with contextlib.ExitStack() as st:
            for b in self.dma_bufs:
                b.sem = st.enter_context(nc.semaphore("d_" + b.name))
            esems = {}
            for e in self.ENGS:
                n = 0
                for o in self.ops[e]:
                    if o.is_dma:
                        continue
                    if o.signal:
                        n += 1
                        o.count = n
                nsem = max(0, n - 1) // SEM_LIMIT + 1
                esems[e] = [st.enter_context(nc.semaphore("e_%s%d" % (e, i))) for i in range(nsem)]

            def semval(o):
                if o.is_dma:
                    return o.buf.sem, o.count, ("d", id(o.buf))
                k = (o.count - 1) // SEM_LIMIT
                return esems[o.eng][k], o.count - k * SEM_LIMIT, ("e", o.eng, k)

            block = st.enter_context(nc.Block())

            def run(ename, eng):
                waited = {}
                for o in self.ops[ename]:
                    need = {}
                    for d in o.deps:
                        sem, val, key = semval(d)
                        if waited.get(key, 0) >= val:
                            continue
                        if key not in need or need[key][1] < val:
                            need[key] = (sem, val)
                    for key, (sem, val) in need.items():
                        eng.wait_ge(sem, val)
                        waited[key] = val
                    ins = o.fn(eng)
                    if o.signal:
                        if o.is_dma:
                            ins.then_inc(o.buf.sem, 16)
                        else:
                            k = (o.count - 1) // SEM_LIMIT
                            ins.then_inc(esems[ename][k], 1)
                if ename == "sp":
                    for o in final_waits:
                        sem, val, key = semval(o)
                        eng.wait_ge(sem, val)

            @block.tensor
            def _(eng):
                run("pe", eng)

            @block.scalar
            def _(eng):
                run("act", eng)

            @block.vector
            def _(eng):
                run("dve", eng)

            @block.gpsimd
            def _(eng):
                run("pool", eng)

            @block.sync
            def _(eng):
                run("sp", eng)


class Ctx:
    pass


class _Stop(Exception):
    pass


def build_program(depth=DEPTH, dbg=None):
    nc = bass.Bass("TRN2", target_bir_lowering=False)
    S = Sched(nc)
    dbg = dbg or {}
    X = Ctx()
    X.nc, X.S = nc, S
    import os
    X.stop_after = tuple(int(v) for v in os.environ['KSTOP'].split(',')) if os.environ.get('KSTOP') else None
    X.nj = int(os.environ.get('KNJ', '8'))
    X.skip = os.environ.get('KSKIP', '')
    X.p3stop = int(os.environ.get('KP3', '0'))
    X.noy = os.environ.get('KNOY', '')

    def din(name, shape, dt=F32):
        return nc.dram_tensor(name, list(shape), dt, kind="ExternalInput").ap()

    def dscr(name, shape, dt=F32):
        return nc.dram_tensor(name, list(shape), dt, kind="Internal").ap()

    X.xin = din("xin", [NT, D])
    cT_d = din("cT", [128, 32])
    X.ada_w = din("ada_w", [DEPTH, D, 3 * D])
    X.w_in = din("w_in", [DEPTH, D, INW])
    X.dlb_d = din("dlb", [DEPTH, 128, 1024])
    X.ilb_d = din("ilb", [DEPTH, 128, 1024])
    X.vla_d = din("vla", [D, 32])
    X.vlb_d = din("vlb", [32, 1024])
    X.pw_d = din("pool_w", [DEPTH, 4, 256, 256])
    X.woa_d = din("w_out_a", [DEPTH, 1024, D])
    X.wob_d = din("w_out_b", [DEPTH, 1024, D])
    X.wo_d = din("w_out", [DEPTH, D, D])
    pv_d = din("pv", [DEPTH, 128, NPV])
    cs_d = din("cst", [128, NCS])
    X.fg_d = din("fgrep", [128, D])
    X.out_d = nc.dram_tensor("out", [NLAT, D], F32, kind="ExternalOutput").ap()

    X.U32_d = dscr("U32", [32, 128, NT])
    X.SG_d = dscr("SG", [16, 128, NT], BF16)
    X.MG_d = dscr("MG", [32, 128, NT], BF16)
    X.VF_d = dscr("VF", [8, 128, NT])
    X.H1_d = dscr("H1", [NT, D])
    X.YG_d = dscr("YG", [8, 128, NT], BF16)
    X.MER_d = dscr("MER", [16, 128, NT], BF16)
    X.B_MER = Buf("MER")
    X.B_YG = Buf("YG")
    X.B_U32 = [Buf("U32")] * 32
    X.B_SG = [Buf("SG")] * 16
    X.B_MG = [Buf("MG")] * 32
    X.B_VF = [Buf("VF")] * 8
    X.B_H1 = [Buf("H1")] * 5
    X.B_OUT = [Buf("OUT")] * 5
    X.dbg_out = {}
    for k, shp in dbg.items():
        X.dbg_out[k] = (nc.dram_tensor("dbg_" + k, list(shp), F32, kind="ExternalOutput").ap(), Buf("dbg_" + k))
    X.finals = []

    def MM(out, lhsT, rhs, R, W, start=True, stop=True):
        S.op("pe", lambda e, o=out, l=lhsT, r=rhs, s=start, t=stop: e.matmul(o, lhsT=l, rhs=r, start=s, stop=t), R, W)

    def TR(out, in_, ident, R, W):
        S.op("pe", lambda e, o=out, i=in_, d=ident: e.transpose(o, i, d), R, W)

    def ACT(out, in_, func, R, W, bias=None, scale=None):
        kw = {}
        if bias is not None:
            kw["bias"] = bias
        if scale is not None:
            kw["scale"] = scale
        S.op("act", lambda e, o=out, i=in_, f=func, k=kw: e.activation(out=o, in_=i, func=f, **k), R, W)

    def SQACC(in_, junk, acc, R, W):
        S.op("act", lambda e, o=junk, i=in_, a=acc: e.activation(out=o, in_=i, func=AF.Square, accum_out=a), R, W)

    def CP(eng, out, in_, R, W):
        if eng == "act":
            S.op("act", lambda e, o=out, i=in_: e.copy(out=o, in_=i), R, W)
        else:
            S.op(eng, lambda e, o=out, i=in_: e.tensor_copy(out=o, in_=i), R, W)

    def TT(out, in0, in1, op, R, W, eng="dve"):
        S.op(eng, lambda e, o=out, a=in0, b=in1, p=op: e.tensor_tensor(out=o, in0=a, in1=b, op=p), R, W)

    def TS(out, in0, s1, s2, op0, op1, R, W, eng="dve"):
        if s2 is None:
            S.op(eng, lambda e, o=out, a=in0, x=s1, p=op0: e.tensor_scalar(out=o, in0=a, scalar1=x, scalar2=None, op0=p), R, W)
        else:
            S.op(eng, lambda e, o=out, a=in0, x=s1, y=s2, p=op0, q=op1: e.tensor_scalar(out=o, in0=a, scalar1=x, scalar2=y, op0=p, op1=q), R, W)

    def STT(out, in0, sc, in1, op0, op1, R, W, eng="dve"):
        S.op(eng, lambda e, o=out, a=in0, x=sc, b=in1, p=op0, q=op1: e.scalar_tensor_tensor(out=o, in0=a, scalar=x, in1=b, op0=p, op1=q), R, W)

    def RECIP(out, in_, R, W):
        S.op("dve", lambda e, o=out, i=in_: e.reciprocal(out=o, in_=i), R, W)

    def SCAN(out, d0, d1, R, W):
        S.op("dve", lambda e, o=out, a=d0, b=d1: e.tensor_tensor_scan(out=o, data0=a, data1=b, initial=0.0, op0=ALU.mult, op1=ALU.add), R, W)

    def MEMSET(eng, ap, val, W):
        S.op(eng, lambda e, a=ap, v=val: e.memset(a, v), [], W)

    X.MM, X.TR, X.ACT, X.SQACC, X.CP, X.TT, X.TS, X.STT, X.RECIP, X.SCAN, X.MEMSET = MM, TR, ACT, SQACC, CP, TT, TS, STT, RECIP, SCAN, MEMSET
    uid = [0]

    def sb(stack, name, shape, dt=F32):
        uid[0] += 1
        nm = "%s_%d" % (name, uid[0])
        t = stack.enter_context(nc.sbuf_tensor(nm, list(shape), dt))
        return t, Buf(nm)
    X.sb = sb

    def dbg_dump(key, ap, B):
        if key in X.dbg_out:
            o, Bo = X.dbg_out[key]
            X.finals.append(S.dma("sp", o, ap, [B], Bo))
    X.dbg_dump = dbg_dump

    with contextlib.ExitStack() as st:
        X.PB = []
        for i in range(6):
            t = st.enter_context(nc.psum_tensor("pb%d" % i, [128, 512], F32))
            X.PB.append((t, Buf("pb%d" % i)))
        X.PTRt = st.enter_context(nc.psum_tensor("ptr", [128, 1024], BF16))
        X.PTRt2 = st.enter_context(nc.psum_tensor("ptr2", [128, 1024], BF16))
        X.B_PTR = Buf("ptr")
        PB = X.PB

        X.cst, X.B_cst = sb(st, "cst", [128, NCS])
        X.pv, X.B_pv = sb(st, "pv", [128, DEPTH, NPV])
        X.identb, X.B_identb = sb(st, "identb", [128, 128], BF16)
        X.obb, X.B_obb = sb(st, "obb", [128, 128], BF16)
        X.ob64, X.B_ob64 = sb(st, "ob64", [128, 128])
        cT, B_cT = sb(st, "cTs", [128, 32])
        sc, B_sc = sb(st, "sc", [128, 16, 2], BF16)
        MODS = [sb(st, "mod%d" % i, [128, 48, 2]) for i in range(DEPTH)]
        GSCS = [sb(st, "gsc%d" % i, [128, 16, 2]) for i in range(DEPTH)]
        X.mod, X.B_mod = MODS[0]
        gsc, B_gsc = GSCS[0]
        X.dlrT, X.B_dlrT = sb(st, "dlrT", [128, NT], BF16)
        X.alrT, X.B_alrT = sb(st, "alrT", [128, NT], BF16)
        X.vlT, X.B_vlT = sb(st, "vlT", [32, NT], BF16)
        t32big, _ = sb(st, "t32big", [128, 12 * 512])
        X.t32big = t32big
        X.T32 = [(t32big[:, i * 512:(i + 1) * 512], Buf("t32_%d" % i)) for i in range(12)]
        cst, B_cst, pv, B_pv = X.cst, X.B_cst, X.pv, X.B_pv
        S.dma("sp", cst[:], cs_d, [], B_cst)
        for l in range(DEPTH):
            S.dma("sp", pv[:, l, :], pv_d[l], [], B_pv)
        S.dma("sp", cT[:], cT_d, [], B_cT)
        CP("act", X.identb[:], cst[:, CS["ident"]:CS["ident"] + 128], [B_cst], [X.B_identb])
        CP("act", X.obb[:], cst[:, CS["ob"]:CS["ob"] + 128], [B_cst], [X.B_obb])
        S.op("act", lambda e: e.mul(out=X.ob64[:], in_=cst[:, CS["ob"]:CS["ob"] + 128], mul=1.0 / 64.0), [B_cst], [X.B_ob64])
        ident = cst[:, CS["ident"]:CS["ident"] + 128]
        ACT(sc[:].rearrange("p k t -> p (k t)"), cT[:], AF.Silu, [B_cT], [B_sc])

        def pcol(l, name, j):
            o = PV[name] + j
            return pv[:, l, o:o + 1]
        X.pcol = pcol

        ws_rr = [0]

        def make_wslots(stack):
            X.WS = [sb(stack, "ws%d" % i, [128, 16, 512], BF16) for i in range(2)]

        def load_w(dram_ap_3d, nk, ncols):
            i = ws_rr[0] % 2
            ws_rr[0] += 1
            t, B = X.WS[i]
            S.dma("pool", t[:, 0:nk, 0:ncols], dram_ap_3d, [], B)
            return t, B
        X.load_w = load_w
        pa_rr = [0]

        def pacc():
            i = pa_rr[0] % 2
            pa_rr[0] += 1
            return PB[i]

        for l in range(depth):
            X.l = l
            X.last = last = (l == DEPTH - 1)
            with contextlib.ExitStack() as sA:
                make_wslots(sA)
                xnT, _bx = sb(sA, "xnT", [128, 16, NT], BF16)
                B_xnTs = [Buf("xnT_%d_%d" % (l, i)) for i in range(16)]
                B_xnT = B_xnTs
                stg = [sb(sA, "stg%d" % i, [128, NT]) for i in range(2)]
                stgb = [sb(sA, "stgb%d" % i, [128, NT], BF16) for i in range(2)]
                htile = [sb(sA, "ht%d" % i, [128, D]) for i in range(2)]
                sqj, B_sqj = stg[0][0][:, 0:D], stg[0][1]
                stat, B_stat = sb(sA, "stat", [128, 8])
                B_stats = [Buf("stat%d_%d" % (l, i)) for i in range(4)]
                htile = htile + [(stg[1][0][:, 0:D], stg[1][1])]
                X.mod, X.B_mod = MODS[l]
                mod, B_mod = MODS[l]
                gsc, B_gsc = GSCS[l]

                def phase0_gen(ll, slots, halfk):
                    pm, B_pm = PB[2]
                    mod_, B_mod_ = MODS[ll]
                    gsc_, B_gsc_ = GSCS[ll]
                    nh = 2 if halfk else 1
                    kper = 16 // nh
                    cnt = 0
                    for blk in range(12):
                        for hf in range(nh):
                            wt, Bw = slots[cnt % len(slots)]
                            cnt += 1
                            src_ = X.ada_w[ll][hf * kper * 128:(hf + 1) * kper * 128, blk * 512:(blk + 1) * 512]
                            S.dma("pool", wt[:, 0:kper, :], src_.rearrange("(k p) c -> p k c", p=128), [], Bw)
                            for ft in range(4):
                                fc = blk * 4 + ft
                                for kc in range(kper):
                                    MM(pm[:, hf * 96 + fc * 2:hf * 96 + fc * 2 + 2], wt[:, kc, ft * 128:(ft + 1) * 128], sc[:, hf * kper + kc, :],
                                       [Bw, B_sc], [B_pm], start=(kc == 0), stop=(kc == kper - 1))
                                yield
                    adab = pv[:, ll, PV["adab"]:PV["adab"] + 48]
                    TT(mod_[:], pm[:, 0:96].rearrange("p (f t) -> p f t", t=2), adab.unsqueeze(2).to_broadcast([128, 48, 2]), ALU.add,
                       [B_pm, B_pv], [B_mod_])
                    if halfk:
                        TT(mod_[:], mod_[:], pm[:, 96:192].rearrange("p (f t) -> p f t", t=2), ALU.add, [B_pm, B_mod_], [B_mod_])
                    ngv = pv[:, ll, PV["ng"]:PV["ng"] + 16]
                    STT(gsc_[:], mod_[:, 16:32, :], 1.0, ngv.unsqueeze(2).to_broadcast([128, 16, 2]), ALU.add, ALU.mult, [B_mod_, B_pv], [B_gsc_])
                    yield
                bg = None
                if l == 0:
                    for _ in phase0_gen(0, X.WS, False):
                        pass
                    if depth > 1:
                        aws = [sb(sA, "aws%d" % i, [128, 8, 512], BF16) for i in range(1)]
                        bg = phase0_gen(1, aws, True)
                if X.stop_after == (l, 0):
                    break
                src = X.xin if l == 0 else X.H1_d
                p1_rr = [0]
                for blk in range(NT // 128):
                    t0 = blk * 128
                    si = 1 if t0 < NCTX else 0
                    ht, Bh = htile[blk % 3]
                    rd = [] if l == 0 else [X.B_H1[0 if t0 < 256 else 1 + (t0 - 256) // 512]]
                    S.dma("sp", ht[:], src[t0:t0 + 128, :], rd, Bh)
                    c0 = (blk % 4) * 2
                    B_st = B_stats[blk % 4]
                    SQACC(ht[:], sqj[:], stat[:, c0:c0 + 1], [Bh], [B_sqj, B_st])
                    TS(stat[:, c0 + 1:c0 + 2], stat[:, c0:c0 + 1], 1.0 / D, 1e-6, ALU.mult, ALU.add, [B_st], [B_st])
                    ACT(stat[:, c0 + 1:c0 + 2], stat[:, c0 + 1:c0 + 2], AF.Sqrt, [B_st], [B_st])
                    RECIP(stat[:, c0 + 1:c0 + 2], stat[:, c0 + 1:c0 + 2], [B_st], [B_st])
                    TS(ht[:], ht[:], stat[:, c0 + 1:c0 + 2], None, ALU.mult, None, [Bh, B_st], [Bh])
                    for g4 in range(4):
                        banks = [PB[p1_rr[0] % 6], PB[(p1_rr[0] + 1) % 6]]
                        p1_rr[0] += 2
                        for q in range(4):
                            fc = g4 * 4 + q
                            pt, Bp = banks[q % 2]
                            TR(pt[:, (q // 2) * 128:(q // 2 + 1) * 128], ht[:, fc * 128:(fc + 1) * 128], ident, [Bh, B_cst], [Bp])
                        for q in range(4):
                            fc = g4 * 4 + q
                            pt, Bp = banks[q % 2]
                            psl = pt[:, (q // 2) * 128:(q // 2 + 1) * 128]
                            if q % 2 == 0:
                                ACT(xnT[:, fc, t0:t0 + 128], psl, AF.Identity, [Bp, B_gsc, B_mod], [B_xnTs[fc]],
                                    bias=mod[:, fc, si:si + 1], scale=gsc[:, fc, si:si + 1])
                            else:
                                TS(xnT[:, fc, t0:t0 + 128], psl, gsc[:, fc, si:si + 1], mod[:, fc, si:si + 1],
                                   ALU.mult, ALU.add, [Bp, B_gsc, B_mod], [B_xnTs[fc]])
                if l == 0:
                    dbg_dump("xn0", xnT[:, 0, :], B_xnT) if "xn0" in X.dbg_out and False else None

                if X.stop_after == (l, 1):
                    break
                def project(col0, ncols, handler):
                    wt, Bw = load_w(X.w_in[l][:, col0:col0 + ncols].rearrange("(k p) c -> p k c", p=128), 16, ncols)
                    for ft in range(ncols // 128):
                        ftile_ = col0 // 128 + ft
                        ctx_needed = (not last) or (8 <= ftile_ < 24) or (ftile_ in (OFF_DEC // 128, OFF_ICL // 128))
                        for (t0, n) in TILES:
                            if t0 == 0 and not ctx_needed:
                                continue
                            pt, Bp = pacc()
                            for kc in range(16):
                                MM(pt[:, 0:n], wt[:, kc, ft * 128:(ft + 1) * 128], xnT[:, kc, t0:t0 + n], [Bw, B_xnTs[kc]], [Bp],
                                   start=(kc == 0), stop=(kc == 15))
                            handler(ftile_, t0, n, pt, Bp)
                        if bg is not None:
                            next(bg, None)

                ev_rr = [0]

                def h_f32(ftile, t0, n, pt, Bp):
                    s_, Bs = stg[ftile % 2]
                    eng = "act" if ev_rr[0] % 2 else "dve"
                    ev_rr[0] += 1
                    CP(eng, s_[:, t0:t0 + n], pt[:, 0:n], [Bp], [Bs])
                    if t0 + n == NT:
                        S.dma("sp", X.U32_d[ftile], s_[:], [Bs], X.B_U32[ftile])

                def h_silu(ftile, t0, n, pt, Bp):
                    idx = ftile - OFF_GA // 128
                    s_, Bs = stgb[idx % 2]
                    ACT(s_[:, t0:t0 + n], pt[:, 0:n], AF.Silu, [Bp], [Bs])
                    if t0 + n == NT:
                        S.dma("sp", X.SG_d[idx], s_[:], [Bs], X.B_SG[idx])

                def h_lora(ftile, t0, n, pt, Bp):
                    if ftile > OFF_DEC // 128 + 1:
                        return
                    if ftile == OFF_DEC // 128:
                        ACT(X.dlrT[:, t0:t0 + n], pt[:, 0:n], AF.Tanh if 'T' not in X.skip else AF.Sigmoid, [Bp], [X.B_dlrT])
                    else:
                        CP("dve", X.alrT[:, t0:t0 + n], pt[:, 0:n], [Bp], [X.B_alrT])

                def h_sig(ftile, t0, n, pt, Bp):
                    idx = ftile - OFF_MG // 128
                    s_, Bs = stgb[idx % 2]
                    ACT(s_[:, t0:t0 + n], pt[:, 0:n], AF.Sigmoid, [Bp], [Bs])
                    if t0 + n == NT:
                        S.dma("sp", X.MG_d[idx], s_[:], [Bs], X.B_MG[idx])

                for blk in range(8 if 'a' not in X.skip else 1):
                    project(blk * 512, 512, h_f32)
                for blk in range(4 if 'b' not in X.skip else 0):
                    project(OFF_GA + blk * 512, 512, h_silu)
                if 'c' not in X.skip:
                    project(OFF_DEC, 512, h_lora)
                for blk in range(8 if 'd' not in X.skip else 0):
                    project(OFF_MG + blk * 512, 512, h_sig)
                if l > 0:
                    i = ws_rr[0] % 2
                    ws_rr[0] += 1
                    wt, Bw = X.WS[i]
                    vst, B_vst = stg[0]
                    S.dma("sp", vst[:, 0:512].rearrange("p (k c) -> p k c", c=32), X.vla_d.rearrange("(k p) c -> p k c", p=128), [], B_vst)
                    CP("act", wt[:, :, 0:32], vst[:, 0:512].rearrange("p (k c) -> p k c", c=32), [B_vst], [Bw])
                    for (t0, n) in TILES:
                        pt, Bp = pacc()
                        for kc in range(16):
                            MM(pt[0:32, 0:n], wt[:, kc, 0:32], xnT[:, kc, t0:t0 + n], [Bw, B_xnTs[kc]], [Bp], start=(kc == 0), stop=(kc == 15))
                        CP("dve", X.vlT[:, t0:t0 + n], pt[0:32, 0:n], [Bp], [X.B_vlT])
                if bg is not None:
                    for _ in bg:
                        pass
                if X.stop_after == (l, 2):
                    break
            S.barrier()
            with contextlib.ExitStack() as sB:
                stopped = False
                with contextlib.ExitStack() as s3:
                    X.ygst = [sb(s3, "ygst%d" % i, [128, 512], BF16) for i in range(2)]
                    try:
                        phase3(X, s3)
                    except _Stop:
                        stopped = True
                S.barrier()
                if stopped or X.stop_after == (l, 3):
                    break
                X.yg, X.B_yg = sb(sB, "yg", [128, 8, NT], BF16)
                for j_ in range(8):
                    S.dma("sp", X.yg[:, j_, :], X.YG_d[j_], [X.B_YG], X.B_yg)
                X.yb, X.B_yb = sb(sB, "yb", [128, 8, NT], BF16)
                with contextlib.ExitStack() as s4:
                    phase4(X, s4)
                S.barrier()
                with contextlib.ExitStack() as s5:
                    make_wslots(s5)
                    phase5a(X, s5)
            S.barrier()
            with contextlib.ExitStack() as s5b:
                phase5b(X, s5b)
            S.barrier()

        S.emit(final_waits=X.finals)
    return nc


def phase3(X, stk):
    S, l, last = X.S, X.l, X.last
    MM, TR, ACT, CP, TT, TS, STT, RECIP, SCAN, MEMSET = X.MM, X.TR, X.ACT, X.CP, X.TT, X.TS, X.STT, X.RECIP, X.SCAN, X.MEMSET
    PB, PTRt, cst, B_cst, B_pv, pcol = X.PB, X.PTRt, X.cst, X.B_cst, X.B_pv, X.pcol
    identb, B_identb, obb, B_obb, ob64, B_ob64 = X.identb, X.B_identb, X.obb, X.B_obb, X.ob64, X.B_ob64
    dlrT, B_dlrT, alrT, B_alrT, vlT, B_vlT = X.dlrT, X.B_dlrT, X.alrT, X.B_alrT, X.vlT, X.B_vlT
    T32 = X.T32

    def sb(name, shape, dt=F32):
        return X.sb(stk, "p3" + name, shape, dt)

    XS = [sb("xs%d" % i, [128, NT]) for i in range(3)]
    KK, B_KK = sb("kk", [128, NT])
    SGa, B_SGa = sb("sga", [128, NT], BF16)
    LW, B_lw = sb("lw", [128, 2, 1024], BF16)
    VLB, B_vlb = sb("vlb", [32, 1024], BF16)
    VP, B_VP = sb("vp", [128, 8, 2, 64], BF16)
    RKB, B_RKB = sb("rkb", [128, 512], BF16)
    RKBs = None
    SQ, B_SQ = sb("sq", [128, 512])
    t32b, _ = sb("t32b", [128, 8 * 512])
    temps = [T32[0:8], [(t32b[:, i * 512:(i + 1) * 512], Buf("t32b_%d" % i)) for i in range(8)]]
    RKBs = [(RKB, B_RKB), (temps[1][0][0].bitcast(BF16)[:, 0:512], temps[1][0][1]), (temps[1][1][0].bitcast(BF16)[:, 0:512], temps[1][1][1])]
    PTRh = [(PTRt[:, 0:512], Buf("ptr0")), (X.PTRt2[:, 0:512], Buf("ptr1"))]
    PSH, B_PSH = PB[0]

    class St:
        pass
    STR = []
    for d in range(2):
        Z = St()
        Z.d = d
        Z.KB = sb("kb%d" % d, [128, NT], BF16)
        Z.Y = sb("y%d" % d, [128, NT])
        Z.UV = sb("uv%d" % d, [128, NCH, 2, 64], BF16)
        Z.AR = sb("ar%d" % d, [128, 8, 2, 64], BF16)
        Z.BK = sb("bk%d" % d, [128, 8, 2, 64], BF16)
        Z.BKp = sb("bkp%d" % d, [128, 8, 2, 64], BF16)
        Z.BKpT = sb("bkpt%d" % d, [128, 8, 128], BF16)
        Z.AqT = sb("aqt%d" % d, [64, 8, 128], BF16)
        Z.GL = sb("gl%d" % d, [128, 8, 64], BF16)
        Z.GR = sb("gr%d" % d, [128, 16, 64], BF16)
        Z.Qs = [sb("q%d_%d" % (d, i), [64, 8, 64], BF16) for i in range(2)]
        Z.Ps = [sb("p%d_%d" % (d, i), [64, 8, 64], BF16) for i in range(2)]
        Z.Ts = [sb("tt%d_%d" % (d, i), [64, 8, 64], BF16) for i in range(2)]
        Z.XL = sb("xl%d" % d, [64, 8, 64], BF16)
        Z.UL = sb("ul%d" % d, [64, 16, 64], BF16)
        Z.AqP = sb("aqp%d" % d, [128, 8, 64], BF16)
        Z.PC = sb("pc%d" % d, [128, 8])
        Z.ST32 = sb("st32_%d" % d, [128, 64])
        Z.STb = sb("stb%d" % d, [128, 64], BF16)
        Z.T = temps[d]
        Z.s0, Z.s1, Z.s2 = PB[3 * d], PB[3 * d + 1], PB[3 * d + 2]
        Z.PTR = PTRh[d]
        STR.append(Z)
    X32, B_X32 = STR[1].Y
    MEMSET("dve", VP[:], 0.0, [B_VP])
    for which, src_d in ((0, X.dlb_d), (1, X.ilb_d)):
        f_ = X.t32big[:, which * 1024:(which + 1) * 1024]
        Bs_ = [T32[which * 2][1], T32[which * 2 + 1][1]]
        S.op("dve", lambda e, a=T32[which * 2 + 1][0][:, 0:1]: e.memset(a, 0.0), [], [Bs_[1]])
        S.dma("sp", f_, src_d[l], [Bs_[1]], Bs_[0])
        CP("act", LW[:, which, :], f_, Bs_, [B_lw])
    if l > 0:
        f_ = X.t32big[0:32, 4 * 512:6 * 512]
        Bs_ = [T32[4][1], T32[5][1]]
        S.op("dve", lambda e, a=T32[5][0][:, 0:1]: e.memset(a, 0.0), [], [Bs_[1]])
        S.dma("sp", f_, X.vlb_d, [Bs_[1]], Bs_[0])
        CP("act", VLB[:], f_, Bs_, [B_vlb])

    mq = [cst[0:64, CS["mq0"]:CS["mq0"] + 64], cst[0:64, CS["mq1"]:CS["mq1"] + 64]]
    mgm = [cst[:, CS["mg0"]:CS["mg0"] + 128], cst[:, CS["mg1"]:CS["mg1"] + 128]]
    rst = cst[:, CS["rst"]:CS["rst"] + 512]
    id64 = cst[0:64, CS["ident"]:CS["ident"] + 64]

    def shift_mix(j, src, dst, Bs, Bd, mu_col, ommu_col):
        def mix(dsl_d, sl_from, sl_self, bnd_d, bnd_s):
            TT(dsl_d, sl_from, sl_self, ALU.subtract, [Bs], [Bd])
            STT(dsl_d, dsl_d, mu_col, sl_self, ALU.mult, ALU.add, [Bs, Bd, B_pv], [Bd])
            TS(bnd_d, bnd_s, ommu_col, None, ALU.mult, None, [Bs, B_pv], [Bd])
        if j < 4:
            mix(dst[:, 1:NCTX], src[:, 0:NCTX - 1], src[:, 1:NCTX], dst[:, 0:1], src[:, 0:1])
        else:
            mix(dst[:, 0:NCTX - 1], src[:, 1:NCTX], src[:, 0:NCTX - 1], dst[:, NCTX - 1:NCTX], src[:, NCTX - 1:NCTX])
        s3 = src[:, NCTX:NT].rearrange("p (r c) -> p r c", c=64)
        d3 = dst[:, NCTX:NT].rearrange("p (r c) -> p r c", c=64)
        q = j // 2
        if q == 0:
            mix(d3[:, :, 1:64], s3[:, :, 0:63], s3[:, :, 1:64], d3[:, :, 0:1], s3[:, :, 0:1])
        elif q == 1:
            mix(d3[:, :, 0:63], s3[:, :, 1:64], s3[:, :, 0:63], d3[:, :, 63:64], s3[:, :, 63:64])
        elif q == 2:
            mix(d3[:, 1:32, :], s3[:, 0:31, :], s3[:, 1:32, :], d3[:, 0:1, :], s3[:, 0:1, :])
        else:
            mix(d3[:, 0:31, :], s3[:, 1:32, :], s3[:, 0:31, :], d3[:, 31:32, :], s3[:, 31:32, :])

    def c3(ap2):
        return ap2.rearrange("p (c t) -> p c t", t=64)

    def rr(gens):
        gens = list(gens)
        while gens:
            for g in list(gens):
                try:
                    next(g)
                except StopIteration:
                    gens.remove(g)

    (RS, B_RS), (KS, B_KS), (VS, B_VS) = XS
    X32b, B_X32b = STR[0].Y

    def sweep(j, Z):
        d = Z.d
        (KB, B_KB), (Yd, B_Yd), (UV, B_UV) = Z.KB, Z.Y, Z.UV
        (AR, B_AR), (BK, B_BK), (BKp, B_BKp) = Z.AR, Z.BK, Z.BKp
        (BKpT, B_BKpT), (AqT, B_AqT), (GL, B_GL), (GR, B_GR) = Z.BKpT, Z.AqT, Z.GL, Z.GR
        Qs, Ps, Ts = Z.Qs, Z.Ps, Z.Ts
        (XL, B_XL), (UL, B_UL), (AqP, B_AqP), (PC, B_PC) = Z.XL, Z.UL, Z.AqP, Z.PC
        (ST32, B_ST32), (STb, B_STb) = Z.ST32, Z.STb
        (P0, B_P0), (P1, B_P1), (P2, B_P2) = Z.s0, Z.s1, Z.s2
        PTRd, B_PTRd = Z.PTR
        lw = LW[:, :, j * 128:(j + 1) * 128]
        order = [0, 1, 2, 3, 4] if d == 0 else [0, 4, 3, 2, 1]
        MEMSET("dve", ST32[:], 0.0, [B_ST32])
        MEMSET("dve", STb[:], 0.0, [B_STb])
        dsl = slice(d * 64, (d + 1) * 64)
        for ti in order:
            t0, n = TILES[ti]
            nch = n // 64
            c0 = t0 // 64
            (A32, B_A32), (SIG, B_SIG), (LS, B_LS), (EA, B_EA), (EB, B_EB), (KD, B_KD), (KA, B_KA), (TMP, B_TMP) = Z.T
            MM(P0[:, 0:n], lw[dsl, 1, :], alrT[dsl, t0:t0 + n], [B_lw, B_alrT], [B_P0])
            ACT(A32[:, 0:n], P0[:, 0:n], AF.Sigmoid, [B_P0, B_pv], [B_A32], bias=pcol(l, "a0", d * 8 + j))
            MM(P0[:, 0:n], lw[dsl, 0, :], dlrT[dsl, t0:t0 + n], [B_lw, B_dlrT], [B_P0])
            ACT(SIG[:, 0:n], P0[:, 0:n], AF.Sigmoid, [B_P0, B_pv], [B_SIG], bias=pcol(l, "w0", d * 8 + j))
            if 'a' not in X.noy:
                yield
            SCAN(LS[:, 0:n], rst[:, 0:n], SIG[:, 0:n], [B_cst, B_SIG], [B_LS])
            L3 = c3(LS[:, 0:n])
            S3 = c3(SIG[:, 0:n])
            T3 = c3(TMP[:, 0:n])
            if d == 1:
                TT(T3, L3[:, :, 63:64].to_broadcast([128, nch, 64]), L3, ALU.subtract, [B_LS], [B_TMP])
                TT(L3, T3, S3, ALU.add, [B_TMP, B_SIG], [B_LS])
                endc = 0
            else:
                endc = 63
            if 'a' not in X.noy:
                yield
            ACT(KD[:, 0:n], A32[:, 0:n], AF.Identity, [B_A32, B_pv], [B_KD], scale=pcol(l, "ka", d * 8 + j), bias=pcol(l, "omka", d * 8 + j))
            TT(KD[:, 0:n], KD[:, 0:n], KS[:, t0:t0 + n], ALU.mult, [B_KD, B_KS], [B_KD])
            TT(KA[:, 0:n], KK[:, t0:t0 + n], A32[:, 0:n], ALU.mult, [B_KK, B_A32], [B_KA])
            ACT(KB[:, t0:t0 + n], KD[:, 0:n], AF.Identity, [B_KD, B_pv], [B_KB], scale=pcol(l, "brk", d * 8 + j))
            if 'a' not in X.noy:
                yield
            TT(TMP[:, 0:n], LS[:, 0:n], SIG[:, 0:n], ALU.subtract, [B_LS, B_SIG], [B_TMP])
            ACT(EA[:, 0:n], TMP[:, 0:n], AF.Exp, [B_TMP], [B_EA], scale=-C0)
            ACT(EB[:, 0:n], LS[:, 0:n], AF.Exp, [B_LS], [B_EB], scale=-C0)
            STT(AR[:, 0:nch, 0, :], c3(KK[:, t0:t0 + n]), -1.0, c3(EA[:, 0:n]), ALU.mult, ALU.mult, [B_KK, B_EA], [B_AR])
            if 'a' not in X.noy:
                yield
            TT(AR[:, 0:nch, 1, :], c3(RS[:, t0:t0 + n]), c3(EB[:, 0:n]), ALU.mult, [B_RS, B_EB], [B_AR])
            CP("act", PC[:, 0:nch], c3(EB[:, 0:n])[:, :, endc:endc + 1].rearrange("p c o -> p (c o)"), [B_EB], [B_PC])
            ACT(EA[:, 0:n], LS[:, 0:n], AF.Exp, [B_LS], [B_EA], scale=C0)
            if 'a' not in X.noy:
                yield
            TT(BK[:, 0:nch, 0, :], c3(KA[:, 0:n]), c3(EA[:, 0:n]), ALU.mult, [B_KA, B_EA], [B_BK])
            TT(BK[:, 0:nch, 1, :], c3(KD[:, 0:n]), c3(EA[:, 0:n]), ALU.mult, [B_KD, B_EA], [B_BK])
            if 'a' not in X.noy:
                yield
            TT(BKp[:, 0:nch, :, :].rearrange("p c a t -> p c (a t)"), BK[:, 0:nch, :, :].rearrange("p c a t -> p c (a t)"),
               PC[:, 0:nch].unsqueeze(2).to_broadcast([128, nch, 128]), ALU.mult, [B_BK, B_PC], [B_BKp])
            if 'a' not in X.noy:
                yield
            for r0 in range(0, nch, 4):
                for c in range(r0, r0 + 4):
                    TR(PTRd[:, (c - r0) * 128:(c - r0 + 1) * 128], BKp[:, c, :, :].rearrange("p a t -> p (a t)"), identb[:],
                       [B_BKp, B_identb], [B_PTRd])
                CP("act", BKpT[:, r0:r0 + 4, :], PTRd[:, 0:512].rearrange("p (c f) -> p c f", f=128), [B_PTRd], [B_BKpT])
                if 'b' not in X.noy:
                    yield
                for c in range(r0, r0 + 4):
                    TR(PTRd[0:64, (c - r0) * 128:(c - r0 + 1) * 128], AR[:, c, 0, :], identb[:], [B_AR, B_identb], [B_PTRd])
                CP("act", AqT[:, r0:r0 + 4, :], PTRd[0:64, 0:512].rearrange("p (c f) -> p c f", f=128), [B_PTRd], [B_AqT])
                if 'b' not in X.noy:
                    yield
            for g0 in range(0, nch, 4):
                units = [(g0 + cl, h) for h in range(2) for cl in range(4)]
                for u, (c, h) in enumerate(units):
                    hs = slice(h * 64, (h + 1) * 64)
                    PGt, B_PGt = (P1, B_P1) if h == 0 else (P2, B_P2)
                    uo = (u % 4) * 128
                    MM(PGt[:, uo:uo + 128], BK[hs, c, :, :].rearrange("p a t -> p (a t)"), AR[hs, c, :, :].rearrange("p a t -> p (a t)"),
                       [B_BK, B_AR], [B_PGt])
                if 'c' not in X.noy:
                    yield
                for half, (PGt, B_PGt) in enumerate(((P1, B_P1), (P2, B_P2))):
                    p4 = PGt[:, :].rearrange("p (u f) -> p u f", f=128)
                    mb = mgm[d].unsqueeze(1).to_broadcast([128, 4, 128])
                    TT(GL[:, half * 4:half * 4 + 4, :], p4[:, :, 0:64], mb[:, :, 0:64], ALU.mult, [B_PGt, B_cst], [B_GL])
                    TT(GR[:, g0 * 2 + half * 4:g0 * 2 + half * 4 + 4, :], p4[:, :, 64:128], mb[:, :, 64:128], ALU.mult, [B_PGt, B_cst], [B_GR])
                for u, (c, h) in enumerate(units):
                    hs = slice(h * 64, (h + 1) * 64)
                    PQh, B_PQh = (P0, B_P0) if h == 0 else (P1, B_P1)
                    MM(PQh[0:64, (u % 4) * 64:(u % 4 + 1) * 64], AR[hs, c, 0, :], BK[hs, c, 0, :], [B_AR, B_BK], [B_PQh])
                if 'c' not in X.noy:
                    yield
                Q0, B_Q0 = Qs[0]
                TT(Q0[:, 0:4, :], P0[0:64, 0:256].rearrange("p (u f) -> p u f", f=64), mq[d].unsqueeze(1).to_broadcast([64, 4, 64]), ALU.mult,
                   [B_P0, B_cst], [B_Q0])
                TT(Q0[:, 4:8, :], P1[0:64, 0:256].rearrange("p (u f) -> p u f", f=64), mq[d].unsqueeze(1).to_broadcast([64, 4, 64]), ALU.mult,
                   [B_P1, B_cst], [B_Q0])
                T0, B_T0 = Ts[0]
                TT(T0[:], GL[0:64, :, :], id64.unsqueeze(1).to_broadcast([64, 8, 64]), ALU.add, [B_GL, B_cst], [B_T0])
                if 'c' not in X.noy:
                    yield
                Pprev, B_Pprev = GL[0:64, :, :], B_GL
                Qprev, B_Qprev = Q0[:], B_Q0
                Tprev, B_Tprev = T0[:], B_T0
                for lvl in range(1, 6):
                    Qn, B_Qn = Qs[lvl % 2]
                    Pn, B_Pn = Ps[lvl % 2]
                    Tn, B_Tn = Ts[lvl % 2]
                    for u in range(8):
                        MM(P0[0:64, u * 64:(u + 1) * 64], Pprev[:, u, :], Qprev[:, u, :], [B_Pprev, B_Qprev], [B_P0])
                    if lvl < 5:
                        for u in range(8):
                            MM(P1[0:64, u * 64:(u + 1) * 64], Qprev[:, u, :], Pprev[:, u, :], [B_Pprev, B_Qprev], [B_P1])
                    if 'c' not in X.noy:
                        yield
                    CP("act", Qn[:], P0[0:64, :].rearrange("p (u f) -> p u f", f=64), [B_P0], [B_Qn])
                    if lvl < 5:
                        CP("act", Pn[:], P1[0:64, :].rearrange("p (u f) -> p u f", f=64), [B_P1], [B_Pn])
                    if 'c' not in X.noy:
                        yield
                    for u in range(8):
                        MM(P2[0:64, u * 64:(u + 1) * 64], Qn[:, u, :], Tprev[:, u, :], [B_Qn, B_Tprev], [B_P2])
                    TT(Tn[:], P2[0:64, :].rearrange("p (u f) -> p u f", f=64), Tprev, ALU.add, [B_P2, B_Tprev], [B_Tn])
                    if 'c' not in X.noy:
                        yield
                    if lvl < 5:
                        Pprev, B_Pprev = Pn[:], B_Pn
                    Qprev, B_Qprev = Qn[:], B_Qn
                    Tprev, B_Tprev = Tn[:], B_Tn
                Tf, B_Tf = Tprev, B_Tprev
                for u, (c, h) in enumerate(units):
                    MM(P1[h * 64:(h + 1) * 64, (u % 4) * 64:(u % 4) * 64 + 64], AqT[:, c, h * 64:(h + 1) * 64], Tf[:, u, :],
                       [B_AqT, B_Tf], [B_P1])
                for u, (c, h) in enumerate(units):
                    MM(P0[0:64, u * 64:(u + 1) * 64], GL[64:128, u, :], UV[64:128, c0 + c, h, :], [B_GL, B_UV], [B_P0])
                if 'c' not in X.noy:
                    yield
                CP("act", AqP[:, g0:g0 + 4, :], P1[:, 0:256].rearrange("p (c f) -> p c f", f=64), [B_P1], [B_AqP])
                CP("act", XL[:], P0[0:64, :].rearrange("p (u f) -> p u f", f=64), [B_P0], [B_XL])
                if 'c' not in X.noy:
                    yield
                for u in range(8):
                    MM(P2[0:64, u * 64:(u + 1) * 64], Tf[:, u, :], XL[:, u, :], [B_Tf, B_XL], [B_P2])
                CP("act", UL[:, g0 * 2:g0 * 2 + 8, :], P2[0:64, :].rearrange("p (u f) -> p u f", f=64), [B_P2], [B_UL])
                if 'c' not in X.noy:
                    yield
            corder = list(range(nch)) if d == 0 else list(range(nch - 1, -1, -1))
            for c in corder:
                cg = c0 + c

                def gi_(h_):
                    return (c // 4) * 8 + h_ * 4 + (c % 4)
                PYs = ((P0, B_P0), (P2, B_P2))
                for h in range(2):
                    hs = slice(h * 64, (h + 1) * 64)
                    MM(PYs[h][0][hs, c * 64:(c + 1) * 64], STb[hs, :], AR[hs, c, 1, :], [B_STb, B_AR], [PYs[h][1]], start=True, stop=False)
                hs0, hs1 = slice(0, 64), slice(64, 128)
                MM(P1[0:64, 0:64], AqP[hs0, c, :], STb[hs0, :], [B_AqP, B_STb], [B_P1])
                MM(P2[0:64, 0:64], AqP[hs1, c, :], STb[hs1, :], [B_AqP, B_STb], [B_P2])
                if 'e' not in X.noy:
                    yield
                TT(UV[0:64, cg, 0, :], P1[0:64, 0:64], UL[:, gi_(0), :], ALU.add, [B_P1, B_UL], [B_UV])
                TT(UV[0:64, cg, 1, :], P2[0:64, 0:64], UL[:, gi_(1), :], ALU.add, [B_P2, B_UL], [B_UV])
                if 'e' not in X.noy:
                    yield
                for h in range(2):
                    hs = slice(h * 64, (h + 1) * 64)
                    MM(P1[hs, 64:128], BKpT[:, c, hs], UV[:, cg, h, :], [B_BKpT, B_UV], [B_P1])
                for h in range(2):
                    hs = slice(h * 64, (h + 1) * 64)
                    MM(PYs[h][0][hs, c * 64:(c + 1) * 64], UV[:, cg, h, :], GR[:, gi_(h), :], [B_UV, B_GR], [PYs[h][1]], start=False, stop=True)
                if 'e' not in X.noy:
                    yield
                STT(STb[:], ST32[:], PC[:, c:c + 1], P1[:, 64:128], ALU.mult, ALU.add, [B_ST32, B_PC, B_P1], [B_STb])
                STT(ST32[:], ST32[:], PC[:, c:c + 1], P1[:, 64:128], ALU.mult, ALU.add, [B_ST32, B_PC, B_P1], [B_ST32])
                if 'e' not in X.noy:
                    yield
            CP("act", Yd[0:64, t0:t0 + n], P0[0:64, 0:n], [B_P0], [B_Yd])
            CP("act", Yd[64:128, t0:t0 + n], P2[64:128, 0:n], [B_P2], [B_Yd])
            if 'a' not in X.noy:
                yield

    for j in range(X.nj):
        S.dma("sp", SGa[:], X.SG_d[j], [X.B_SG[j]], B_SGa)
        vlb = VLB[:, j * 128:(j + 1) * 128]
        lbufs = [(X32, B_X32), (X32b, B_X32b), (X32, B_X32)]

        def ld(m):
            S.dma("sp", lbufs[m][0][:], X.U32_d[m * 8 + j], [X.B_U32[m * 8 + j]], lbufs[m][1])

        def sh(m):
            shift_mix(j, lbufs[m][0], XS[m][0], lbufs[m][1], XS[m][1], pcol(l, "mu", m * 8 + j), pcol(l, "ommu", m * 8 + j))
        ld(0)
        ld(1)
        sh(0)
        ld(2)
        sh(1)
        sh(2)
        if l == 0:
            S.dma("sp", X.VF_d[j], VS[:], [B_VS], X.B_VF[j])
        else:
            S.dma("sp", X32[:], X.VF_d[j], [X.B_VF[j]], B_X32)

            def vres_chain(i, t0, n):
                pb_, Bpb_ = PB[i]
                g_, Bg = T32[2 * i]
                d_, Bd_ = T32[2 * i + 1]
                MM(pb_[0:128, 0:n], vlb, vlT[:, t0:t0 + n], [B_vlb, B_vlT], [Bpb_])
                yield
                ACT(g_[:, 0:n], pb_[:, 0:n], AF.Sigmoid, [Bpb_, B_pv], [Bg], bias=pcol(l, "v0", j))
                TT(d_[:, 0:n], X32[:, t0:t0 + n], VS[:, t0:t0 + n], ALU.subtract, [B_X32, B_VS], [Bd_])
                yield
                TT(d_[:, 0:n], d_[:, 0:n], g_[:, 0:n], ALU.mult, [Bd_, Bg], [Bd_])
                yield
                TT(VS[:, t0:t0 + n], VS[:, t0:t0 + n], d_[:, 0:n], ALU.add, [B_VS, Bd_], [B_VSt[i]])
                yield
            B_VSt = [Buf("vs_t%d_%d_%d" % (l, j, i)) for i in range(5)]
            rr([vres_chain(i, t0, n) for i, (t0, n) in enumerate(TILES)])
            S.op("dve", lambda e, a=T32[11][0][:, 0:1]: e.memset(a, 0.0), B_VSt, [T32[11][1], B_VS])

        def kk_chain(i, t0, n):
            pb_, Bpb_ = PB[i]
            kr, Bkr = T32[3 * i]
            nr, Bnr = T32[3 * i + 1]
            sq_, Bsq = T32[3 * i + 2]
            ACT(kr[:, 0:n], KS[:, t0:t0 + n], AF.Identity, [B_KS, B_pv], [Bkr], scale=pcol(l, "kk", j))
            yield
            TT(sq_[:, 0:n], kr[:, 0:n], kr[:, 0:n], ALU.mult, [Bkr], [Bsq])
            yield
            MM(pb_[:, 0:n], cst[:, CS["ob"]:CS["ob"] + 128], sq_[:, 0:n], [B_cst, Bsq], [Bpb_])
            yield
            ACT(nr[:, 0:n], pb_[:, 0:n], AF.Sqrt, [Bpb_], [Bnr])
            yield
            TS(nr[:, 0:n], nr[:, 0:n], 1e-12, None, ALU.max, None, [Bnr], [Bnr])
            RECIP(nr[:, 0:n], nr[:, 0:n], [Bnr], [Bnr])
            yield
            TT(KK[:, t0:t0 + n], kr[:, 0:n], nr[:, 0:n], ALU.mult, [Bkr, Bnr], [B_KKt[i]])
            yield
        B_KKt = [Buf("kk_t%d_%d_%d" % (l, j, i)) for i in range(5)]
        if 'K' in X.skip:
            for i, (t0, n) in enumerate(TILES[0:4]):
                rr([kk_chain(i, t0, n)])
        else:
            rr([kk_chain(i, t0, n) for i, (t0, n) in enumerate(TILES[0:4])])
        rr([kk_chain(0, *TILES[4])])
        S.op("dve", lambda e, a=T32[11][0][:, 1:2]: e.memset(a, 0.0), B_KKt, [T32[11][1], B_KK])
        Bp01 = [PTRh[0][1], PTRh[1][1]]
        for (t0, n) in TILES:
            nch = n // 64
            c0 = t0 // 64
            CP("act", VP[:, 0:nch, 1, :], c3(VS[:, t0:t0 + n]), [B_VS], [B_VP])
            for c in range(nch):
                TR(PTRt[:, c * 128:(c + 1) * 128], VP[:, c, :, :].rearrange("p a t -> p (a t)"), identb[:], [B_VP, B_identb], Bp01[0:1])
            for Z in STR:
                CP("dve", Z.UV[0][64:128, c0:c0 + nch, :, :].rearrange("p c h v -> p c (h v)"),
                   PTRt[64:128, 0:nch * 128].rearrange("p (c f) -> p c f", f=128), Bp01[0:1], [Z.UV[1]])
        if X.p3stop == 3:
            raise _Stop()
        gens = [sweep(j, STR[0]), sweep(j, STR[1])]
        if 'Q' in X.skip:
            for g in gens:
                for _ in g:
                    pass
            gens = []
        while gens:
            for g in list(gens):
                try:
                    next(g)
                except StopIteration:
                    gens.remove(g)
        (Y0, B_Y0), (Y1, B_Y1) = STR[0].Y, STR[1].Y
        (KB0, B_KB0), (KB1, B_KB1) = STR[0].KB, STR[1].KB

        def fin_chain(k, ti, t0, n):
            (YS, B_YS), (YC, B_YC), (BON, B_BON), (KBS, B_KBS) = T32[4 * k:4 * k + 4]
            SQf, B_SQf = KBS, B_KBS
            pb_, Bpb_ = PB[k]
            TT(YS[:, 0:n], Y0[:, t0:t0 + n], Y1[:, t0:t0 + n], ALU.add, [B_Y0, B_Y1], [B_YS])
            TT(KBS[:, 0:n], KB0[:, t0:t0 + n], KB1[:, t0:t0 + n], ALU.add, [B_KB0, B_KB1], [B_KBS])
            yield
            rkb, B_rkb = RKBs[k]
            TT(rkb[:, 0:n], RS[:, t0:t0 + n], KBS[:, 0:n], ALU.mult, [B_RS, B_KBS], [B_rkb])
            yield
            MM(pb_[:, 0:n], obb[:], rkb[:, 0:n], [B_obb, B_rkb], [Bpb_])
            yield
            TT(BON[:, 0:n], pb_[:, 0:n], VS[:, t0:t0 + n], ALU.mult, [Bpb_, B_VS], [B_BON])
            MM(pb_[:, 0:n], ob64[:], YS[:, 0:n], [B_ob64, B_YS], [Bpb_])
            yield
            TT(YC[:, 0:n], YS[:, 0:n], pb_[:, 0:n], ALU.subtract, [B_YS, Bpb_], [B_YC])
            yield
            TT(SQf[:, 0:n], YC[:, 0:n], YC[:, 0:n], ALU.mult, [B_YC], [B_SQf])
            yield
            MM(pb_[:, 0:n], ob64[:], SQf[:, 0:n], [B_ob64, B_SQf], [Bpb_])
            yield
            TS(YS[:, 0:n], pb_[:, 0:n], GN_EPS, None, ALU.add, None, [Bpb_], [B_YS])
            yield
            ACT(YS[:, 0:n], YS[:, 0:n], AF.Sqrt, [B_YS], [B_YS])
            yield
            RECIP(YS[:, 0:n], YS[:, 0:n], [B_YS], [B_YS])
            yield
            TT(YC[:, 0:n], YC[:, 0:n], YS[:, 0:n], ALU.mult, [B_YC, B_YS], [B_YC])
            yield
            ACT(YC[:, 0:n], YC[:, 0:n], AF.Identity, [B_YC, B_pv], [B_YC], scale=pcol(l, "gnw", j), bias=pcol(l, "gnb", j))
            yield
            TT(YC[:, 0:n], YC[:, 0:n], BON[:, 0:n], ALU.add, [B_YC, B_BON], [B_YC])
            yield
            ygt, B_ygt = X.ygst[(j * 5 + ti) % 2]
            TT(ygt[:, 0:n], YC[:, 0:n], SGa[:, t0:t0 + n], ALU.mult, [B_YC, B_SGa], [B_ygt])
            S.dma("sp", X.YG_d[j][:, t0:t0 + n], ygt[:, 0:n], [B_ygt], X.B_YG)
            yield
        ftiles = [(ti, t0, n) for ti, (t0, n) in enumerate(TILES) if not (last and ti == 0)]
        NF = 3
        for i0 in range(0, len(ftiles), NF):
            if 'F' in X.skip:
                for k in range(min(NF, len(ftiles) - i0)):
                    rr([fin_chain(k, *ftiles[i0 + k])])
            else:
                rr([fin_chain(k, *ftiles[i0 + k]) for k in range(min(NF, len(ftiles) - i0))])


def phase4(X, stk):
    S, l, last = X.S, X.l, X.last
    MM, TT, STT, MEMSET = X.MM, X.TT, X.STT, X.MEMSET
    PB, cst, B_cst, B_pv, pcol = X.PB, X.cst, X.B_cst, X.B_pv, X.pcol
    yb, B_yb = X.yb, X.B_yb
    T32 = X.T32

    def sb(name, shape, dt=F32):
        return X.sb(stk, "p4" + name, shape, dt)
    PPs = [sb("pp%d" % i, [128, NT + 32]) for i in range(2)]
    PW = [sb("w%d" % i, [128, 2, 128], BF16) for i in range(2)]
    PLD = [sb("pl%d" % i, [128, NT], BF16) for i in range(2)]
    SGb, B_SGb = sb("sgb", [128, NT], BF16)
    WA, B_WA = sb("wa", [128, 2080])
    WB, B_WB = sb("wb", [128, 2080])
    for t_, B_ in PPs:
        MEMSET("dve", t_[:], 0.0, [B_])
    wins = (2, 4, 8, 16)
    segs = [(8, NCTX, 0), (NCTX + 24, NLAT, NCTX)]
    for gi in range(4):
        w = wins[gi]
        for k2 in range(2):
            ptile = gi * 2 + k2
            pp, B_pp = PPs[k2]
            pl, B_pl = PLD[k2]
            for (po, ln, to) in segs:
                S.dma("sp", pp[:, po:po + ln], X.U32_d[24 + ptile][:, to:to + ln], [X.B_U32[24 + ptile]], B_pp)
            for (po, ln, to) in segs:
                for s0 in range(0, ln, 2048):
                    n = min(2048, ln - s0)
                    base = po + s0
                    cur, B_cur, nxt, B_nxt = WA, B_WA, WB, B_WB
                    TT(cur[:, 1:n + 15], pp[:, base - 8:base + n + 6], pp[:, base - 7:base + n + 7], ALU.add, [B_pp], [B_cur])
                    ww, lo, hi = 2, 1, n + 15
                    while ww < w:
                        hf = ww // 2
                        lo2, hi2 = lo + hf, hi - hf
                        TT(nxt[:, lo2:hi2], cur[:, lo2 - hf:hi2 - hf], cur[:, lo2 + hf:hi2 + hf], ALU.add, [B_cur], [B_nxt])
                        cur, B_cur, nxt, B_nxt = nxt, B_nxt, cur, B_cur
                        lo, hi = lo2, hi2
                        ww *= 2
                    STT(pl[:, to + s0:to + s0 + n], cur[:, 8:8 + n], 1.0 / w, pp[:, base:base + n], ALU.mult, ALU.subtract, [B_cur, B_pp], [B_pl])
                    tm, B_tm = T32[2]
                    if s0 == 0:
                        cc = cst[:, CS["pcor"] + gi * 16:CS["pcor"] + gi * 16 + 8]
                        TT(tm[:, 0:8], cur[:, 8:16], cc, ALU.mult, [B_cur, B_cst], [B_tm])
                        TT(pl[:, to:to + 8], tm[:, 0:8], pp[:, base:base + 8], ALU.subtract, [B_tm, B_pp], [B_pl])
                    if s0 + n == ln:
                        cc = cst[:, CS["pcor"] + gi * 16 + 8:CS["pcor"] + gi * 16 + 16]
                        TT(tm[:, 8:16], cur[:, n:n + 8], cc, ALU.mult, [B_cur, B_cst], [B_tm])
                        TT(pl[:, to + ln - 8:to + ln], tm[:, 8:16], pp[:, base + n - 8:base + n], ALU.subtract, [B_tm, B_pp], [B_pl])
        for k2 in range(2):
            wt, Bw = PW[k2]
            f_, Bf_ = T32[4 + k2]
            S.dma("sp", f_[:, 0:256], X.pw_d[l][gi][k2 * 128:(k2 + 1) * 128, :], [], Bf_)
            X.CP("act", wt[:], f_[:, 0:256].rearrange("p (o c) -> p o c", c=128), [Bf_], [Bw])
        for o2 in range(2):
            otile = gi * 2 + o2
            S.dma("sp", SGb[:], X.SG_d[8 + otile], [X.B_SG[8 + otile]], B_SGb)
            for ti, (t0, n) in enumerate(TILES):
                if last and ti == 0:
                    continue
                pt, Bp = PB[(o2 + ti) % 2]
                for k2 in range(2):
                    MM(pt[:, 0:n], PW[k2][0][:, o2, :], PLD[k2][0][:, t0:t0 + n], [PW[k2][1], PLD[k2][1]], [Bp], start=(k2 == 0), stop=(k2 == 1))
                STT(yb[:, otile, t0:t0 + n], pt[:, 0:n], pcol(l, "psc", otile), SGb[:, t0:t0 + n], ALU.mult, ALU.mult, [Bp, B_pv, B_SGb], [B_yb])


def phase5a(X, stk):
    S, l, last = X.S, X.l, X.last
    MM, TT = X.MM, X.TT
    PB = X.PB
    yg, B_yg, yb, B_yb = X.yg, X.B_yg, X.yb, X.B_yb

    def sb(name, shape, dt=F32):
        return X.sb(stk, "p5a" + name, shape, dt)
    MGT = [sb("mg%d" % i, [128, 2, NT], BF16) for i in range(2)]
    MST = [sb("ms%d" % i, [128, NT], BF16) for i in range(2)]
    YA = [X.T32[0], X.T32[1]]
    tiles = [(ti, t0, n) for ti, (t0, n) in enumerate(TILES) if not (last and ti == 0)]
    tlo = tiles[0][1]
    rr = 0
    for fb in range(4):
        wab, Bwab = X.WS[fb % 2]
        S.dma("pool", wab[:, 0:8, :], X.woa_d[l][:, fb * 512:(fb + 1) * 512].rearrange("(k p) c -> p k c", p=128), [], Bwab)
        S.dma("pool", wab[:, 8:16, :], X.wob_d[l][:, fb * 512:(fb + 1) * 512].rearrange("(k p) c -> p k c", p=128), [Bwab], Bwab)
        wa, Bwa = wab[:, 0:8, :], Bwab
        wb, Bwb = wab[:, 8:16, :], Bwab
        for q in range(4):
            f = fb * 4 + q
            mg, Bmg = MGT[f % 2]
            ms, Bms = MST[f % 2]
            S.dma("sp", mg[:, 0, tlo:NT], X.MG_d[f][:, tlo:NT], [X.B_MG[f]], Bmg)
            S.dma("sp", mg[:, 1, tlo:NT], X.MG_d[16 + f][:, tlo:NT], [X.B_MG[16 + f]], Bmg)
            for (ti, t0, n) in tiles:
                pa_, Bpa = PB[(rr * 2) % 6]
                pb_, Bpb = PB[(rr * 2 + 1) % 6]
                ya, B_ya = YA[rr % 2]
                rr += 1
                for k in range(8):
                    MM(pa_[:, 0:n], wa[:, k, q * 128:(q + 1) * 128], yg[:, k, t0:t0 + n], [Bwa, B_yg], [Bpa], start=(k == 0), stop=(k == 7))
                for k in range(8):
                    MM(pb_[:, 0:n], wb[:, k, q * 128:(q + 1) * 128], yb[:, k, t0:t0 + n], [Bwb, B_yb], [Bpb], start=(k == 0), stop=(k == 7))
                TT(ya[:, 0:n], pa_[:, 0:n], mg[:, 0, t0:t0 + n], ALU.mult, [Bpa, Bmg], [B_ya])
                TT(ms[:, t0:t0 + n], pb_[:, 0:n], mg[:, 1, t0:t0 + n], ALU.mult, [Bpb, Bmg], [Bms])
                TT(ms[:, t0:t0 + n], ms[:, t0:t0 + n], ya[:, 0:n], ALU.add, [Bms, B_ya], [Bms])
            S.dma("act", X.MER_d[f][:, tlo:NT], ms[:, tlo:NT], [Bms], X.B_MER)


def phase5b(X, stk):
    S, l, last = X.S, X.l, X.last
    MM, TR, ACT, SQACC, TT, TS, STT, RECIP = X.MM, X.TR, X.ACT, X.SQACC, X.TT, X.TS, X.STT, X.RECIP
    PB, cst, B_cst, mod, B_mod = X.PB, X.cst, X.B_cst, X.mod, X.B_mod
    ident = cst[:, CS["ident"]:CS["ident"] + 128]

    def sb(name, shape, dt=F32):
        return X.sb(stk, "p5b" + name, shape, dt)
    WO, B_WO = sb("wo", [128, 16, D], BF16)
    WOB = [Buf("wo_%d_%d" % (l, i)) for i in range(4)]
    MERs = [sb("mer%d" % i, [128, 16, 512], BF16) for i in range(2)]
    Hs = [sb("H%d" % i, [128, 4, D]) for i in range(2)]
    OG = [X.T32[0], X.T32[1]]
    ST5, B_ST5 = sb("st", [128, 8])
    for fb in range(4):
        S.dma("pool", WO[:, :, fb * 512:(fb + 1) * 512], X.wo_d[l][:, fb * 512:(fb + 1) * 512].rearrange("(k p) c -> p k c", p=128), [], WOB[fb])
    if last:
        FG = X.t32big[:, 3 * 512:7 * 512]
        FGB = [X.T32[i][1] for i in range(3, 7)]
        B_FG = FGB[0]
        for B_ in FGB[1:]:
            S.op("dve", lambda e, a=X.T32[7][0][:, 0:1]: e.memset(a, 0.0), [], [B_, X.T32[7][1]])
        S.dma("sp", FG, X.fg_d, [X.T32[7][1]], B_FG)
    src = X.xin if l == 0 else X.H1_d
    cnt = 0
    for ti, (t0, n) in enumerate(TILES):
        if last and ti == 0:
            continue
        nb = n // 128
        si = 1 if ti == 0 else 0
        rd = [] if l == 0 else [X.B_H1[ti]]
        MER, B_MER = MERs[cnt % 2]
        H, B_H = Hs[cnt % 2]
        cnt += 1
        S.dma("sp", MER[:, :, 0:n], X.MER_d[:, :, t0:t0 + n].rearrange("f p t -> p f t"), [X.B_MER], B_MER)
        S.dma("sp", H[:, 0:nb, :], src[t0:t0 + n, :].rearrange("(b p) f -> p b f", p=128), rd, B_H)
        def tail(f, og, Bog):
            ptt, Bptt = PB[3 + (f % 3)]
            for b in range(nb):
                TR(ptt[:, b * 128:(b + 1) * 128], og[:, b * 128:(b + 1) * 128], ident, [Bog, B_cst], [Bptt])
            TT(H[:, 0:nb, f * 128:(f + 1) * 128], H[:, 0:nb, f * 128:(f + 1) * 128],
               ptt[:, 0:nb * 128].rearrange("p (b f) -> p b f", f=128), ALU.add, [B_H, Bptt], [B_H])
        pend = None
        for f in range(16):
            po_, Bpo = PB[f % 3]
            for k in range(16):
                MM(po_[:, 0:n], WO[:, k, f * 128:(f + 1) * 128], MER[:, k, 0:n], [WOB[f // 4], B_MER], [Bpo], start=(k == 0), stop=(k == 15))
            og, Bog = OG[f % 2]
            ACT(og[:, 0:n], po_[:, 0:n], AF.Identity, [Bpo, B_mod], [Bog], scale=mod[:, 32 + f, si:si + 1])
            if pend is not None:
                tail(*pend)
            pend = (f, og, Bog)
        tail(*pend)
        if not last:
            S.dma("act", X.H1_d[t0:t0 + n, :].rearrange("(b p) f -> p b f", p=128), H[:, 0:nb, :], [B_H], X.B_H1[ti])
        else:
            for b in range(nb):
                SQACC(H[:, b, :], MER[:, 0:4, :].rearrange("p a t -> p (a t)"), ST5[:, b:b + 1], [B_H], [B_MER, B_ST5])
            TS(ST5[:, 0:nb], ST5[:, 0:nb], 1.0 / D, 1e-6, ALU.mult, ALU.add, [B_ST5], [B_ST5])
            ACT(ST5[:, 0:nb], ST5[:, 0:nb], AF.Sqrt, [B_ST5], [B_ST5])
            RECIP(ST5[:, 0:nb], ST5[:, 0:nb], [B_ST5], [B_ST5])
            for b in range(nb):
                STT(H[:, b, :], H[:, b, :], ST5[:, b:b + 1], FG, ALU.mult, ALU.mult, [B_H, B_ST5, B_FG], [B_H])
            lt0 = t0 - NCTX
            X.finals.append(S.dma("act", X.out_d[lt0:lt0 + n, :].rearrange("(b p) f -> p b f", p=128), H[:, 0:nb, :], [B_H], X.B_OUT[ti]))


def _pk(v):
    v = np.asarray(v, np.float32).reshape(-1, 128)
    return np.ascontiguousarray(v.T)


def _consts():
    cs = np.zeros((128, NCS), np.float32)
    cs[:, CS["ident"]:CS["ident"] + 128] = np.eye(128, dtype=np.float32)
    ob = np.zeros((128, 128), np.float32)
    ob[0:64, 0:64] = 1.0
    ob[64:128, 64:128] = 1.0
    cs[:, CS["ob"]:CS["ob"] + 128] = ob
    s = np.arange(64)[:, None]
    t = np.arange(64)[None, :]
    for d in range(2):
        strict = (s < t) if d == 0 else (s > t)
        incl = (s <= t) if d == 0 else (s >= t)
        m = np.zeros((128, 128), np.float32)
        m[0:64, 0:64] = strict
        m[64:128, 0:64] = strict
        m[0:64, 64:128] = incl
        m[64:128, 64:128] = incl
        cs[:, CS["mg%d" % d]:CS["mg%d" % d] + 128] = m
        cs[0:64, CS["mq%d" % d]:CS["mq%d" % d] + 64] = strict.T
    r = np.ones(512, np.float32)
    r[::64] = 0.0
    cs[:, CS["rst"]:CS["rst"] + 512] = r[None, :]
    for gi, w in enumerate((2, 4, 8, 16)):
        left = w // 2
        right = w - 1 - left
        for i in range(8):
            cnt_f = min(i + right, 10 ** 9) - max(i - left, 0) + 1
            cs[:, CS["pcor"] + gi * 16 + i] = 1.0 / cnt_f
            tb = -8 + i
            cnt_b = min(tb + right, -1) - (tb - left) + 1
            cs[:, CS["pcor"] + gi * 16 + 8 + i] = 1.0 / cnt_b
    return cs


def _pack_pv(inp):
    pvs = np.zeros((DEPTH, 128, NPV), np.float32)
    for l in range(DEPTH):
        def put(name, arr):
            a = _pk(arr)
            pvs[l, :, PV[name]:PV[name] + a.shape[1]] = a
        put("mu", inp["tok_mu"][l].reshape(-1))
        put("ommu", 1.0 - inp["tok_mu"][l].reshape(-1))
        put("w0", inp["decay_w0"][l].reshape(-1))
        put("a0", inp["iclr_a0"][l].reshape(-1))
        put("kk", inp["key_k"][l])
        put("ka", inp["key_a"][l].reshape(-1))
        put("omka", 1.0 - inp["key_a"][l].reshape(-1))
        put("brk", inp["bonus_rk"][l].reshape(-1))
        put("gnw", inp["gn_w"][l])
        put("gnb", inp["gn_b"][l])
        put("psc", inp["pool_scale"][l])
        put("ng", inp["norm_g"][l])
        put("adab", inp["ada_b"][l])
        if l > 0:
            put("v0", inp["vres_v0"][l - 1])
        put("fg", inp["final_g"])
    return pvs


_NC_CACHE = {}


def kernel(**inputs):
    inp = {k: np.asarray(v) for k, v in inputs.items()}
    nb = inp["x"].shape[0]
    if "nc" not in _NC_CACHE:
        _NC_CACHE["nc"] = build_program()
    nc = _NC_CACHE["nc"]
    cs = _consts()
    pvs = _pack_pv(inp)
    fgrep = np.ascontiguousarray(np.broadcast_to(inp["final_g"].astype(np.float32)[None, :], (128, D)))
    shared = {
        "ada_w": np.ascontiguousarray(inp["ada_w"], np.float32),
        "w_in": np.ascontiguousarray(inp["w_in"], np.float32),
        "dlb": np.ascontiguousarray(inp["decay_lora_b"].reshape(DEPTH, 128, 1024), np.float32),
        "ilb": np.ascontiguousarray(inp["iclr_lora_b"].reshape(DEPTH, 128, 1024), np.float32),
        "vla": np.ascontiguousarray(inp["vres_lora_a"][0], np.float32),
        "vlb": np.ascontiguousarray(inp["vres_lora_b"][0], np.float32),
        "pool_w": np.ascontiguousarray(inp["pool_w"], np.float32),
        "w_out_a": np.ascontiguousarray(inp["w_out_a"], np.float32),
        "w_out_b": np.ascontiguousarray(inp["w_out_b"], np.float32),
        "w_out": np.ascontiguousarray(inp["w_out"], np.float32),
        "pv": pvs,
        "cst": cs,
        "fgrep": fgrep,
    }
    in_maps = []
    for b in range(nb):
        m = dict(shared)
        m["xin"] = np.ascontiguousarray(np.concatenate([inp["ctx"][b], inp["x"][b]], axis=0), np.float32)
        cT = np.zeros((128, 32), np.float32)
        cT[:, 0::2] = _pk(inp["c"][b])
        cT[:, 1::2] = _pk(inp["c_ctx"])
        m["cT"] = cT
        in_maps.append(m)
    res = run_bass_kernel_spmd(nc, in_maps, core_ids=list(range(nb)))
    out = np.stack([np.asarray(r["out"], np.float32) for r in res.results], axis=0)
    return out
```

```python
import contextlib
import numpy as np
import concourse.bass as bass
import concourse.mybir as mybir
from concourse.bass_utils import run_bass_kernel_spmd

F32 = mybir.dt.float32
BF16 = mybir.dt.bfloat16
AF = mybir.ActivationFunctionType
ALU = mybir.AluOpType

import os as _os
SAME_ENGINE_KINDS = tuple(_os.environ.get('KSE', 'raw').split(','))
SEM_LIMIT = 30000

D = 2048
NT = 2304
NCTX = 256
NLAT = 2048
DEPTH = 2
INW = 10496
OFF_POOL = 3072
OFF_GA = 4096
OFF_GB = 5120
OFF_DEC = 6144
OFF_ICL = 6272
OFF_MG = 6400
C0 = float(np.exp(-0.5))
GN_EPS = 64e-5
TILES = [(0, 256), (256, 512), (768, 512), (1280, 512), (1792, 512)]
NCH = NT // 64

PV = {}
_o = 0
for _n, _w in [("mu", 24), ("w0", 16), ("a0", 16), ("kk", 8), ("ka", 16), ("brk", 16), ("gnw", 8), ("gnb", 8),
               ("psc", 8), ("ng", 16), ("adab", 48), ("v0", 8), ("fg", 16), ("omka", 16), ("ommu", 24)]:
    PV[_n] = _o
    _o += _w
NPV = _o
CS = {}
_o = 0
for _n, _w in [("ident", 128), ("ob", 128), ("mg0", 128), ("mg1", 128), ("mq0", 64), ("mq1", 64), ("rst", 512),
               ("pcor", 64)]:
    CS[_n] = _o
    _o += _w
NCS = _o


class Buf:
    __slots__ = ("name", "w", "r", "sem", "ndma")

    def __init__(self, name):
        self.name = name
        self.w = None
        self.r = []
        self.sem = None
        self.ndma = 0


class Op:
    __slots__ = ("eng", "fn", "deps", "signal", "count", "is_dma", "buf")

    def __init__(self, eng, fn, is_dma=False, buf=None):
        self.eng = eng
        self.fn = fn
        self.deps = []
        self.signal = False
        self.count = None
        self.is_dma = is_dma
        self.buf = buf


class Sched:
    ENGS = ("pe", "act", "dve", "pool", "sp")

    def __init__(self, nc):
        self.nc = nc
        self.ops = {e: [] for e in self.ENGS}
        self.dma_bufs = []
        self.bar_id = 0
        self.bar_deps = []
        self.eng_bar = {e: 0 for e in self.ENGS}
        self.dmas_since = []

    def barrier(self):
        X = []
        for e in self.ENGS:
            for o in reversed(self.ops[e]):
                if not o.is_dma:
                    X.append(o)
                    break
        lastd = {}
        for o in self.dmas_since:
            lastd[id(o.buf)] = o
        X.extend(lastd.values())
        self.dmas_since = []
        self.bar_deps = X
        self.bar_id += 1

    def _add(self, op, reads, writes):
        deps = []
        if self.eng_bar[op.eng] < self.bar_id:
            deps.extend((d, "raw") for d in self.bar_deps)
            self.eng_bar[op.eng] = self.bar_id
        for b in reads:
            if b.w is not None:
                deps.append((b.w, "raw"))
        for b in writes:
            if b.w is not None:
                deps.append((b.w, "waw"))
            deps.extend((r, "war") for r in b.r)
        seen = set()
        for d, kind in deps:
            if d is op:
                continue
            if (not d.is_dma) and (not op.is_dma) and d.eng == op.eng:
                if d.eng == "pe" or kind not in SAME_ENGINE_KINDS:
                    continue
            if id(d) in seen:
                continue
            seen.add(id(d))
            d.signal = True
            op.deps.append(d)
        for b in reads:
            b.r.append(op)
        for b in writes:
            b.w = op
            b.r = []
        self.ops[op.eng].append(op)
        return op

    def op(self, eng, fn, reads=(), writes=()):
        return self._add(Op(eng, fn), list(reads), list(writes))

    def dma(self, eng, out_ap, in_ap, reads, write):
        op = Op(eng, None, is_dma=True, buf=write)
        op.fn = lambda e, o=out_ap, i=in_ap: e.dma_start(out=o, in_=i)
        if write.sem is None:
            self.dma_bufs.append(write)
            write.sem = True
        self._add(op, list(reads), [write])
        write.ndma += 1
        op.count = 16 * write.ndma
        op.signal = True
        self.dmas_since.append(op)
        return op

    def emit(self, final_waits=()):
        nc = self.nc
        with contextlib.ExitStack() as st:
            for b in self.dma_bufs:
                b.sem = st.enter_context(nc.semaphore("d_" + b.name))
            esems = {}
            for e in self.ENGS:
                n = 0
                for o in self.ops[e]:
                    if o.is_dma:
                        continue
                    if o.signal:
                        n += 1
                        o.count = n
                nsem = max(0, n - 1) // SEM_LIMIT + 1
                esems[e] = [st.enter_context(nc.semaphore("e_%s%d" % (e, i))) for i in range(nsem)]

            def semval(o):
                if o.is_dma:
                    return o.buf.sem, o.count, ("d", id(o.buf))
                k = (o.count - 1) // SEM_LIMIT
                return esems[o.eng][k], o.count - k * SEM_LIMIT, ("e", o.eng, k)

            block = st.enter_context(nc.Block())

            def run(ename, eng):
                waited = {}
                for o in self.ops[ename]:
                    need = {}
                    for d in o.deps:
                        sem, val, key = semval(d)
                        if waited.get(key, 0) >= val:
                            continue
                        if key not in need or need[key][1] < val:
                            need[key] = (sem, val)
                    for key, (sem, val) in need.items():
                        eng.wait_ge(sem, val)
                        waited[key] = val
                    ins = o.fn(eng)
                    if o.signal:
                        if o.is_dma:
                            ins.then_inc(o.buf.sem, 16)
                        else:
                            k = (o.count - 1) // SEM_LIMIT
                            ins.then_inc(esems[ename][k], 1)
                if ename == "sp":
                    for o in final_waits:
                        sem, val, key = semval(o)
                        eng.wait_ge(sem, val)

            @block.tensor
            def _(eng):
                run("pe", eng)

            @block.scalar
            def _(eng):
                run("act", eng)

            @block.vector
            def _(eng):
                run("dve", eng)

            @block.gpsimd
            def _(eng):
                run("pool", eng)

            @block.sync
            def _(eng):
                run("sp", eng)


class Ctx:
    pass


class _Stop(Exception):
    pass


def build_program(depth=DEPTH, dbg=None):
    nc = bass.Bass("TRN2", target_bir_lowering=False)
    S = Sched(nc)
    dbg = dbg or {}
    X = Ctx()
    X.nc, X.S = nc, S
    import os
    X.stop_after = tuple(int(v) for v in os.environ['KSTOP'].split(',')) if os.environ.get('KSTOP') else None
    X.nj = int(os.environ.get('KNJ', '8'))
    X.skip = os.environ.get('KSKIP', '')
    X.p3stop = int(os.environ.get('KP3', '0'))
    X.noy = os.environ.get('KNOY', '')

    def din(name, shape, dt=F32):
        return nc.dram_tensor(name, list(shape), dt, kind="ExternalInput").ap()

    def dscr(name, shape, dt=F32):
        return nc.dram_tensor(name, list(shape), dt, kind="Internal").ap()

    X.xin = din("xin", [NT, D])
    cT_d = din("cT", [128, 32])
    X.ada_w = din("ada_w", [DEPTH, D, 3 * D])
    X.w_in = din("w_in", [DEPTH, D, INW])
    X.dlb_d = din("dlb", [DEPTH, 128, 1024])
    X.ilb_d = din("ilb", [DEPTH, 128, 1024])
    X.vla_d = din("vla", [D, 32])
    X.vlb_d = din("vlb", [32, 1024])
    X.pw_d = din("pool_w", [DEPTH, 4, 256, 256])
    X.woa_d = din("w_out_a", [DEPTH, 1024, D])
    X.wob_d = din("w_out_b", [DEPTH, 1024, D])
    X.wo_d = din("w_out", [DEPTH, D, D])
    pv_d = din("pv", [DEPTH, 128, NPV])
    cs_d = din("cst", [128, NCS])
    X.fg_d = din("fgrep", [128, D])
    X.out_d = nc.dram_tensor("out", [NLAT, D], F32, kind="ExternalOutput").ap()

    X.U32_d = dscr("U32", [32, 128, NT])
    X.SG_d = dscr("SG", [16, 128, NT], BF16)
    X.MG_d = dscr("MG", [32, 128, NT], BF16)
    X.VF_d = dscr("VF", [8, 128, NT])
    X.H1_d = dscr("H1", [NT, D])
    X.YG_d = dscr("YG", [8, 128, NT], BF16)
    X.MER_d = dscr("MER", [16, 128, NT], BF16)
    X.B_MER = Buf("MER")
    X.B_YG = Buf("YG")
    X.B_U32 = [Buf("U32")] * 32
    X.B_SG = [Buf("SG")] * 16
    X.B_MG = [Buf("MG")] * 32
    X.B_VF = [Buf("VF")] * 8
    X.B_H1 = [Buf("H1")] * 5
    X.B_OUT = [Buf("OUT")] * 5
    X.dbg_out = {}
    for k, shp in dbg.items():
        X.dbg_out[k] = (nc.dram_tensor("dbg_" + k, list(shp), F32, kind="ExternalOutput").ap(), Buf("dbg_" + k))
    X.finals = []

    def MM(out, lhsT, rhs, R, W, start=True, stop=True):
        S.op("pe", lambda e, o=out, l=lhsT, r=rhs, s=start, t=stop: e.matmul(o, lhsT=l, rhs=r, start=s, stop=t), R, W)

    def TR(out, in_, ident, R, W):
        S.op("pe", lambda e, o=out, i=in_, d=ident: e.transpose(o, i, d), R, W)

    def ACT(out, in_, func, R, W, bias=None, scale=None):
        kw = {}
        if bias is not None:
            kw["bias"] = bias
        if scale is not None:
            kw["scale"] = scale
        S.op("act", lambda e, o=out, i=in_, f=func, k=kw: e.activation(out=o, in_=i, func=f, **k), R, W)

    def SQACC(in_, junk, acc, R, W):
        S.op("act", lambda e, o=junk, i=in_, a=acc: e.activation(out=o, in_=i, func=AF.Square, accum_out=a), R, W)

    def CP(eng, out, in_, R, W):
        if eng == "act":
            S.op("act", lambda e, o=out, i=in_: e.copy(out=o, in_=i), R, W)
        else:
            S.op(eng, lambda e, o=out, i=in_: e.tensor_copy(out=o, in_=i), R, W)

    def TT(out, in0, in1, op, R, W, eng="dve"):
        S.op(eng, lambda e, o=out, a=in0, b=in1, p=op: e.tensor_tensor(out=o, in0=a, in1=b, op=p), R, W)

    def TS(out, in0, s1, s2, op0, op1, R, W, eng="dve"):
        if s2 is None:
            S.op(eng, lambda e, o=out, a=in0, x=s1, p=op0: e.tensor_scalar(out=o, in0=a, scalar1=x, scalar2=None, op0=p), R, W)
        else:
            S.op(eng, lambda e, o=out, a=in0, x=s1, y=s2, p=op0, q=op1: e.tensor_scalar(out=o, in0=a, scalar1=x, scalar2=y, op0=p, op1=q), R, W)

    def STT(out, in0, sc, in1, op0, op1, R, W, eng="dve"):
        S.op(eng, lambda e, o=out, a=in0, x=sc, b=in1, p=op0, q=op1: e.scalar_tensor_tensor(out=o, in0=a, scalar=x, in1=b, op0=p, op1=q), R, W)

    def RECIP(out, in_, R, W):
        S.op("dve", lambda e, o=out, i=in_: e.reciprocal(out=o, in_=i), R, W)

    def SCAN(out, d0, d1, R, W):
        S.op("dve", lambda e, o=out, a=d0, b=d1: e.tensor_tensor_scan(out=o, data0=a, data1=b, initial=0.0, op0=ALU.mult, op1=ALU.add), R, W)

    def MEMSET(eng, ap, val, W):
        S.op(eng, lambda e, a=ap, v=val: e.memset(a, v), [], W)

    X.MM, X.TR, X.ACT, X.SQACC, X.CP, X.TT, X.TS, X.STT, X.RECIP, X.SCAN, X.MEMSET = MM, TR, ACT, SQACC, CP, TT, TS, STT, RECIP, SCAN, MEMSET
    uid = [0]

    def sb(stack, name, shape, dt=F32):
        uid[0] += 1
        nm = "%s_%d" % (name, uid[0])
        t = stack.enter_context(nc.sbuf_tensor(nm, list(shape), dt))
        return t, Buf(nm)
    X.sb = sb

    def dbg_dump(key, ap, B):
        if key in X.dbg_out:
            o, Bo = X.dbg_out[key]
            X.finals.append(S.dma("sp", o, ap, [B], Bo))
    X.dbg_dump = dbg_dump

    with contextlib.ExitStack() as st:
        X.PB = []
        for i in range(6):
            t = st.enter_context(nc.psum_tensor("pb%d" % i, [128, 512], F32))
            X.PB.append((t, Buf("pb%d" % i)))
        X.PTRt = st.enter_context(nc.psum_tensor("ptr", [128, 1024], BF16))
        X.PTRt2 = st.enter_context(nc.psum_tensor("ptr2", [128, 1024], BF16))
        X.B_PTR = Buf("ptr")
        PB = X.PB

        X.cst, X.B_cst = sb(st, "cst", [128, NCS])
        X.pv, X.B_pv = sb(st, "pv", [128, DEPTH, NPV])
        X.identb, X.B_identb = sb(st, "identb", [128, 128], BF16)
        X.obb, X.B_obb = sb(st, "obb", [128, 128], BF16)
        X.ob64, X.B_ob64 = sb(st, "ob64", [128, 128])
        cT, B_cT = sb(st, "cTs", [128, 32])
        sc, B_sc = sb(st, "sc", [128, 16, 2], BF16)
        MODS = [sb(st, "mod%d" % i, [128, 48, 2]) for i in range(DEPTH)]
        GSCS = [sb(st, "gsc%d" % i, [128, 16, 2]) for i in range(DEPTH)]
        X.mod, X.B_mod = MODS[0]
        gsc, B_gsc = GSCS[0]
        X.dlrT, X.B_dlrT = sb(st, "dlrT", [128, NT], BF16)
        X.alrT, X.B_alrT = sb(st, "alrT", [128, NT], BF16)
        X.vlT, X.B_vlT = sb(st, "vlT", [32, NT], BF16)
        t32big, _ = sb(st, "t32big", [128, 12 * 512])
        X.t32big = t32big
        X.T32 = [(t32big[:, i * 512:(i + 1) * 512], Buf("t32_%d" % i)) for i in range(12)]
        cst, B_cst, pv, B_pv = X.cst, X.B_cst, X.pv, X.B_pv
        S.dma("sp", cst[:], cs_d, [], B_cst)
        for l in range(DEPTH):
            S.dma("sp", pv[:, l, :], pv_d[l], [], B_pv)
        S.dma("sp", cT[:], cT_d, [], B_cT)
        CP("act", X.identb[:], cst[:, CS["ident"]:CS["ident"] + 128], [B_cst], [X.B_identb])
        CP("act", X.obb[:], cst[:, CS["ob"]:CS["ob"] + 128], [B_cst], [X.B_obb])
        S.op("act", lambda e: e.mul(out=X.ob64[:], in_=cst[:, CS["ob"]:CS["ob"] + 128], mul=1.0 / 64.0), [B_cst], [X.B_ob64])
        ident = cst[:, CS["ident"]:CS["ident"] + 128]
        ACT(sc[:].rearrange("p k t -> p (k t)"), cT[:], AF.Silu, [B_cT], [B_sc])

        def pcol(l, name, j):
            o = PV[name] + j
            return pv[:, l, o:o + 1]
        X.pcol = pcol

        ws_rr = [0]

        def make_wslots(stack):
            X.WS = [sb(stack, "ws%d" % i, [128, 16, 512], BF16) for i in range(2)]

        def load_w(dram_ap_3d, nk, ncols):
            i = ws_rr[0] % 2
            ws_rr[0] += 1
            t, B = X.WS[i]
            S.dma("pool", t[:, 0:nk, 0:ncols], dram_ap_3d, [], B)
            return t, B
        X.load_w = load_w
        pa_rr = [0]

        def pacc():
            i = pa_rr[0] % 2
            pa_rr[0] += 1
            return PB[i]

        for l in range(depth):
            X.l = l
            X.last = last = (l == DEPTH - 1)
            with contextlib.ExitStack() as sA:
                make_wslots(sA)
                xnT, _bx = sb(sA, "xnT", [128, 16, NT], BF16)
                B_xnTs = [Buf("xnT_%d_%d" % (l, i)) for i in range(16)]
                B_xnT = B_xnTs
                stg = [sb(sA, "stg%d" % i, [128, NT]) for i in range(2)]
                stgb = [sb(sA, "stgb%d" % i, [128, NT], BF16) for i in range(2)]
                htile = [sb(sA, "ht%d" % i, [128, D]) for i in range(2)]
                sqj, B_sqj = stg[0][0][:, 0:D], stg[0][1]
                stat, B_stat = sb(sA, "stat", [128, 8])
                B_stats = [Buf("stat%d_%d" % (l, i)) for i in range(4)]
                htile = htile + [(stg[1][0][:, 0:D], stg[1][1])]
                X.mod, X.B_mod = MODS[l]
                mod, B_mod = MODS[l]
                gsc, B_gsc = GSCS[l]

                def phase0_gen(ll, slots, halfk):
                    pm, B_pm = PB[2]
                    mod_, B_mod_ = MODS[ll]
                    gsc_, B_gsc_ = GSCS[ll]
                    nh = 2 if halfk else 1
                    kper = 16 // nh
                    cnt = 0
                    for blk in range(12):
                        for hf in range(nh):
                            wt, Bw = slots[cnt % len(slots)]
                            cnt += 1
                            src_ = X.ada_w[ll][hf * kper * 128:(hf + 1) * kper * 128, blk * 512:(blk + 1) * 512]
                            S.dma("pool", wt[:, 0:kper, :], src_.rearrange("(k p) c -> p k c", p=128), [], Bw)
                            for ft in range(4):
                                fc = blk * 4 + ft
                                for kc in range(kper):
                                    MM(pm[:, hf * 96 + fc * 2:hf * 96 + fc * 2 + 2], wt[:, kc, ft * 128:(ft + 1) * 128], sc[:, hf * kper + kc, :],
                                       [Bw, B_sc], [B_pm], start=(kc == 0), stop=(kc == kper - 1))
                                yield
                    adab = pv[:, ll, PV["adab"]:PV["adab"] + 48]
                    TT(mod_[:], pm[:, 0:96].rearrange("p (f t) -> p f t", t=2), adab.unsqueeze(2).to_broadcast([128, 48, 2]), ALU.add,
                       [B_pm, B_pv], [B_mod_])
                    if halfk:
                        TT(mod_[:], mod_[:], pm[:, 96:192].rearrange("p (f t) -> p f t", t=2), ALU.add, [B_pm, B_mod_], [B_mod_])
                    ngv = pv[:, ll, PV["ng"]:PV["ng"] + 16]
                    STT(gsc_[:], mod_[:, 16:32, :], 1.0, ngv.unsqueeze(2).to_broadcast([128, 16, 2]), ALU.add, ALU.mult, [B_mod_, B_pv], [B_gsc_])
                    yield
                bg = None
                if l == 0:
                    for _ in phase0_gen(0, X.WS, False):
                        pass
                    if depth > 1:
                        aws = [sb(sA, "aws%d" % i, [128, 8, 512], BF16) for i in range(1)]
                        bg = phase0_gen(1, aws, True)
                if X.stop_after == (l, 0):
                    break
                src = X.xin if l == 0 else X.H1_d
                p1_rr = [0]
                for blk in range(NT // 128):
                    t0 = blk * 128
                    si = 1 if t0 < NCTX else 0
                    ht, Bh = htile[blk % 3]
                    rd = [] if l == 0 else [X.B_H1[0 if t0 < 256 else 1 + (t0 - 256) // 512]]
                    S.dma("sp", ht[:], src[t0:t0 + 128, :], rd, Bh)
                    c0 = (blk % 4) * 2
                    B_st = B_stats[blk % 4]
                    SQACC(ht[:], sqj[:], stat[:, c0:c0 + 1], [Bh], [B_sqj, B_st])
                    TS(stat[:, c0 + 1:c0 + 2], stat[:, c0:c0 + 1], 1.0 / D, 1e-6, ALU.mult, ALU.add, [B_st], [B_st])
                    ACT(stat[:, c0 + 1:c0 + 2], stat[:, c0 + 1:c0 + 2], AF.Sqrt, [B_st], [B_st])
                    RECIP(stat[:, c0 + 1:c0 + 2], stat[:, c0 + 1:c0 + 2], [B_st], [B_st])
                    TS(ht[:], ht[:], stat[:, c0 + 1:c0 + 2], None, ALU.mult, None, [Bh, B_st], [Bh])
                    for g4 in range(4):
                        banks = [PB[p1_rr[0] % 6], PB[(p1_rr[0] + 1) % 6]]
                        p1_rr[0] += 2
                        for q in range(4):
                            fc = g4 * 4 + q
                            pt, Bp = banks[q % 2]
                            TR(pt[:, (q // 2) * 128:(q // 2 + 1) * 128], ht[:, fc * 128:(fc + 1) * 128], ident, [Bh, B_cst], [Bp])
                        for q in range(4):
                            fc = g4 * 4 + q
                            pt, Bp = banks[q % 2]
                            psl = pt[:, (q // 2) * 128:(q // 2 + 1) * 128]
                            if q % 2 == 0:
                                ACT(xnT[:, fc, t0:t0 + 128], psl, AF.Identity, [Bp, B_gsc, B_mod], [B_xnTs[fc]],
                                    bias=mod[:, fc, si:si + 1], scale=gsc[:, fc, si:si + 1])
                            else:
                                TS(xnT[:, fc, t0:t0 + 128], psl, gsc[:, fc, si:si + 1], mod[:, fc, si:si + 1],
                                   ALU.mult, ALU.add, [Bp, B_gsc, B_mod], [B_xnTs[fc]])
                if l == 0:
                    dbg_dump("xn0", xnT[:, 0, :], B_xnT) if "xn0" in X.dbg_out and False else None

                if X.stop_after == (l, 1):
                    break
                def project(col0, ncols, handler):
                    wt, Bw = load_w(X.w_in[l][:, col0:col0 + ncols].rearrange("(k p) c -> p k c", p=128), 16, ncols)
                    for ft in range(ncols // 128):
                        ftile_ = col0 // 128 + ft
                        ctx_needed = (not last) or (8 <= ftile_ < 24) or (ftile_ in (OFF_DEC // 128, OFF_ICL // 128))
                        for (t0, n) in TILES:
                            if t0 == 0 and not ctx_needed:
                                continue
                            pt, Bp = pacc()
                            for kc in range(16):
                                MM(pt[:, 0:n], wt[:, kc, ft * 128:(ft + 1) * 128], xnT[:, kc, t0:t0 + n], [Bw, B_xnTs[kc]], [Bp],
                                   start=(kc == 0), stop=(kc == 15))
                            handler(ftile_, t0, n, pt, Bp)
                        if bg is not None:
                            next(bg, None)

                ev_rr = [0]

                def h_f32(ftile, t0, n, pt, Bp):
                    s_, Bs = stg[ftile % 2]
                    eng = "act" if ev_rr[0] % 2 else "dve"
                    ev_rr[0] += 1
                    CP(eng, s_[:, t0:t0 + n], pt[:, 0:n], [Bp], [Bs])
                    if t0 + n == NT:
                        S.dma("sp", X.U32_d[ftile], s_[:], [Bs], X.B_U32[ftile])

                def h_silu(ftile, t0, n, pt, Bp):
                    idx = ftile - OFF_GA // 128
                    s_, Bs = stgb[idx % 2]
                    ACT(s_[:, t0:t0 + n], pt[:, 0:n], AF.Silu, [Bp], [Bs])
                    if t0 + n == NT:
                        S.dma("sp", X.SG_d[idx], s_[:], [Bs], X.B_SG[idx])

                def h_lora(ftile, t0, n, pt, Bp):
                    if ftile > OFF_DEC // 128 + 1:
                        return
                    if ftile == OFF_DEC // 128:
                        ACT(X.dlrT[:, t0:t0 + n], pt[:, 0:n], AF.Tanh if 'T' not in X.skip else AF.Sigmoid, [Bp], [X.B_dlrT])
                    else:
                        CP("dve", X.alrT[:, t0:t0 + n], pt[:, 0:n], [Bp], [X.B_alrT])

                def h_sig(ftile, t0, n, pt, Bp):
                    idx = ftile - OFF_MG // 128
                    s_, Bs = stgb[idx % 2]
                    ACT(s_[:, t0:t0 + n], pt[:, 0:n], AF.Sigmoid, [Bp], [Bs])
                    if t0 + n == NT:
                        S.dma("sp", X.MG_d[idx], s_[:], [Bs], X.B_MG[idx])

                for blk in range(8 if 'a' not in X.skip else 1):
                    project(blk * 512, 512, h_f32)
                for blk in range(4 if 'b' not in X.skip else 0):
                    project(OFF_GA + blk * 512, 512, h_silu)
                if 'c' not in X.skip:
                    project(OFF_DEC, 512, h_lora)
                for blk in range(8 if 'd' not in X.skip else 0):
                    project(OFF_MG + blk * 512, 512, h_sig)
                if l > 0:
                    i = ws_rr[0] % 2
                    ws_rr[0] += 1
                    wt, Bw = X.WS[i]
                    vst, B_vst = stg[0]
                    S.dma("sp", vst[:, 0:512].rearrange("p (k c) -> p k c", c=32), X.vla_d.rearrange("(k p) c -> p k c", p=128), [], B_vst)
                    CP("act", wt[:, :, 0:32], vst[:, 0:512].rearrange("p (k c) -> p k c", c=32), [B_vst], [Bw])
                    for (t0, n) in TILES:
                        pt, Bp = pacc()
                        for kc in range(16):
                            MM(pt[0:32, 0:n], wt[:, kc, 0:32], xnT[:, kc, t0:t0 + n], [Bw, B_xnTs[kc]], [Bp], start=(kc == 0), stop=(kc == 15))
                        CP("dve", X.vlT[:, t0:t0 + n], pt[0:32, 0:n], [Bp], [X.B_vlT])
                if bg is not None:
                    for _ in bg:
                        pass
                if X.stop_after == (l, 2):
                    break
            S.barrier()
            with contextlib.ExitStack() as sB:
                stopped = False
                with contextlib.ExitStack() as s3:
                    X.ygst = [sb(s3, "ygst%d" % i, [128, 512], BF16) for i in range(2)]
                    try:
                        phase3(X, s3)
                    except _Stop:
                        stopped = True
                S.barrier()
                if stopped or X.stop_after == (l, 3):
                    break
                X.yg, X.B_yg = sb(sB, "yg", [128, 8, NT], BF16)
                for j_ in range(8):
                    S.dma("sp", X.yg[:, j_, :], X.YG_d[j_], [X.B_YG], X.B_yg)
                X.yb, X.B_yb = sb(sB, "yb", [128, 8, NT], BF16)
                with contextlib.ExitStack() as s4:
                    phase4(X, s4)
                S.barrier()
                with contextlib.ExitStack() as s5:
                    make_wslots(s5)
                    phase5a(X, s5)
            S.barrier()
            with contextlib.ExitStack() as s5b:
                phase5b(X, s5b)
            S.barrier()

        S.emit(final_waits=X.finals)
    return nc


def phase3(X, stk):
    S, l, last = X.S, X.l, X.last
    MM, TR, ACT, CP, TT, TS, STT, RECIP, SCAN, MEMSET = X.MM, X.TR, X.ACT, X.CP, X.TT, X.TS, X.STT, X.RECIP, X.SCAN, X.MEMSET
    PB, PTRt, cst, B_cst, B_pv, pcol = X.PB, X.PTRt, X.cst, X.B_cst, X.B_pv, X.pcol
    identb, B_identb, obb, B_obb, ob64, B_ob64 = X.identb, X.B_identb, X.obb, X.B_obb, X.ob64, X.B_ob64
    dlrT, B_dlrT, alrT, B_alrT, vlT, B_vlT = X.dlrT, X.B_dlrT, X.alrT, X.B_alrT, X.vlT, X.B_vlT
    T32 = X.T32

    def sb(name, shape, dt=F32):
        return X.sb(stk, "p3" + name, shape, dt)

    XS = [sb("xs%d" % i, [128, NT]) for i in range(3)]
    KK, B_KK = sb("kk", [128, NT])
    SGa, B_SGa = sb("sga", [128, NT], BF16)
    LW, B_lw = sb("lw", [128, 2, 1024], BF16)
    VLB, B_vlb = sb("vlb", [32, 1024], BF16)
    VP, B_VP = sb("vp", [128, 8, 2, 64], BF16)
    RKB, B_RKB = sb("rkb", [128, 512], BF16)
    RKBs = None
    SQ, B_SQ = sb("sq", [128, 512])
    t32b, _ = sb("t32b", [128, 8 * 512])
    temps = [T32[0:8], [(t32b[:, i * 512:(i + 1) * 512], Buf("t32b_%d" % i)) for i in range(8)]]
    RKBs = [(RKB, B_RKB), (temps[1][0][0].bitcast(BF16)[:, 0:512], temps[1][0][1]), (temps[1][1][0].bitcast(BF16)[:, 0:512], temps[1][1][1])]
    PTRh = [(PTRt[:, 0:512], Buf("ptr0")), (X.PTRt2[:, 0:512], Buf("ptr1"))]
    PSH, B_PSH = PB[0]

    class St:
        pass
    STR = []
    for d in range(2):
        Z = St()
        Z.d = d
        Z.KB = sb("kb%d" % d, [128, NT], BF16)
        Z.Y = sb("y%d" % d, [128, NT])
        Z.UV = sb("uv%d" % d, [128, NCH, 2, 64], BF16)
        Z.AR = sb("ar%d" % d, [128, 8, 2, 64], BF16)
        Z.BK = sb("bk%d" % d, [128, 8, 2, 64], BF16)
        Z.BKp = sb("bkp%d" % d, [128, 8, 2, 64], BF16)
        Z.BKpT = sb("bkpt%d" % d, [128, 8, 128], BF16)
        Z.AqT = sb("aqt%d" % d, [64, 8, 128], BF16)
        Z.GL = sb("gl%d" % d, [128, 8, 64], BF16)
        Z.GR = sb("gr%d" % d, [128, 16, 64], BF16)
        Z.Qs = [sb("q%d_%d" % (d, i), [64, 8, 64], BF16) for i in range(2)]
        Z.Ps = [sb("p%d_%d" % (d, i), [64, 8, 64], BF16) for i in range(2)]
        Z.Ts = [sb("tt%d_%d" % (d, i), [64, 8, 64], BF16) for i in range(2)]
        Z.XL = sb("xl%d" % d, [64, 8, 64], BF16)
        Z.UL = sb("ul%d" % d, [64, 16, 64], BF16)
        Z.AqP = sb("aqp%d" % d, [128, 8, 64], BF16)
        Z.PC = sb("pc%d" % d, [128, 8])
        Z.ST32 = sb("st32_%d" % d, [128, 64])
        Z.STb = sb("stb%d" % d, [128, 64], BF16)
        Z.T = temps[d]
        Z.s0, Z.s1, Z.s2 = PB[3 * d], PB[3 * d + 1], PB[3 * d + 2]
        Z.PTR = PTRh[d]
        STR.append(Z)
    X32, B_X32 = STR[1].Y
    MEMSET("dve", VP[:], 0.0, [B_VP])
    for which, src_d in ((0, X.dlb_d), (1, X.ilb_d)):
        f_ = X.t32big[:, which * 1024:(which + 1) * 1024]
        Bs_ = [T32[which * 2][1], T32[which * 2 + 1][1]]
        S.op("dve", lambda e, a=T32[which * 2 + 1][0][:, 0:1]: e.memset(a, 0.0), [], [Bs_[1]])
        S.dma("sp", f_, src_d[l], [Bs_[1]], Bs_[0])
        CP("act", LW[:, which, :], f_, Bs_, [B_lw])
    if l > 0:
        f_ = X.t32big[0:32, 4 * 512:6 * 512]
        Bs_ = [T32[4][1], T32[5][1]]
        S.op("dve", lambda e, a=T32[5][0][:, 0:1]: e.memset(a, 0.0), [], [Bs_[1]])
        S.dma("sp", f_, X.vlb_d, [Bs_[1]], Bs_[0])
        CP("act", VLB[:], f_, Bs_, [B_vlb])

    mq = [cst[0:64, CS["mq0"]:CS["mq0"] + 64], cst[0:64, CS["mq1"]:CS["mq1"] + 64]]
    mgm = [cst[:, CS["mg0"]:CS["mg0"] + 128], cst[:, CS["mg1"]:CS["mg1"] + 128]]
    rst = cst[:, CS["rst"]:CS["rst"] + 512]
    id64 = cst[0:64, CS["ident"]:CS["ident"] + 64]

    def shift_mix(j, src, dst, Bs, Bd, mu_col, ommu_col):
        def mix(dsl_d, sl_from, sl_self, bnd_d, bnd_s):
            TT(dsl_d, sl_from, sl_self, ALU.subtract, [Bs], [Bd])
            STT(dsl_d, dsl_d, mu_col, sl_self, ALU.mult, ALU.add, [Bs, Bd, B_pv], [Bd])
            TS(bnd_d, bnd_s, ommu_col, None, ALU.mult, None, [Bs, B_pv], [Bd])
        if j < 4:
            mix(dst[:, 1:NCTX], src[:, 0:NCTX - 1], src[:, 1:NCTX], dst[:, 0:1], src[:, 0:1])
        else:
            mix(dst[:, 0:NCTX - 1], src[:, 1:NCTX], src[:, 0:NCTX - 1], dst[:, NCTX - 1:NCTX], src[:, NCTX - 1:NCTX])
        s3 = src[:, NCTX:NT].rearrange("p (r c) -> p r c", c=64)
        d3 = dst[:, NCTX:NT].rearrange("p (r c) -> p r c", c=64)
        q = j // 2
        if q == 0:
            mix(d3[:, :, 1:64], s3[:, :, 0:63], s3[:, :, 1:64], d3[:, :, 0:1], s3[:, :, 0:1])
        elif q == 1:
            mix(d3[:, :, 0:63], s3[:, :, 1:64], s3[:, :, 0:63], d3[:, :, 63:64], s3[:, :, 63:64])
        elif q == 2:
            mix(d3[:, 1:32, :], s3[:, 0:31, :], s3[:, 1:32, :], d3[:, 0:1, :], s3[:, 0:1, :])
        else:
            mix(d3[:, 0:31, :], s3[:, 1:32, :], s3[:, 0:31, :], d3[:, 31:32, :], s3[:, 31:32, :])

    def c3(ap2):
        return ap2.rearrange("p (c t) -> p c t", t=64)

    def rr(gens):
        gens = list(gens)
        while gens:
            for g in list(gens):
                try:
                    next(g)
                except StopIteration:
                    gens.remove(g)

    (RS, B_RS), (KS, B_KS), (VS, B_VS) = XS
    X32b, B_X32b = STR[0].Y

    def sweep(j, Z):
        d = Z.d
        (KB, B_KB), (Yd, B_Yd), (UV, B_UV) = Z.KB, Z.Y, Z.UV
        (AR, B_AR), (BK, B_BK), (BKp, B_BKp) = Z.AR, Z.BK, Z.BKp
        (BKpT, B_BKpT), (AqT, B_AqT), (GL, B_GL), (GR, B_GR) = Z.BKpT, Z.AqT, Z.GL, Z.GR
        Qs, Ps, Ts = Z.Qs, Z.Ps, Z.Ts
        (XL, B_XL), (UL, B_UL), (AqP, B_AqP), (PC, B_PC) = Z.XL, Z.UL, Z.AqP, Z.PC
        (ST32, B_ST32), (STb, B_STb) = Z.ST32, Z.STb
        (P0, B_P0), (P1, B_P1), (P2, B_P2) = Z.s0, Z.s1, Z.s2
        PTRd, B_PTRd = Z.PTR
        lw = LW[:, :, j * 128:(j + 1) * 128]
        order = [0, 1, 2, 3, 4] if d == 0 else [0, 4, 3, 2, 1]
        MEMSET("dve", ST32[:], 0.0, [B_ST32])
        MEMSET("dve", STb[:], 0.0, [B_STb])
        dsl = slice(d * 64, (d + 1) * 64)
        for ti in order:
            t0, n = TILES[ti]
            nch = n // 64
            c0 = t0 // 64
            (A32, B_A32), (SIG, B_SIG), (LS, B_LS), (EA, B_EA), (EB, B_EB), (KD, B_KD), (KA, B_KA), (TMP, B_TMP) = Z.T
            MM(P0[:, 0:n], lw[dsl, 1, :], alrT[dsl, t0:t0 + n], [B_lw, B_alrT], [B_P0])
            ACT(A32[:, 0:n], P0[:, 0:n], AF.Sigmoid, [B_P0, B_pv], [B_A32], bias=pcol(l, "a0", d * 8 + j))
            MM(P0[:, 0:n], lw[dsl, 0, :], dlrT[dsl, t0:t0 + n], [B_lw, B_dlrT], [B_P0])
            ACT(SIG[:, 0:n], P0[:, 0:n], AF.Sigmoid, [B_P0, B_pv], [B_SIG], bias=pcol(l, "w0", d * 8 + j))
            if 'a' not in X.noy:
                yield
            SCAN(LS[:, 0:n], rst[:, 0:n], SIG[:, 0:n], [B_cst, B_SIG], [B_LS])
            L3 = c3(LS[:, 0:n])
            S3 = c3(SIG[:, 0:n])
            T3 = c3(TMP[:, 0:n])
            if d == 1:
                TT(T3, L3[:, :, 63:64].to_broadcast([128, nch, 64]), L3, ALU.subtract, [B_LS], [B_TMP])
                TT(L3, T3, S3, ALU.add, [B_TMP, B_SIG], [B_LS])
                endc = 0
            else:
                endc = 63
            if 'a' not in X.noy:
                yield
            ACT(KD[:, 0:n], A32[:, 0:n], AF.Identity, [B_A32, B_pv], [B_KD], scale=pcol(l, "ka", d * 8 + j), bias=pcol(l, "omka", d * 8 + j))
            TT(KD[:, 0:n], KD[:, 0:n], KS[:, t0:t0 + n], ALU.mult, [B_KD, B_KS], [B_KD])
            TT(KA[:, 0:n], KK[:, t0:t0 + n], A32[:, 0:n], ALU.mult, [B_KK, B_A32], [B_KA])
            ACT(KB[:, t0:t0 + n], KD[:, 0:n], AF.Identity, [B_KD, B_pv], [B_KB], scale=pcol(l, "brk", d * 8 + j))
            if 'a' not in X.noy:
                yield
            TT(TMP[:, 0:n], LS[:, 0:n], SIG[:, 0:n], ALU.subtract, [B_LS, B_SIG], [B_TMP])
            ACT(EA[:, 0:n], TMP[:, 0:n], AF.Exp, [B_TMP], [B_EA], scale=-C0)
            ACT(EB[:, 0:n], LS[:, 0:n], AF.Exp, [B_LS], [B_EB], scale=-C0)
            STT(AR[:, 0:nch, 0, :], c3(KK[:, t0:t0 + n]), -1.0, c3(EA[:, 0:n]), ALU.mult, ALU.mult, [B_KK, B_EA], [B_AR])
            if 'a' not in X.noy:
                yield
            TT(AR[:, 0:nch, 1, :], c3(RS[:, t0:t0 + n]), c3(EB[:, 0:n]), ALU.mult, [B_RS, B_EB], [B_AR])
            CP("act", PC[:, 0:nch], c3(EB[:, 0:n])[:, :, endc:endc + 1].rearrange("p c o -> p (c o)"), [B_EB], [B_PC])
            ACT(EA[:, 0:n], LS[:, 0:n], AF.Exp, [B_LS], [B_EA], scale=C0)
            if 'a' not in X.noy:
                yield
            TT(BK[:, 0:nch, 0, :], c3(KA[:, 0:n]), c3(EA[:, 0:n]), ALU.mult, [B_KA, B_EA], [B_BK])
            TT(BK[:, 0:nch, 1, :], c3(KD[:, 0:n]), c3(EA[:, 0:n]), ALU.mult, [B_KD, B_EA], [B_BK])
            if 'a' not in X.noy:
                yield
            TT(BKp[:, 0:nch, :, :].rearrange("p c a t -> p c (a t)"), BK[:, 0:nch, :, :].rearrange("p c a t -> p c (a t)"),
               PC[:, 0:nch].unsqueeze(2).to_broadcast([128, nch, 128]), ALU.mult, [B_BK, B_PC], [B_BKp])
            if 'a' not in X.noy:
                yield
            for r0 in range(0, nch, 4):
                for c in range(r0, r0 + 4):
                    TR(PTRd[:, (c - r0) * 128:(c - r0 + 1) * 128], BKp[:, c, :, :].rearrange("p a t -> p (a t)"), identb[:],
                       [B_BKp, B_identb], [B_PTRd])
                CP("act", BKpT[:, r0:r0 + 4, :], PTRd[:, 0:512].rearrange("p (c f) -> p c f", f=128), [B_PTRd], [B_BKpT])
                if 'b' not in X.noy:
                    yield
                for c in range(r0, r0 + 4):
                    TR(PTRd[0:64, (c - r0) * 128:(c - r0 + 1) * 128], AR[:, c, 0, :], identb[:], [B_AR, B_identb], [B_PTRd])
                CP("act", AqT[:, r0:r0 + 4, :], PTRd[0:64, 0:512].rearrange("p (c f) -> p c f", f=128), [B_PTRd], [B_AqT])
                if 'b' not in X.noy:
                    yield
            for g0 in range(0, nch, 4):
                units = [(g0 + cl, h) for h in range(2) for cl in range(4)]
                for u, (c, h) in enumerate(units):
                    hs = slice(h * 64, (h + 1) * 64)
                    PGt, B_PGt = (P1, B_P1) if h == 0 else (P2, B_P2)
                    uo = (u % 4) * 128
                    MM(PGt[:, uo:uo + 128], BK[hs, c, :, :].rearrange("p a t -> p (a t)"), AR[hs, c, :, :].rearrange("p a t -> p (a t)"),
                       [B_BK, B_AR], [B_PGt])
                if 'c' not in X.noy:
                    yield
                for half, (PGt, B_PGt) in enumerate(((P1, B_P1), (P2, B_P2))):
                    p4 = PGt[:, :].rearrange("p (u f) -> p u f", f=128)
                    mb = mgm[d].unsqueeze(1).to_broadcast([128, 4, 128])
                    TT(GL[:, half * 4:half * 4 + 4, :], p4[:, :, 0:64], mb[:, :, 0:64], ALU.mult, [B_PGt, B_cst], [B_GL])
                    TT(GR[:, g0 * 2 + half * 4:g0 * 2 + half * 4 + 4, :], p4[:, :, 64:128], mb[:, :, 64:128], ALU.mult, [B_PGt, B_cst], [B_GR])
                for u, (c, h) in enumerate(units):
                    hs = slice(h * 64, (h + 1) * 64)
                    PQh, B_PQh = (P0, B_P0) if h == 0 else (P1, B_P1)
                    MM(PQh[0:64, (u % 4) * 64:(u % 4 + 1) * 64], AR[hs, c, 0, :], BK[hs, c, 0, :], [B_AR, B_BK], [B_PQh])
                if 'c' not in X.noy:
                    yield
                Q0, B_Q0 = Qs[0]
                TT(Q0[:, 0:4, :], P0[0:64, 0:256].rearrange("p (u f) -> p u f", f=64), mq[d].unsqueeze(1).to_broadcast([64, 4, 64]), ALU.mult,
                   [B_P0, B_cst], [B_Q0])
                TT(Q0[:, 4:8, :], P1[0:64, 0:256].rearrange("p (u f) -> p u f", f=64), mq[d].unsqueeze(1).to_broadcast([64, 4, 64]), ALU.mult,
                   [B_P1, B_cst], [B_Q0])
                T0, B_T0 = Ts[0]
                TT(T0[:], GL[0:64, :, :], id64.unsqueeze(1).to_broadcast([64, 8, 64]), ALU.add, [B_GL, B_cst], [B_T0])
                if 'c' not in X.noy:
                    yield
                Pprev, B_Pprev = GL[0:64, :, :], B_GL
                Qprev, B_Qprev = Q0[:], B_Q0
                Tprev, B_Tprev = T0[:], B_T0
                for lvl in range(1, 6):
                    Qn, B_Qn = Qs[lvl % 2]
                    Pn, B_Pn = Ps[lvl % 2]
                    Tn, B_Tn = Ts[lvl % 2]
                    for u in range(8):
                        MM(P0[0:64, u * 64:(u + 1) * 64], Pprev[:, u, :], Qprev[:, u, :], [B_Pprev, B_Qprev], [B_P0])
                    if lvl < 5:
                        for u in range(8):
                            MM(P1[0:64, u * 64:(u + 1) * 64], Qprev[:, u, :], Pprev[:, u, :], [B_Pprev, B_Qprev], [B_P1])
                    if 'c' not in X.noy:
                        yield
                    CP("act", Qn[:], P0[0:64, :].rearrange("p (u f) -> p u f", f=64), [B_P0], [B_Qn])
                    if lvl < 5:
                        CP("act", Pn[:], P1[0:64, :].rearrange("p (u f) -> p u f", f=64), [B_P1], [B_Pn])
                    if 'c' not in X.noy:
                        yield
                    for u in range(8):
                        MM(P2[0:64, u * 64:(u + 1) * 64], Qn[:, u, :], Tprev[:, u, :], [B_Qn, B_Tprev], [B_P2])
                    TT(Tn[:], P2[0:64, :].rearrange("p (u f) -> p u f", f=64), Tprev, ALU.add, [B_P2, B_Tprev], [B_Tn])
                    if 'c' not in X.noy:
                        yield
                    if lvl < 5:
                        Pprev, B_Pprev = Pn[:], B_Pn
                    Qprev, B_Qprev = Qn[:], B_Qn
                    Tprev, B_Tprev = Tn[:], B_Tn
                Tf, B_Tf = Tprev, B_Tprev
                for u, (c, h) in enumerate(units):
                    MM(P1[h * 64:(h + 1) * 64, (u % 4) * 64:(u % 4) * 64 + 64], AqT[:, c, h * 64:(h + 1) * 64], Tf[:, u, :],
                       [B_AqT, B_Tf], [B_P1])
                for u, (c, h) in enumerate(units):
                    MM(P0[0:64, u * 64:(u + 1) * 64], GL[64:128, u, :], UV[64:128, c0 + c, h, :], [B_GL, B_UV], [B_P0])
                if 'c' not in X.noy:
                    yield
                CP("act", AqP[:, g0:g0 + 4, :], P1[:, 0:256].rearrange("p (c f) -> p c f", f=64), [B_P1], [B_AqP])
                CP("act", XL[:], P0[0:64, :].rearrange("p (u f) -> p u f", f=64), [B_P0], [B_XL])
                if 'c' not in X.noy:
                    yield
                for u in range(8):
                    MM(P2[0:64, u * 64:(u + 1) * 64], Tf[:, u, :], XL[:, u, :], [B_Tf, B_XL], [B_P2])
                CP("act", UL[:, g0 * 2:g0 * 2 + 8, :], P2[0:64, :].rearrange("p (u f) -> p u f", f=64), [B_P2], [B_UL])
                if 'c' not in X.noy:
                    yield
            corder = list(range(nch)) if d == 0 else list(range(nch - 1, -1, -1))
            for c in corder:
                cg = c0 + c

                def gi_(h_):
                    return (c // 4) * 8 + h_ * 4 + (c % 4)
                PYs = ((P0, B_P0), (P2, B_P2))
                for h in range(2):
                    hs = slice(h * 64, (h + 1) * 64)
                    MM(PYs[h][0][hs, c * 64:(c + 1) * 64], STb[hs, :], AR[hs, c, 1, :], [B_STb, B_AR], [PYs[h][1]], start=True, stop=False)
                hs0, hs1 = slice(0, 64), slice(64, 128)
                MM(P1[0:64, 0:64], AqP[hs0, c, :], STb[hs0, :], [B_AqP, B_STb], [B_P1])
                MM(P2[0:64, 0:64], AqP[hs1, c, :], STb[hs1, :], [B_AqP, B_STb], [B_P2])
                if 'e' not in X.noy:
                    yield
                TT(UV[0:64, cg, 0, :], P1[0:64, 0:64], UL[:, gi_(0), :], ALU.add, [B_P1, B_UL], [B_UV])
                TT(UV[0:64, cg, 1, :], P2[0:64, 0:64], UL[:, gi_(1), :], ALU.add, [B_P2, B_UL], [B_UV])
                if 'e' not in X.noy:
                    yield
                for h in range(2):
                    hs = slice(h * 64, (h + 1) * 64)
                    MM(P1[hs, 64:128], BKpT[:, c, hs], UV[:, cg, h, :], [B_BKpT, B_UV], [B_P1])
                for h in range(2):
                    hs = slice(h * 64, (h + 1) * 64)
                    MM(PYs[h][0][hs, c * 64:(c + 1) * 64], UV[:, cg, h, :], GR[:, gi_(h), :], [B_UV, B_GR], [PYs[h][1]], start=False, stop=True)
                if 'e' not in X.noy:
                    yield
                STT(STb[:], ST32[:], PC[:, c:c + 1], P1[:, 64:128], ALU.mult, ALU.add, [B_ST32, B_PC, B_P1], [B_STb])
                STT(ST32[:], ST32[:], PC[:, c:c + 1], P1[:, 64:128], ALU.mult, ALU.add, [B_ST32, B_PC, B_P1], [B_ST32])
                if 'e' not in X.noy:
                    yield
            CP("act", Yd[0:64, t0:t0 + n], P0[0:64, 0:n], [B_P0], [B_Yd])
            CP("act", Yd[64:128, t0:t0 + n], P2[64:128, 0:n], [B_P2], [B_Yd])
            if 'a' not in X.noy:
                yield

    for j in range(X.nj):
        vlb = VLB[:, j * 128:(j + 1) * 128]
        lbufs = [(X32, B_X32), (X32b, B_X32b), (X32, B_X32)]

        def ld(m):
            S.dma("sp", lbufs[m][0][:], X.U32_d[m * 8 + j], [X.B_U32[m * 8 + j]], lbufs[m][1])

        def sh(m):
            shift_mix(j, lbufs[m][0], XS[m][0], lbufs[m][1], XS[m][1], pcol(l, "mu", m * 8 + j), pcol(l, "ommu", m * 8 + j))
        ld(0)
        ld(1)
        sh(0)
        ld(2)
        sh(1)
        sh(2)
        S.dma("sp", SGa[:], X.SG_d[j], [X.B_SG[j]], B_SGa)
        if l == 0:
            S.dma("pool", X.VF_d[j], VS[:], [B_VS], X.B_VF[j])
        else:
            S.dma("sp", X32[:], X.VF_d[j], [X.B_VF[j]], B_X32)

            def vres_chain(i, t0, n):
                pb_, Bpb_ = PB[i]
                g_, Bg = T32[2 * i]
                d_, Bd_ = T32[2 * i + 1]
                MM(pb_[0:128, 0:n], vlb, vlT[:, t0:t0 + n], [B_vlb, B_vlT], [Bpb_])
                yield
                ACT(g_[:, 0:n], pb_[:, 0:n], AF.Sigmoid, [Bpb_, B_pv], [Bg], bias=pcol(l, "v0", j))
                TT(d_[:, 0:n], X32[:, t0:t0 + n], VS[:, t0:t0 + n], ALU.subtract, [B_X32, B_VS], [Bd_])
                yield
                TT(d_[:, 0:n], d_[:, 0:n], g_[:, 0:n], ALU.mult, [Bd_, Bg], [Bd_])
                yield
                TT(VS[:, t0:t0 + n], VS[:, t0:t0 + n], d_[:, 0:n], ALU.add, [B_VS, Bd_], [B_VSt[i]])
                yield
            B_VSt = [Buf("vs_t%d_%d_%d" % (l, j, i)) for i in range(5)]
            rr([vres_chain(i, t0, n) for i, (t0, n) in enumerate(TILES)])
            S.op("dve", lambda e, a=T32[11][0][:, 0:1]: e.memset(a, 0.0), B_VSt, [T32[11][1], B_VS])

        def kk_chain(i, t0, n):
            pb_, Bpb_ = PB[i]
            kr, Bkr = T32[3 * i]
            nr, Bnr = T32[3 * i + 1]
            sq_, Bsq = T32[3 * i + 2]
            ACT(kr[:, 0:n], KS[:, t0:t0 + n], AF.Identity, [B_KS, B_pv], [Bkr], scale=pcol(l, "kk", j))
            yield
            TT(sq_[:, 0:n], kr[:, 0:n], kr[:, 0:n], ALU.mult, [Bkr], [Bsq])
            yield
            MM(pb_[:, 0:n], cst[:, CS["ob"]:CS["ob"] + 128], sq_[:, 0:n], [B_cst, Bsq], [Bpb_])
            yield
            ACT(nr[:, 0:n], pb_[:, 0:n], AF.Sqrt, [Bpb_], [Bnr])
            yield
            TS(nr[:, 0:n], nr[:, 0:n], 1e-12, None, ALU.max, None, [Bnr], [Bnr])
            RECIP(nr[:, 0:n], nr[:, 0:n], [Bnr], [Bnr])
            yield
            TT(KK[:, t0:t0 + n], kr[:, 0:n], nr[:, 0:n], ALU.mult, [Bkr, Bnr], [B_KKt[i]])
            yield
        B_KKt = [Buf("kk_t%d_%d_%d" % (l, j, i)) for i in range(5)]
        if 'K' in X.skip:
            for i, (t0, n) in enumerate(TILES[0:4]):
                rr([kk_chain(i, t0, n)])
        else:
            rr([kk_chain(i, t0, n) for i, (t0, n) in enumerate(TILES[0:4])])
        rr([kk_chain(0, *TILES[4])])
        S.op("dve", lambda e, a=T32[11][0][:, 1:2]: e.memset(a, 0.0), B_KKt, [T32[11][1], B_KK])
        Bp01 = [PTRh[0][1], PTRh[1][1]]
        for (t0, n) in TILES:
            nch = n // 64
            c0 = t0 // 64
            CP("act", VP[:, 0:nch, 1, :], c3(VS[:, t0:t0 + n]), [B_VS], [B_VP])
            for c in range(nch):
                TR(PTRt[:, c * 128:(c + 1) * 128], VP[:, c, :, :].rearrange("p a t -> p (a t)"), identb[:], [B_VP, B_identb], Bp01[0:1])
            for Z in STR:
                CP("dve", Z.UV[0][64:128, c0:c0 + nch, :, :].rearrange("p c h v -> p c (h v)"),
                   PTRt[64:128, 0:nch * 128].rearrange("p (c f) -> p c f", f=128), Bp01[0:1], [Z.UV[1]])
        if X.p3stop == 3:
            raise _Stop()
        gens = [sweep(j, STR[0]), sweep(j, STR[1])]
        if 'Q' in X.skip:
            for g in gens:
                for _ in g:
                    pass
            gens = []
        while gens:
            for g in list(gens):
                try:
                    next(g)
                except StopIteration:
                    gens.remove(g)
        (Y0, B_Y0), (Y1, B_Y1) = STR[0].Y, STR[1].Y
        (KB0, B_KB0), (KB1, B_KB1) = STR[0].KB, STR[1].KB

        def fin_chain(k, ti, t0, n):
            (YS, B_YS), (YC, B_YC), (BON, B_BON), (KBS, B_KBS) = T32[4 * k:4 * k + 4]
            SQf, B_SQf = KBS, B_KBS
            pb_, Bpb_ = PB[k]
            TT(YS[:, 0:n], Y0[:, t0:t0 + n], Y1[:, t0:t0 + n], ALU.add, [B_Y0, B_Y1], [B_YS])
            TT(KBS[:, 0:n], KB0[:, t0:t0 + n], KB1[:, t0:t0 + n], ALU.add, [B_KB0, B_KB1], [B_KBS])
            yield
            rkb, B_rkb = RKBs[k]
            TT(rkb[:, 0:n], RS[:, t0:t0 + n], KBS[:, 0:n], ALU.mult, [B_RS, B_KBS], [B_rkb])
            yield
            MM(pb_[:, 0:n], obb[:], rkb[:, 0:n], [B_obb, B_rkb], [Bpb_])
            yield
            TT(BON[:, 0:n], pb_[:, 0:n], VS[:, t0:t0 + n], ALU.mult, [Bpb_, B_VS], [B_BON])
            MM(pb_[:, 0:n], ob64[:], YS[:, 0:n], [B_ob64, B_YS], [Bpb_])
            yield
            TT(YC[:, 0:n], YS[:, 0:n], pb_[:, 0:n], ALU.subtract, [B_YS, Bpb_], [B_YC])
            yield
            TT(SQf[:, 0:n], YC[:, 0:n], YC[:, 0:n], ALU.mult, [B_YC], [B_SQf])
            yield
            MM(pb_[:, 0:n], ob64[:], SQf[:, 0:n], [B_ob64, B_SQf], [Bpb_])
            yield
            TS(YS[:, 0:n], pb_[:, 0:n], GN_EPS, None, ALU.add, None, [Bpb_], [B_YS])
            yield
            ACT(YS[:, 0:n], YS[:, 0:n], AF.Sqrt, [B_YS], [B_YS])
            yield
            RECIP(YS[:, 0:n], YS[:, 0:n], [B_YS], [B_YS])
            yield
            TT(YC[:, 0:n], YC[:, 0:n], YS[:, 0:n], ALU.mult, [B_YC, B_YS], [B_YC])
            yield
            ACT(YC[:, 0:n], YC[:, 0:n], AF.Identity, [B_YC, B_pv], [B_YC], scale=pcol(l, "gnw", j), bias=pcol(l, "gnb", j))
            yield
            TT(YC[:, 0:n], YC[:, 0:n], BON[:, 0:n], ALU.add, [B_YC, B_BON], [B_YC])
            yield
            ygt, B_ygt = X.ygst[(j * 5 + ti) % 2]
            TT(ygt[:, 0:n], YC[:, 0:n], SGa[:, t0:t0 + n], ALU.mult, [B_YC, B_SGa], [B_ygt])
            S.dma("pool", X.YG_d[j][:, t0:t0 + n], ygt[:, 0:n], [B_ygt], X.B_YG)
            yield
        ftiles = [(ti, t0, n) for ti, (t0, n) in enumerate(TILES) if not (last and ti == 0)]
        NF = 3
        for i0 in range(0, len(ftiles), NF):
            if 'F' in X.skip:
                for k in range(min(NF, len(ftiles) - i0)):
                    rr([fin_chain(k, *ftiles[i0 + k])])
            else:
                rr([fin_chain(k, *ftiles[i0 + k]) for k in range(min(NF, len(ftiles) - i0))])


def phase4(X, stk):
    S, l, last = X.S, X.l, X.last
    MM, TT, STT, MEMSET = X.MM, X.TT, X.STT, X.MEMSET
    PB, cst, B_cst, B_pv, pcol = X.PB, X.cst, X.B_cst, X.B_pv, X.pcol
    yb, B_yb = X.yb, X.B_yb
    T32 = X.T32

    def sb(name, shape, dt=F32):
        return X.sb(stk, "p4" + name, shape, dt)
    PPs = [sb("pp%d" % i, [128, NT + 32]) for i in range(2)]
    PW = [sb("w%d" % i, [128, 2, 128], BF16) for i in range(2)]
    PLD = [sb("pl%d" % i, [128, NT], BF16) for i in range(2)]
    SGb, B_SGb = sb("sgb", [128, NT], BF16)
    WA, B_WA = sb("wa", [128, 2080])
    WB, B_WB = sb("wb", [128, 2080])
    for t_, B_ in PPs:
        MEMSET("dve", t_[:], 0.0, [B_])
    wins = (2, 4, 8, 16)
    segs = [(8, NCTX, 0), (NCTX + 24, NLAT, NCTX)]
    for gi in range(4):
        w = wins[gi]
        for k2 in range(2):
            ptile = gi * 2 + k2
            pp, B_pp = PPs[k2]
            pl, B_pl = PLD[k2]
            for (po, ln, to) in segs:
                S.dma("sp", pp[:, po:po + ln], X.U32_d[24 + ptile][:, to:to + ln], [X.B_U32[24 + ptile]], B_pp)
            for (po, ln, to) in segs:
                for s0 in range(0, ln, 2048):
                    n = min(2048, ln - s0)
                    base = po + s0
                    cur, B_cur, nxt, B_nxt = WA, B_WA, WB, B_WB
                    TT(cur[:, 1:n + 15], pp[:, base - 8:base + n + 6], pp[:, base - 7:base + n + 7], ALU.add, [B_pp], [B_cur])
                    ww, lo, hi = 2, 1, n + 15
                    while ww < w:
                        hf = ww // 2
                        lo2, hi2 = lo + hf, hi - hf
                        TT(nxt[:, lo2:hi2], cur[:, lo2 - hf:hi2 - hf], cur[:, lo2 + hf:hi2 + hf], ALU.add, [B_cur], [B_nxt])
                        cur, B_cur, nxt, B_nxt = nxt, B_nxt, cur, B_cur
                        lo, hi = lo2, hi2
                        ww *= 2
                    STT(pl[:, to + s0:to + s0 + n], cur[:, 8:8 + n], 1.0 / w, pp[:, base:base + n], ALU.mult, ALU.subtract, [B_cur, B_pp], [B_pl])
                    tm, B_tm = T32[2]
                    if s0 == 0:
                        cc = cst[:, CS["pcor"] + gi * 16:CS["pcor"] + gi * 16 + 8]
                        TT(tm[:, 0:8], cur[:, 8:16], cc, ALU.mult, [B_cur, B_cst], [B_tm])
                        TT(pl[:, to:to + 8], tm[:, 0:8], pp[:, base:base + 8], ALU.subtract, [B_tm, B_pp], [B_pl])
                    if s0 + n == ln:
                        cc = cst[:, CS["pcor"] + gi * 16 + 8:CS["pcor"] + gi * 16 + 16]
                        TT(tm[:, 8:16], cur[:, n:n + 8], cc, ALU.mult, [B_cur, B_cst], [B_tm])
                        TT(pl[:, to + ln - 8:to + ln], tm[:, 8:16], pp[:, base + n - 8:base + n], ALU.subtract, [B_tm, B_pp], [B_pl])
        for k2 in range(2):
            wt, Bw = PW[k2]
            f_, Bf_ = T32[4 + k2]
            S.dma("sp", f_[:, 0:256], X.pw_d[l][gi][k2 * 128:(k2 + 1) * 128, :], [], Bf_)
            X.CP("act", wt[:], f_[:, 0:256].rearrange("p (o c) -> p o c", c=128), [Bf_], [Bw])
        for o2 in range(2):
            otile = gi * 2 + o2
            S.dma("sp", SGb[:], X.SG_d[8 + otile], [X.B_SG[8 + otile]], B_SGb)
            for ti, (t0, n) in enumerate(TILES):
                if last and ti == 0:
                    continue
                pt, Bp = PB[(o2 + ti) % 2]
                for k2 in range(2):
                    MM(pt[:, 0:n], PW[k2][0][:, o2, :], PLD[k2][0][:, t0:t0 + n], [PW[k2][1], PLD[k2][1]], [Bp], start=(k2 == 0), stop=(k2 == 1))
                STT(yb[:, otile, t0:t0 + n], pt[:, 0:n], pcol(l, "psc", otile), SGb[:, t0:t0 + n], ALU.mult, ALU.mult, [Bp, B_pv, B_SGb], [B_yb])


def phase5a(X, stk):
    S, l, last = X.S, X.l, X.last
    MM, TT = X.MM, X.TT
    PB = X.PB
    yg, B_yg, yb, B_yb = X.yg, X.B_yg, X.yb, X.B_yb

    def sb(name, shape, dt=F32):
        return X.sb(stk, "p5a" + name, shape, dt)
    MGT = [sb("mg%d" % i, [128, 2, NT], BF16) for i in range(2)]
    MST = [sb("ms%d" % i, [128, NT], BF16) for i in range(2)]
    YA = [X.T32[0], X.T32[1]]
    tiles = [(ti, t0, n) for ti, (t0, n) in enumerate(TILES) if not (last and ti == 0)]
    tlo = tiles[0][1]
    rr = 0
    for fb in range(4):
        wab, Bwab = X.WS[fb % 2]
        S.dma("pool", wab[:, 0:8, :], X.woa_d[l][:, fb * 512:(fb + 1) * 512].rearrange("(k p) c -> p k c", p=128), [], Bwab)
        S.dma("pool", wab[:, 8:16, :], X.wob_d[l][:, fb * 512:(fb + 1) * 512].rearrange("(k p) c -> p k c", p=128), [Bwab], Bwab)
        wa, Bwa = wab[:, 0:8, :], Bwab
        wb, Bwb = wab[:, 8:16, :], Bwab
        for q in range(4):
            f = fb * 4 + q
            mg, Bmg = MGT[f % 2]
            ms, Bms = MST[f % 2]
            S.dma("sp", mg[:, 0, tlo:NT], X.MG_d[f][:, tlo:NT], [X.B_MG[f]], Bmg)
            S.dma("sp", mg[:, 1, tlo:NT], X.MG_d[16 + f][:, tlo:NT], [X.B_MG[16 + f]], Bmg)
            for (ti, t0, n) in tiles:
                pa_, Bpa = PB[(rr * 2) % 6]
                pb_, Bpb = PB[(rr * 2 + 1) % 6]
                ya, B_ya = YA[rr % 2]
                rr += 1
                for k in range(8):
                    MM(pa_[:, 0:n], wa[:, k, q * 128:(q + 1) * 128], yg[:, k, t0:t0 + n], [Bwa, B_yg], [Bpa], start=(k == 0), stop=(k == 7))
                for k in range(8):
                    MM(pb_[:, 0:n], wb[:, k, q * 128:(q + 1) * 128], yb[:, k, t0:t0 + n], [Bwb, B_yb], [Bpb], start=(k == 0), stop=(k == 7))
                TT(ya[:, 0:n], pa_[:, 0:n], mg[:, 0, t0:t0 + n], ALU.mult, [Bpa, Bmg], [B_ya])
                TT(ms[:, t0:t0 + n], pb_[:, 0:n], mg[:, 1, t0:t0 + n], ALU.mult, [Bpb, Bmg], [Bms])
                TT(ms[:, t0:t0 + n], ms[:, t0:t0 + n], ya[:, 0:n], ALU.add, [Bms, B_ya], [Bms])
            S.dma("act", X.MER_d[f][:, tlo:NT], ms[:, tlo:NT], [Bms], X.B_MER)


def phase5b(X, stk):
    S, l, last = X.S, X.l, X.last
    MM, TR, ACT, SQACC, TT, TS, STT, RECIP = X.MM, X.TR, X.ACT, X.SQACC, X.TT, X.TS, X.STT, X.RECIP
    PB, cst, B_cst, mod, B_mod = X.PB, X.cst, X.B_cst, X.mod, X.B_mod
    ident = cst[:, CS["ident"]:CS["ident"] + 128]

    def sb(name, shape, dt=F32):
        return X.sb(stk, "p5b" + name, shape, dt)
    WO, B_WO = sb("wo", [128, 16, D], BF16)
    WOB = [Buf("wo_%d_%d" % (l, i)) for i in range(4)]
    MERs = [sb("mer%d" % i, [128, 16, 512], BF16) for i in range(2)]
    Hs = [sb("H%d" % i, [128, 4, D]) for i in range(2)]
    OG = [X.T32[0], X.T32[1]]
    ST5, B_ST5 = sb("st", [128, 8])
    for fb in range(4):
        S.dma("pool", WO[:, :, fb * 512:(fb + 1) * 512], X.wo_d[l][:, fb * 512:(fb + 1) * 512].rearrange("(k p) c -> p k c", p=128), [], WOB[fb])
    if last:
        FG = X.t32big[:, 3 * 512:7 * 512]
        FGB = [X.T32[i][1] for i in range(3, 7)]
        B_FG = FGB[0]
        for B_ in FGB[1:]:
            S.op("dve", lambda e, a=X.T32[7][0][:, 0:1]: e.memset(a, 0.0), [], [B_, X.T32[7][1]])
        S.dma("sp", FG, X.fg_d, [X.T32[7][1]], B_FG)
    src = X.xin if l == 0 else X.H1_d
    cnt = 0
    for ti, (t0, n) in enumerate(TILES):
        if last and ti == 0:
            continue
        nb = n // 128
        si = 1 if ti == 0 else 0
        rd = [] if l == 0 else [X.B_H1[ti]]
        MER, B_MER = MERs[cnt % 2]
        H, B_H = Hs[cnt % 2]
        cnt += 1
        S.dma("sp", MER[:, :, 0:n], X.MER_d[:, :, t0:t0 + n].rearrange("f p t -> p f t"), [X.B_MER], B_MER)
        S.dma("sp", H[:, 0:nb, :], src[t0:t0 + n, :].rearrange("(b p) f -> p b f", p=128), rd, B_H)
        def tail(f, og, Bog):
            ptt, Bptt = PB[3 + (f % 3)]
            for b in range(nb):
                TR(ptt[:, b * 128:(b + 1) * 128], og[:, b * 128:(b + 1) * 128], ident, [Bog, B_cst], [Bptt])
            TT(H[:, 0:nb, f * 128:(f + 1) * 128], H[:, 0:nb, f * 128:(f + 1) * 128],
               ptt[:, 0:nb * 128].rearrange("p (b f) -> p b f", f=128), ALU.add, [B_H, Bptt], [B_H])
        pend = None
        for f in range(16):
            po_, Bpo = PB[f % 3]
            for k in range(16):
                MM(po_[:, 0:n], WO[:, k, f * 128:(f + 1) * 128], MER[:, k, 0:n], [WOB[f // 4], B_MER], [Bpo], start=(k == 0), stop=(k == 15))
            og, Bog = OG[f % 2]
            ACT(og[:, 0:n], po_[:, 0:n], AF.Identity, [Bpo, B_mod], [Bog], scale=mod[:, 32 + f, si:si + 1])
            if pend is not None:
                tail(*pend)
            pend = (f, og, Bog)
        tail(*pend)
        if not last:
            S.dma("act", X.H1_d[t0:t0 + n, :].rearrange("(b p) f -> p b f", p=128), H[:, 0:nb, :], [B_H], X.B_H1[ti])
        else:
            for b in range(nb):
                SQACC(H[:, b, :], MER[:, 0:4, :].rearrange("p a t -> p (a t)"), ST5[:, b:b + 1], [B_H], [B_MER, B_ST5])
            TS(ST5[:, 0:nb], ST5[:, 0:nb], 1.0 / D, 1e-6, ALU.mult, ALU.add, [B_ST5], [B_ST5])
            ACT(ST5[:, 0:nb], ST5[:, 0:nb], AF.Sqrt, [B_ST5], [B_ST5])
            RECIP(ST5[:, 0:nb], ST5[:, 0:nb], [B_ST5], [B_ST5])
            for b in range(nb):
                STT(H[:, b, :], H[:, b, :], ST5[:, b:b + 1], FG, ALU.mult, ALU.mult, [B_H, B_ST5, B_FG], [B_H])
            lt0 = t0 - NCTX
            X.finals.append(S.dma("act", X.out_d[lt0:lt0 + n, :].rearrange("(b p) f -> p b f", p=128), H[:, 0:nb, :], [B_H], X.B_OUT[ti]))


def _pk(v):
    v = np.asarray(v, np.float32).reshape(-1, 128)
    return np.ascontiguousarray(v.T)


def _consts():
    cs = np.zeros((128, NCS), np.float32)
    cs[:, CS["ident"]:CS["ident"] + 128] = np.eye(128, dtype=np.float32)
    ob = np.zeros((128, 128), np.float32)
    ob[0:64, 0:64] = 1.0
    ob[64:128, 64:128] = 1.0
    cs[:, CS["ob"]:CS["ob"] + 128] = ob
    s = np.arange(64)[:, None]
    t = np.arange(64)[None, :]
    for d in range(2):
        strict = (s < t) if d == 0 else (s > t)
        incl = (s <= t) if d == 0 else (s >= t)
        m = np.zeros((128, 128), np.float32)
        m[0:64, 0:64] = strict
        m[64:128, 0:64] = strict
        m[0:64, 64:128] = incl
        m[64:128, 64:128] = incl
        cs[:, CS["mg%d" % d]:CS["mg%d" % d] + 128] = m
        cs[0:64, CS["mq%d" % d]:CS["mq%d" % d] + 64] = strict.T
    r = np.ones(512, np.float32)
    r[::64] = 0.0
    cs[:, CS["rst"]:CS["rst"] + 512] = r[None, :]
    for gi, w in enumerate((2, 4, 8, 16)):
        left = w // 2
        right = w - 1 - left
        for i in range(8):
            cnt_f = min(i + right, 10 ** 9) - max(i - left, 0) + 1
            cs[:, CS["pcor"] + gi * 16 + i] = 1.0 / cnt_f
            tb = -8 + i
            cnt_b = min(tb + right, -1) - (tb - left) + 1
            cs[:, CS["pcor"] + gi * 16 + 8 + i] = 1.0 / cnt_b
    return cs


def _pack_pv(inp):
    pvs = np.zeros((DEPTH, 128, NPV), np.float32)
    for l in range(DEPTH):
        def put(name, arr):
            a = _pk(arr)
            pvs[l, :, PV[name]:PV[name] + a.shape[1]] = a
        put("mu", inp["tok_mu"][l].reshape(-1))
        put("ommu", 1.0 - inp["tok_mu"][l].reshape(-1))
        put("w0", inp["decay_w0"][l].reshape(-1))
        put("a0", inp["iclr_a0"][l].reshape(-1))
        put("kk", inp["key_k"][l])
        put("ka", inp["key_a"][l].reshape(-1))
        put("omka", 1.0 - inp["key_a"][l].reshape(-1))
        put("brk", inp["bonus_rk"][l].reshape(-1))
        put("gnw", inp["gn_w"][l])
        put("gnb", inp["gn_b"][l])
        put("psc", inp["pool_scale"][l])
        put("ng", inp["norm_g"][l])
        put("adab", inp["ada_b"][l])
        if l > 0:
            put("v0", inp["vres_v0"][l - 1])
        put("fg", inp["final_g"])
    return pvs


_NC_CACHE = {}


def kernel(**inputs):
    inp = {k: np.asarray(v) for k, v in inputs.items()}
    nb = inp["x"].shape[0]
    if "nc" not in _NC_CACHE:
        _NC_CACHE["nc"] = build_program()
    nc = _NC_CACHE["nc"]
    cs = _consts()
    pvs = _pack_pv(inp)
    fgrep = np.ascontiguousarray(np.broadcast_to(inp["final_g"].astype(np.float32)[None, :], (128, D)))
    shared = {
        "ada_w": np.ascontiguousarray(inp["ada_w"], np.float32),
        "w_in": np.ascontiguousarray(inp["w_in"], np.float32),
        "dlb": np.ascontiguousarray(inp["decay_lora_b"].reshape(DEPTH, 128, 1024), np.float32),
        "ilb": np.ascontiguousarray(inp["iclr_lora_b"].reshape(DEPTH, 128, 1024), np.float32),
        "vla": np.ascontiguousarray(inp["vres_lora_a"][0], np.float32),
        "vlb": np.ascontiguousarray(inp["vres_lora_b"][0], np.float32),
        "pool_w": np.ascontiguousarray(inp["pool_w"], np.float32),
        "w_out_a": np.ascontiguousarray(inp["w_out_a"], np.float32),
        "w_out_b": np.ascontiguousarray(inp["w_out_b"], np.float32),
        "w_out": np.ascontiguousarray(inp["w_out"], np.float32),
        "pv": pvs,
        "cst": cs,
        "fgrep": fgrep,
    }
    in_maps = []
    for b in range(nb):
        m = dict(shared)
        m["xin"] = np.ascontiguousarray(np.concatenate([inp["ctx"][b], inp["x"][b]], axis=0), np.float32)
        cT = np.zeros((128, 32), np.float32)
        cT[:, 0::2] = _pk(inp["c"][b])
        cT[:, 1::2] = _pk(inp["c_ctx"])
        m["cT"] = cT
        in_maps.append(m)
    res = run_bass_kernel_spmd(nc, in_maps, core_ids=list(range(nb)))
    out = np.stack([np.asarray(r["out"], np.float32) for r in res.results], axis=0)
    return out
```
